# Optimizing a Trainium2 kernel written in Bass

```python
import jax
import jax.numpy as jnp
from jax import lax
import numpy as np

D_MODEL = 2048
BATCH = 16
SEQ = 256
DEPTH = 4
DEC_BATCH = 4
DEC_SEQ = 4096
PAST_LEN = 512

GRID_W = 64
HEAD_DIM = 128
MIX_W = D_MODEL
HALF_W = MIX_W // 2
H_A = HALF_W // HEAD_DIM
KV_A = H_A // 4
WINDOW = 128
BAND_BLK = 128
H_B = 4
DK_B = HALF_W // (2 * H_B)
DV_B = HALF_W // H_B
GLA_RANK = 16
GLA_NORMALIZER = 16.0
GLA_CHUNK = 64
H_C = HALF_W // HEAD_DIM
NA_KH = 8
NA_KW = 16
U_D = HALF_W
G_D = 4
D_CHUNK = 128
ROPE_BASE = 10000.0
N_EVEN = (DEPTH + 1) // 2
N_ODD = DEPTH // 2
EVEN_SPLITS = (H_A * HEAD_DIM, KV_A * HEAD_DIM, KV_A * HEAD_DIM,
               H_B * DK_B, H_B * DK_B, H_B * DV_B, 2 * GLA_RANK, MIX_W)
ODD_SPLITS = (H_C * HEAD_DIM, H_C * HEAD_DIM, H_C * HEAD_DIM, U_D, U_D, MIX_W)
IN_EVEN = sum(EVEN_SPLITS)
IN_ODD = sum(ODD_SPLITS)
NEG = -1e30

kernel_name = 'hybrid_diffusion_prefix_trunk_step'


def rms_norm(x, g, eps=1e-6):
    xf = x.astype(jnp.float32)
    y = xf * lax.rsqrt(jnp.mean(xf * xf, axis=-1, keepdims=True) + eps)
    return (y * g.astype(jnp.float32)).astype(x.dtype)


def split_cols(h, sizes):
    idx = [int(s) for s in np.cumsum(sizes)[:-1]]
    return jnp.split(h, idx, axis=-1)


def modulation(cond, w, b):
    m = jax.nn.silu(cond) @ w + b
    return jnp.split(m, 3, axis=-1)


def rope_2d(x):
    T = x.shape[1]
    t = jnp.arange(T)
    pos = jnp.stack([t // GRID_W, t % GRID_W], axis=-1).astype(jnp.float32)
    nf = HEAD_DIM // 4
    inv = ROPE_BASE ** (-jnp.arange(nf, dtype=jnp.float32) / nf)
    ang = pos[:, :, None] * inv
    cos = jnp.cos(ang)[None, :, None]
    sin = jnp.sin(ang)[None, :, None]
    xr = x.astype(jnp.float32).reshape(x.shape[:-1] + (2, 2, nf))
    a, b = xr[..., 0, :], xr[..., 1, :]
    out = jnp.stack([a * cos - b * sin, a * sin + b * cos], axis=-2)
    return out.reshape(x.shape).astype(x.dtype)


def ctx_attention(q, k, v, sink):
    B, L, H, _ = q.shape
    kv = k.shape[2]
    g = H // kv
    qg = q.reshape(B, L, kv, g, HEAD_DIM)
    s = jnp.einsum('bqkgd,blkd->bkgql', qg, k).astype(jnp.float32) * HEAD_DIM ** -0.5
    if sink is not None:
        sk = jnp.broadcast_to(sink.astype(jnp.float32).reshape(1, kv, g, 1, 1), s.shape[:-1] + (1,))
        s = jnp.concatenate([s, sk], axis=-1)
    p = jax.nn.softmax(s, axis=-1)[..., :L]
    o = jnp.einsum('bkgql,blkd->bqkgd', p.astype(v.dtype), v)
    return o.reshape(B, L, H * HEAD_DIM)


def window_sink_attention(q, k, v, ck, cv, sink):
    B, T = q.shape[:2]
    L = ck.shape[1]
    nb = T // BAND_BLK
    g = H_A // KV_A
    scale = HEAD_DIM ** -0.5
    qb = q.reshape(B, nb, BAND_BLK, KV_A, g, HEAD_DIM)

    def band(a):
        ap = jnp.pad(a, ((0, 0), (BAND_BLK, BAND_BLK), (0, 0), (0, 0)))
        ap = ap.reshape(B, nb + 2, BAND_BLK, KV_A, HEAD_DIM)
        return jnp.concatenate([ap[:, :-2], ap[:, 1:-1], ap[:, 2:]], axis=2)

    kb, vb = band(k), band(v)
    s_band = jnp.einsum('bnqkgd,bnjkd->bnkgqj', qb, kb).astype(jnp.float32) * scale
    blk = jnp.arange(nb)[:, None]
    qpos = blk * BAND_BLK + jnp.arange(BAND_BLK)[None]
    kpos = (blk - 1) * BAND_BLK + jnp.arange(3 * BAND_BLK)[None]
    valid = ((kpos >= 0) & (kpos < T))[:, None, :] & \
        (jnp.abs(qpos[:, :, None] - kpos[:, None, :]) <= WINDOW)
    s_band = jnp.where(valid[None, :, None, None], s_band, NEG)
    s_ctx = jnp.einsum('bnqkgd,blkd->bnkgql', qb, ck).astype(jnp.float32) * scale
    sk = jnp.broadcast_to(sink.astype(jnp.float32).reshape(1, 1, KV_A, g, 1, 1), s_ctx.shape[:-1] + (1,))
    p = jax.nn.softmax(jnp.concatenate([s_band, s_ctx, sk], axis=-1), axis=-1)
    pb = p[..., :3 * BAND_BLK]
    pc = p[..., 3 * BAND_BLK:3 * BAND_BLK + L]
    o = jnp.einsum('bnkgqj,bnjkd->bnqkgd', pb.astype(v.dtype), vb) + \
        jnp.einsum('bnkgql,blkd->bnqkgd', pc.astype(cv.dtype), cv)
    return o.reshape(B, T, H_A * HEAD_DIM)


def gla_scan(q, k, v, g, s0):
    B, T, H, _ = q.shape
    DV = v.shape[-1]
    nc = T // GLA_CHUNK

    def chunks(a):
        return a.astype(jnp.float32).reshape(B, nc, GLA_CHUNK, H, a.shape[-1]).transpose(1, 0, 3, 2, 4)

    tri = jnp.tril(jnp.ones((GLA_CHUNK, GLA_CHUNK), dtype=bool))

    def step(S, inp):
        qc, kc, vc, gc = inp
        bcum = jnp.cumsum(gc, axis=2)
        blast = bcum[:, :, -1:]
        qi = qc * jnp.exp(bcum)
        ki = kc * jnp.exp(-bcum)
        ks = kc * jnp.exp(blast - bcum)
        att = jnp.where(tri, jnp.einsum('bhid,bhjd->bhij', qi, ki), 0.0)
        o = jnp.einsum('bhij,bhjv->bhiv', att, vc) + jnp.einsum('bhid,bhdv->bhiv', qi, S)
        S = jnp.exp(blast[:, :, 0])[..., None] * S + jnp.einsum('bhjd,bhjv->bhdv', ks, vc)
        return S, o

    S, o = lax.scan(step, s0.astype(jnp.float32), (chunks(q), chunks(k), chunks(v), chunks(g)))
    o = o.transpose(1, 0, 3, 2, 4).reshape(B, T, H, DV)
    return o, S


def gla_branch(qb, kb, vb, rb, wg, bg, gnorm, s0):
    B, T = qb.shape[:2]
    q = qb.reshape(B, T, H_B, DK_B) * DK_B ** -0.5
    k = kb.reshape(B, T, H_B, DK_B)
    v = vb.reshape(B, T, H_B, DV_B)
    r = rb.reshape(B, T, 2, GLA_RANK)
    logit = jnp.einsum('btdr,drk->btdk', r, wg) + bg
    g = (jax.nn.log_sigmoid(logit.astype(jnp.float32)) / GLA_NORMALIZER).reshape(B, T, 2, H_B, DK_B)
    o_f, s_f = gla_scan(q, k, v, g[:, :, 0], s0[:, 0])
    o_b, s_b = gla_scan(jnp.flip(q, 1), jnp.flip(k, 1), jnp.flip(v, 1), jnp.flip(g[:, :, 1], 1), s0[:, 1])
    o = rms_norm(o_f + jnp.flip(o_b, 1), gnorm)
    return o.reshape(B, T, H_B * DV_B).astype(qb.dtype), jnp.stack([s_f, s_b], axis=1)


def neighbourhood_attention(q, k, v, ck, cv, bias_tab):
    B, T = q.shape[:2]
    rows = T // GRID_W
    kh = min(NA_KH, rows)
    scale = HEAD_DIM ** -0.5
    r = jnp.arange(rows)
    rs = jnp.clip(r - kh // 2, 0, rows - kh)
    row_idx = rs[:, None] + jnp.arange(kh)[None]
    qg = q.reshape(B, rows, GRID_W, H_C, HEAD_DIM)
    kg = k.reshape(B, rows, GRID_W, H_C, HEAD_DIM)[:, row_idx]
    vg = v.reshape(B, rows, GRID_W, H_C, HEAD_DIM)[:, row_idx]
    s_nb = jnp.einsum('brqhd,brkwhd->brhqkw', qg, kg).astype(jnp.float32) * scale
    col = jnp.arange(GRID_W)
    cs = jnp.clip(col - NA_KW // 2, 0, GRID_W - NA_KW)
    col_ok = (col[None, :] >= cs[:, None]) & (col[None, :] < cs[:, None] + NA_KW)
    dr = row_idx - r[:, None] + (NA_KH - 1)
    dc = jnp.clip(col[None, :] - col[:, None], -(NA_KW - 1), NA_KW - 1) + (NA_KW - 1)
    bias = bias_tab[:, dr[:, None, :, None], dc[None, :, None, :]]
    bias = bias.transpose(1, 0, 2, 3, 4).astype(jnp.float32)
    s_nb = jnp.where(col_ok[:, None, :], s_nb + bias, NEG)
    s_nb = s_nb.reshape(B, rows, H_C, GRID_W, kh * GRID_W)
    s_ctx = jnp.einsum('brqhd,blhd->brhql', qg, ck).astype(jnp.float32) * scale
    p = jax.nn.softmax(jnp.concatenate([s_nb, s_ctx], axis=-1), axis=-1)
    pn = p[..., :kh * GRID_W].reshape(B, rows, H_C, GRID_W, kh, GRID_W)
    pc = p[..., kh * GRID_W:]
    o = jnp.einsum('brhqkw,brkwhd->brqhd', pn.astype(v.dtype), vg) + \
        jnp.einsum('brhql,blhd->brqhd', pc.astype(cv.dtype), cv)
    return o.reshape(B, T, H_C * HEAD_DIM)


def spatial_gating(u, v, gnorm, ws, b):
    B, T = u.shape[:2]
    nc = T // D_CHUNK
    vf = v.astype(jnp.float32)
    mu = jnp.mean(vf, axis=-1, keepdims=True)
    var = jnp.mean(jnp.square(vf - mu), axis=-1, keepdims=True)
    vn = ((vf - mu) * lax.rsqrt(var + 1e-5) * gnorm.astype(jnp.float32)).astype(v.dtype)
    vc = vn.reshape(B, nc, D_CHUNK, G_D, U_D // G_D)
    sp = jnp.einsum('gij,bnjgc->bnigc', ws, vc) + b.T[None, None, :, :, None]
    return u * sp.reshape(B, T, U_D)


def even_mixer(h, w_in, sink, wg, bg, gnorm, cache):
    B, T = h.shape[:2]
    qa, ka, va, qb, kb, vb, rb, gp = split_cols(h @ w_in, EVEN_SPLITS)
    qa = qa.reshape(B, T, H_A, HEAD_DIM)
    ka = ka.reshape(B, T, KV_A, HEAD_DIM)
    va = va.reshape(B, T, KV_A, HEAD_DIM)
    if cache is None:
        ya = ctx_attention(qa, ka, va, sink)
        s0 = jnp.zeros((B, 2, H_B, DK_B, DV_B), jnp.float32)
    else:
        ck, cv, s0 = cache
        ya = window_sink_attention(rope_2d(qa), rope_2d(ka), va, ck, cv, sink)
    yb, s_fin = gla_branch(qb, kb, vb, rb, wg, bg, gnorm, s0)
    y = jnp.concatenate([ya, yb.astype(ya.dtype)], axis=-1) * jax.nn.silu(gp)
    side = (ka, va, s_fin) if cache is None else None
    return y, side


def odd_mixer(h, w_in, bias_tab, gnorm, ws, b, cache):
    B, T = h.shape[:2]
    qc, kc, vc, u, v, gp = split_cols(h @ w_in, ODD_SPLITS)
    qc = qc.reshape(B, T, H_C, HEAD_DIM)
    kc = kc.reshape(B, T, H_C, HEAD_DIM)
    vc = vc.reshape(B, T, H_C, HEAD_DIM)
    if cache is None:
        yc = ctx_attention(qc, kc, vc, None)
    else:
        yc = neighbourhood_attention(qc, kc, vc, cache[0], cache[1], bias_tab)
    yd = spatial_gating(u, v, gnorm, ws, b)
    y = jnp.concatenate([yc, yd.astype(yc.dtype)], axis=-1) * jax.nn.silu(gp)
    side = (kc, vc) if cache is None else None
    return y, side


def setup_inputs(seed: int = 0) -> dict:
    key = jax.random.key(seed)
    ks = jax.random.split(key, 24)
    f32 = jnp.float32

    def nrm(k, shape, s=1.0):
        return jax.random.normal(k, shape, f32) * s

    return {
        'x_prompt': nrm(ks[0], (BATCH, SEQ, D_MODEL)),
        'x_sample': nrm(ks[1], (DEC_BATCH, DEC_SEQ, D_MODEL)),
        'cache_attn_k': nrm(ks[2], (DEC_BATCH, N_EVEN, PAST_LEN, KV_A, HEAD_DIM)),
        'cache_attn_v': nrm(ks[3], (DEC_BATCH, N_EVEN, PAST_LEN, KV_A, HEAD_DIM)),
        'state_gla': nrm(ks[4], (DEC_BATCH, N_EVEN, 2, H_B, DK_B, DV_B), 0.5),
        'cache_na_k': nrm(ks[5], (DEC_BATCH, N_ODD, PAST_LEN, H_C, HEAD_DIM)),
        'cache_na_v': nrm(ks[6], (DEC_BATCH, N_ODD, PAST_LEN, H_C, HEAD_DIM)),
        'c': nrm(ks[7], (DEC_BATCH, D_MODEL)),
        'c_ctx': nrm(ks[8], (D_MODEL,)),
        'ada_w': nrm(ks[9], (DEPTH, D_MODEL, 3 * D_MODEL), 0.5 * D_MODEL ** -0.5),
        'ada_b': nrm(ks[10], (DEPTH, 3 * D_MODEL), 0.02),
        'norm_g': 1.0 + nrm(ks[11], (DEPTH, D_MODEL), 0.05),
        'w_in_even': nrm(ks[12], (N_EVEN, D_MODEL, IN_EVEN), D_MODEL ** -0.5),
        'w_in_odd': nrm(ks[13], (N_ODD, D_MODEL, IN_ODD), D_MODEL ** -0.5),
        'w_out': nrm(ks[14], (DEPTH, MIX_W, D_MODEL), MIX_W ** -0.5),
        'attn_sink': nrm(ks[15], (N_EVEN, H_A), 0.5),
        'gla_wg': nrm(ks[16], (N_EVEN, 2, GLA_RANK, H_B * DK_B), GLA_RANK ** -0.5),
        'gla_bg': nrm(ks[17], (N_EVEN, 2, H_B * DK_B), 0.1),
        'gla_norm_g': 1.0 + nrm(ks[18], (N_EVEN, DV_B), 0.05),
        'na_bias': nrm(ks[19], (N_ODD, H_C, 2 * NA_KH - 1, 2 * NA_KW - 1), 0.2),
        'gmlp_norm_g': 1.0 + nrm(ks[20], (N_ODD, U_D), 0.05),
        'gmlp_ws': nrm(ks[21], (N_ODD, G_D, D_CHUNK, D_CHUNK), D_CHUNK ** -0.5),
        'gmlp_b': 1.0 + nrm(ks[22], (N_ODD, G_D, D_CHUNK), 0.1),
        'final_norm_g': 1.0 + nrm(ks[23], (D_MODEL,), 0.05),
    }


def reference(x_prompt, x_sample, cache_attn_k, cache_attn_v, state_gla, cache_na_k, cache_na_v,
              c, c_ctx, ada_w, ada_b, norm_g, w_in_even, w_in_odd, w_out, attn_sink,
              gla_wg, gla_bg, gla_norm_g, na_bias, gmlp_norm_g, gmlp_ws, gmlp_b, final_norm_g):
    x = x_prompt
    ks_a, vs_a, ss_b, ks_c, vs_c = [], [], [], [], []
    for l in range(DEPTH):
        shift, scale, gate = modulation(c_ctx, ada_w[l], ada_b[l])
        h = rms_norm(x, norm_g[l]) * (1.0 + scale) + shift
        if l % 2 == 0:
            e = l // 2
            y, side = even_mixer(h, w_in_even[e], attn_sink[e], gla_wg[e], gla_bg[e], gla_norm_g[e], None)
            ks_a.append(side[0])
            vs_a.append(side[1])
            ss_b.append(side[2])
        else:
            o = l // 2
            y, side = odd_mixer(h, w_in_odd[o], na_bias[o], gmlp_norm_g[o], gmlp_ws[o], gmlp_b[o], None)
            ks_c.append(side[0])
            vs_c.append(side[1])
        x = x + gate * (y @ w_out[l])
    y_prompt = rms_norm(x, final_norm_g)
    new_attn_k = jnp.stack(ks_a, axis=1)
    new_attn_v = jnp.stack(vs_a, axis=1)
    new_gla_state = jnp.stack(ss_b, axis=1)
    new_na_k = jnp.stack(ks_c, axis=1)
    new_na_v = jnp.stack(vs_c, axis=1)

    x = x_sample
    for l in range(DEPTH):
        shift, scale, gate = modulation(c, ada_w[l], ada_b[l])
        h = rms_norm(x, norm_g[l]) * (1.0 + scale[:, None]) + shift[:, None]
        if l % 2 == 0:
            e = l // 2
            y, _ = even_mixer(h, w_in_even[e], attn_sink[e], gla_wg[e], gla_bg[e], gla_norm_g[e],
                              (cache_attn_k[:, e], cache_attn_v[:, e], state_gla[:, e]))
        else:
            o = l // 2
            y, _ = odd_mixer(h, w_in_odd[o], na_bias[o], gmlp_norm_g[o], gmlp_ws[o], gmlp_b[o],
                             (cache_na_k[:, o], cache_na_v[:, o]))
        x = x + gate[:, None] * (y @ w_out[l])
    y_sample = rms_norm(x, final_norm_g)
    return (y_prompt, y_sample, new_attn_k, new_attn_v, new_gla_state, new_na_k, new_na_v)
```

```python
import contextlib
import numpy as np
import concourse.bass as bass
import concourse.mybir as mybir
from concourse.bass_utils import run_bass_kernel_spmd

F32 = mybir.dt.float32
BF16 = mybir.dt.bfloat16
AF = mybir.ActivationFunctionType
ALU = mybir.AluOpType
AX = mybir.AxisListType

D = 2048
KC = 16
NS = 2048
NPR = 256
TOK = NS + 2 * NPR
NT = TOK // 512
NB = TOK // 128
IN_EVEN = 5664
IN_ODD = 7168
NEG = -1.0e9
SC = 128 ** -0.5

import os
STOP_AT = int(os.environ.get("KSTOP", "0"))
SEM_ROLL = int(os.environ.get("KROLL", "12000"))
SAME_ENGINE_SYNC = True


class Res:
    __slots__ = ("name", "w", "r", "dsem", "dcnt", "excl")

    def __init__(self, name=""):
        self.name = name
        self.excl = False
        self.w = {}
        self.r = {}
        self.dsem = {}
        self.dcnt = {}


class Ctx:
    def __init__(self, nc):
        self.nc = nc
        self.eng = {"pe": nc.tensor, "act": nc.scalar, "dve": nc.vector,
                    "pool": nc.gpsimd, "sp": nc.sync}
        self.esem = {}
        self.ecnt = {}
        self.seen = {k: {} for k in self.eng}
        self.nsem = 0
        self.nwait = 0
        self.nops = {k: 0 for k in self.eng}
        self.all_res = []
        self.pool = {"sw": [], "hw": [], "cc": []}
        self.ccs = []
        for k in self.eng:
            self._new_esem(k)

    def recycle(self):
        for r in self.all_res:
            for kind, sem in r.dsem.items():
                self.pool[kind].append((sem, r.dcnt[kind]))
            r.dsem = {}
            r.dcnt = {}
            r.w = {}
            r.r = {}
        self.pool["cc"].extend(self.ccs)
        self.ccs = []

    def res(self, name=""):
        r = Res(name)
        self.all_res.append(r)
        return r

    def _alloc_sem(self, name):
        self.nsem += 1
        return self.nc.alloc_semaphore(name=f"{name}_{self.nsem}")

    def _new_esem(self, k):
        self.esem[k] = self._alloc_sem("e" + k)
        self.ecnt[k] = 0

    def _wait(self, e, deps):
        seen = self.seen[e]
        for sem, val in deps.items():
            if seen.get(sem, 0) < val:
                self.eng[e].wait_ge(sem, val)
                self.nwait += 1
                seen[sem] = val

    def _deps(self, reads, writes, skip_sem=None, own_sem=None):
        deps = {}

        def add(s, v):
            if s is skip_sem:
                return
            if deps.get(s, 0) < v:
                deps[s] = v
        for r in reads:
            for s, v in r.w.items():
                add(s, v)
            if r.excl:
                for s, v in r.r.items():
                    if s is not own_sem:
                        add(s, v)
        for w in writes:
            for s, v in w.w.items():
                add(s, v)
            for s, v in w.r.items():
                add(s, v)
        return deps

    def _commit(self, ev, reads, writes, merge=False):
        s, v = ev
        for r in reads:
            if r.r.get(s, 0) < v:
                r.r[s] = v
        for w in writes:
            if merge:
                w.w = dict(w.w)
                w.w[s] = v
            else:
                w.w = {s: v}
            w.r = {}

    def op(self, e, fn, reads=(), writes=(), same_ok=False):
        skip = self.esem[e] if (same_ok or not SAME_ENGINE_SYNC) else None
        deps = self._deps(reads, writes, skip_sem=skip, own_sem=self.esem[e])
        self._wait(e, deps)
        if self.ecnt[e] >= SEM_ROLL:
            self._new_esem(e)
        inst = fn()
        self.ecnt[e] += 1
        self.nops[e] += 1
        inst.then_inc(self.esem[e], 1)
        ev = (self.esem[e], self.ecnt[e])
        self._commit(ev, reads, writes)
        return ev

    def dma(self, q, out, in_, reads=(), writes=(), nowaw=False, **kw):
        sr = writes[0]
        kind = "sw" if q == "pool" else "hw"
        dsem = sr.dsem.get(kind)
        deps = self._deps(reads, writes, skip_sem=dsem if nowaw else None)
        self._wait(q, deps)
        if dsem is None:
            if self.pool[kind]:
                dsem, cnt = self.pool[kind].pop()
            else:
                dsem, cnt = self._alloc_sem("d"), 0
            sr.dsem[kind] = dsem
            sr.dcnt[kind] = cnt
        inst = self.eng[q].dma_start(out=out, in_=in_, **kw)
        sr.dcnt[kind] += 1
        self.nops[q] += 1
        inst.then_inc(dsem, 16)
        ev = (dsem, 16 * sr.dcnt[kind])
        self._commit(ev, reads, writes, merge=nowaw)
        return ev

    def collective(self, ins, outs, groups, reads=(), writes=()):
        deps = self._deps(reads, writes)
        self._wait("pool", deps)
        if self.pool["cc"]:
            sem, cnt = self.pool["cc"].pop()
        else:
            sem, cnt = self._alloc_sem("cc"), 0
        inst = self.nc.gpsimd.collective_compute(
            "AllGather", ALU.bypass, replica_groups=groups, ins=ins, outs=outs)
        inst.then_inc(sem, 1)
        ev = (sem, cnt + 1)
        self.ccs.append((sem, cnt + 1))
        self._commit(ev, reads, writes)
        return ev

    def wait_all(self, e, ress=None):
        deps = {}
        for r in (ress if ress is not None else self.all_res):
            for s, v in list(r.r.items()) + list(r.w.items()):
                if deps.get(s, 0) < v:
                    deps[s] = v
        self._wait(e, deps)


def _consts():
    p = np.arange(128)[:, None]
    f = np.arange(128)[None, :]
    ident = (p == f)
    u_incl = (p <= f)
    l_incl = (p >= f)
    l_strict = (p > f)
    u_strict = (p < f)
    ones = np.ones((128, 128), bool)
    partner = np.where((np.arange(128) % 64) < 32, np.arange(128) + 32, np.arange(128) - 32)
    perm = (p == partner[None, :])
    mats = [ident, u_incl, l_incl, l_strict, u_strict, ones, perm]
    c = np.stack([m.astype(np.float32) for m in mats], axis=1)
    prev = np.where(f >= p, 0.0, NEG)
    nxt = np.where(f <= p, 0.0, NEG)
    halo = np.where(p + f >= 127, 0.0, NEG)
    m = np.stack([prev.T, nxt.T, halo.T], axis=1).astype(np.float32)
    return np.ascontiguousarray(np.concatenate([c, m], axis=1))


C_ID, C_UI, C_LI, C_LS, C_US, C_ONE, C_PERM, C_MPREV, C_MNEXT, C_MHALO = range(10)


def _fm(v):
    return np.ascontiguousarray(v.reshape(KC, 128).T)


def _glob_pos(side, t):
    return t if side == 0 else 4095 - t


def _rope_tables(side):
    t = np.arange(NS)
    g = _glob_pos(side, t)
    pos = np.stack([g // 64, g % 64], axis=-1).astype(np.float32)
    nf = 32
    inv = (np.float32(10000.0) ** (-np.arange(nf, dtype=np.float32) / nf)).astype(np.float32)
    ang = pos[:, :, None] * inv
    cos = np.cos(ang).astype(np.float32)
    sin = np.sin(ang).astype(np.float32)
    C = np.zeros((128, NS), np.float32)
    S = np.zeros((128, NS), np.float32)
    for half in range(2):
        for ab in range(2):
            d0 = half * 64 + ab * 32
            C[d0:d0 + 32, :] = cos[:, half, :].T
            S[d0:d0 + 32, :] = sin[:, half, :].T * (-1.0 if ab == 0 else 1.0)
    return C, S


NA_PAT_M = [0, 1, 2, 14, 15]


def _na_pattern_of(m):
    if m <= 1:
        return m
    if m >= 14:
        return m - 11
    return 2


def _na_lo(m):
    return min(max(2 * m - 4, 0), 26)


def _na_tables(side, na_bias_o):
    out = np.full((8, 5, 128, 640), NEG, np.float32)
    for pi, m in enumerate(NA_PAT_M):
        lo = _na_lo(m)
        qt = m * 128 + np.arange(128)
        qg = _glob_pos(side, qt)
        qr, qc = qg // 64, qg % 64
        kl = lo * 64 + np.arange(640)
        own = kl < NS
        up = np.maximum(kl - NS, 0)
        ptok = np.where(up < 128, 1920 + up, 1664 + up)
        kg = np.where(own, _glob_pos(side, np.minimum(kl, NS - 1)), _glob_pos(1 - side, ptok))
        kr, kc = kg // 64, kg % 64
        rs = np.clip(qr - 4, 0, 56)[:, None]
        cs = np.clip(qc - 8, 0, 48)[:, None]
        valid = (kr[None] >= rs) & (kr[None] < rs + 8) & (kc[None] >= cs) & (kc[None] < cs + 16)
        dr = np.clip(kr[None] - qr[:, None] + 7, 0, 14)
        dc = np.clip(kc[None] - qc[:, None], -15, 15) + 15
        for h in range(8):
            b = na_bias_o[h][dr, dc]
            out[h, pi] = np.where(valid, b, NEG)
    return out


def _prep_core(cid, I):
    b, side = cid // 2, cid % 2
    xs = I["x_sample"][b]
    xloc = xs[:NS] if side == 0 else xs[::-1][:NS]
    m = {}
    m["xin"] = np.ascontiguousarray(np.concatenate(
        [xloc, I["x_prompt"][2 * cid], I["x_prompt"][2 * cid + 1]], axis=0))
    m["condT"] = np.ascontiguousarray(np.stack([_fm(I["c"][b]), _fm(I["c_ctx"])], axis=-1))
    m["ada_w"] = I["ada_w"]
    m["ada_bT"] = np.ascontiguousarray(I["ada_b"].reshape(4, 48, 128).transpose(0, 2, 1))
    m["norm_gT"] = np.ascontiguousarray(I["norm_g"].reshape(4, KC, 128).transpose(0, 2, 1))
    m["final_gT"] = _fm(I["final_norm_g"])
    m["w_in_even"] = I["w_in_even"]
    m["w_in_odd"] = I["w_in_odd"]
    m["w_out"] = I["w_out"]
    m["consts"] = _consts()
    m["ck_a"] = np.ascontiguousarray(I["cache_attn_k"][b].transpose(0, 2, 3, 1))
    m["cv_a"] = np.ascontiguousarray(I["cache_attn_v"][b].transpose(0, 2, 1, 3))
    rc, rs = _rope_tables(side)
    m["ropeC"], m["ropeS"] = rc, rs
    m["sink"] = np.ascontiguousarray(I["attn_sink"])
    m["s0"] = np.ascontiguousarray(I["state_gla"][b][:, side])
    wg = I["gla_wg"]
    bg = I["gla_bg"]

    def wg_mat(first_dir):
        W = np.zeros((2, 33, 1024), np.float32)
        for e in range(2):
            for ld in range(2):
                d = first_dir if ld == 0 else 1 - first_dir
                W[e, d * 16:(d + 1) * 16, ld * 512:(ld + 1) * 512] = wg[e, d]
                W[e, 32, ld * 512:(ld + 1) * 512] = bg[e, d]
        return W
    m["wg_s"] = wg_mat(side)
    m["wg_p"] = wg_mat(0)
    m["gla_gn"] = np.ascontiguousarray(I["gla_norm_g"])
    m["ck_c"] = np.ascontiguousarray(I["cache_na_k"][b].transpose(0, 2, 3, 1))
    m["cv_c"] = np.ascontiguousarray(I["cache_na_v"][b].transpose(0, 2, 1, 3))
    m["na_tab"] = np.ascontiguousarray(np.stack([_na_tables(side, I["na_bias"][o]) for o in range(2)]).transpose(0, 1, 2, 4, 3))
    ws = I["gmlp_ws"]
    wsT = np.ascontiguousarray(ws.transpose(0, 1, 3, 2))
    m["wsT_p"] = wsT
    m["wsT_s"] = wsT if side == 0 else np.ascontiguousarray(wsT[:, :, ::-1, ::-1])
    gb = I["gmlp_b"]
    m["gb_p"] = np.ascontiguousarray(gb)
    m["gb_s"] = np.ascontiguousarray(gb if side == 0 else gb[:, :, ::-1])
    m["gmlp_gn"] = np.ascontiguousarray(I["gmlp_norm_g"])
    pm = np.zeros((128, 2), np.float32)
    pm[:, 1 - side] = 1.0
    m["pm"] = pm
    m["onesrow"] = np.ones((1, TOK), np.float32)
    return m


class Prog:
    def __init__(self, n_pairs_cores=8, upto=None, dbg=None):
        self.upto = upto
        self.dbgsel = dbg
        self.ncores = n_pairs_cores
        nc = bass.Bass("TRN2", target_bir_lowering=False)
        self.nc = nc
        self.c = Ctx(nc)
        self.es = contextlib.ExitStack()

    def din(self, name, shape, dt=F32):
        return self.nc.dram_tensor(name, list(shape), dt, kind="ExternalInput").ap()

    def dout(self, name, shape, dt=F32):
        return self.nc.dram_tensor(name, list(shape), dt, kind="ExternalOutput").ap()

    def dscr(self, name, shape, dt=F32):
        return self.nc.dram_tensor(name, list(shape), dt)

    def sb(self, name, shape, dt=F32):
        return self.es.enter_context(self.nc.sbuf_tensor(name, list(shape), dt))

    def build(self):
        with self.es:
            self._build()
        return self.nc

    def _build(self):
        nc, c = self.nc, self.c
        P = self
        xin = P.din("xin", [TOK, D])
        condT = P.din("condT", [128, KC, 2])
        ada_w = P.din("ada_w", [4, D, 3 * D])
        ada_bT = P.din("ada_bT", [4, 128, 48])
        norm_gT = P.din("norm_gT", [4, 128, KC])
        final_gT = P.din("final_gT", [128, KC])
        w_in = [P.din("w_in_even", [2, D, IN_EVEN]), P.din("w_in_odd", [2, D, IN_ODD])]
        w_out = P.din("w_out", [4, D, D])
        consts_d = P.din("consts", [128, 10, 128])
        ck_a = P.din("ck_a", [2, 2, 128, 512])
        cv_a = P.din("cv_a", [2, 2, 512, 128])
        ropeC_d = P.din("ropeC", [128, NS])
        ropeS_d = P.din("ropeS", [128, NS])
        sink_d = P.din("sink", [2, 8])
        s0_d = P.din("s0", [2, 4, 128, 256])
        wg_s_d = P.din("wg_s", [2, 33, 1024])
        wg_p_d = P.din("wg_p", [2, 33, 1024])
        gla_gn_d = P.din("gla_gn", [2, 256])
        ck_c = P.din("ck_c", [2, 8, 128, 512])
        cv_c = P.din("cv_c", [2, 8, 512, 128])
        na_tab_d = P.din("na_tab", [2, 8, 5, 640, 128])
        wsT_s_d = P.din("wsT_s", [2, 4, 128, 128])
        wsT_p_d = P.din("wsT_p", [2, 4, 128, 128])
        gb_s_d = P.din("gb_s", [2, 4, 128])
        gb_p_d = P.din("gb_p", [2, 4, 128])
        gmlp_gn_d = P.din("gmlp_gn", [2, 1024])
        pm_d = P.din("pm", [128, 2])
        onesrow_d = P.din("onesrow", [1, TOK])

        y_out = P.dout("y_out", [TOK, D])
        nk_a = P.dout("nk_a", [2, 2, NPR, 2, 128])
        nv_a = P.dout("nv_a", [2, 2, NPR, 2, 128])
        ngs = P.dout("ngs", [2, 2, 2, 4, 128, 256])
        nk_c = P.dout("nk_c", [2, 2, NPR, 8, 128])
        nv_c = P.dout("nv_c", [2, 2, NPR, 8, 128])
        self.dbg_out = P.dout("dbg", [KC, 128, TOK]) if self.dbgsel else None

        xs = P.dscr("xs", [KC, 128, TOK]).ap()
        yT = P.dscr("yT", [KC, 128, TOK], BF16).ap()
        ygT = P.dscr("ygT", [KC, 128, TOK], BF16).ap()
        osp = P.dscr("osp", [4, 16, 128, 256]).ap()
        qts = P.dscr("qts", [4, 128, NS], BF16).ap()
        R_xs = [c.res(f"xs{t}") for t in range(NT)]
        R_yT = [c.res(f"yT{t}") for t in range(NT)]
        R_ygT = [c.res(f"ygT{t}") for t in range(NT)]
        self.R_osp = [[c.res() for _ in range(16)] for _ in range(4)]
        self.R_qts = [c.res() for _ in range(4)]

        BIG = P.sb("BIG", [128, 20480])
        XR = P.sb("XR", [128, 8192])
        WR = P.sb("WR", [128, 12288])
        MB = P.sb("MB", [128, 9472])
        CON = P.sb("CON", [128, 10, 128])
        CONB = P.sb("CONB", [128, 5, 128], BF16)
        MOD = P.sb("MOD", [128, 4, 48, 2])
        MA = P.sb("MA", [128, 4, KC, 2])
        SM = P.sb("SM", [128, 512])
        PSA = self.es.enter_context(nc.psum_tensor("psa", [128, 1536], F32))
        PSB = self.es.enter_context(nc.psum_tensor("psb", [128, 1536], F32))
        PS = [PSA[:, i * 512:(i + 1) * 512] for i in range(3)] + [PSB[:, i * 512:(i + 1) * 512] for i in range(3)]
        PS.append(self.es.enter_context(nc.psum_tensor("ps6", [128, 512], F32)))
        PS.append(self.es.enter_context(nc.psum_tensor("ps7", [128, 512], F32)))
        self.PSsets = [PSA, PSB]
        R_PS = [c.res(f"ps{i}") for i in range(8)]
        for r_ in R_PS:
            r_.excl = True
        self.PS, self.R_PS = PS, R_PS
        self.CON, self.CONB = CON, CONB

        HT = BIG[:].bitcast(BF16)[:, 0:KC * TOK].rearrange("p (k t) -> p k t", k=KC)
        R_HT = [c.res(f"ht{t}") for t in range(NT)]
        R_BIG = R_HT
        WS = [WR[:, i * 4096:(i + 1) * 4096].bitcast(BF16).rearrange("p (k n) -> p k n", k=KC) for i in range(3)]
        R_WS = [c.res(f"w{i}") for i in range(3)]
        self.HT, self.R_HT, self.WS, self.R_WS = HT, R_HT, WS, R_WS
        R_XR = c.res("xr")
        R_MB = c.res("mb")
        R_CON = c.res("con")
        R_MOD = c.res("mod")
        self.wslot = 0

        c.dma("sp", CON[:], consts_d[:, :, :], writes=[R_CON])
        c.op("dve", lambda: nc.vector.tensor_copy(out=CONB[:, 0, :], in_=CON[:, C_ID, :]), reads=[R_CON], writes=[R_CON])
        c.op("dve", lambda: nc.vector.tensor_copy(out=CONB[:, 1, :], in_=CON[:, C_ONE, :]), reads=[R_CON], writes=[R_CON])
        c.op("dve", lambda: nc.vector.tensor_copy(out=CONB[:, 2:5, :], in_=CON[:, C_MPREV:C_MHALO + 1, :]), reads=[R_CON], writes=[R_CON])
        self.maskb = [CONB[:, 2 + i, :] for i in range(3)]
        ident = CON[:, C_ID, :]
        identb = CONB[:, 0, :]
        onesb = CONB[:, 1, :]
        self.identb, self.onesb = identb, onesb
        self.olf = P.dscr("olf", [4, 16, 128, 256]).ap()
        self.ccg_in_t = P.dscr("ccg_in", [512, 256])
        self.ccg_out_t = P.dscr("ccg_out", [1024, 256])
        self.ccg_in = self.ccg_in_t.ap()
        self.ccg_out = self.ccg_out_t.ap().rearrange("(r p) n -> r p n", r=2)
        self.R_ccgo = c.res("ccgo")
        self.groups = [[2 * i, 2 * i + 1] for i in range(self.ncores // 2)]
        self.PM = SM[:, 360:362]
        self.R_pm = c.res("pm")
        c.dma("sp", self.PM, pm_d[:, :], writes=[self.R_pm])
        self.cc_kv = [(P.dscr(f"cckv_in{g}", [128, 256], BF16), P.dscr(f"cckv_out{g}", [256, 256], BF16), c.res(), c.res())
                      for g in range(2)]
        self.cc_na = [(P.dscr(f"ccna_in{h}", [128, 512], BF16), P.dscr(f"ccna_out{h}", [256, 512], BF16), c.res(), c.res())
                      for h in range(8)]
        self.R_fin = [c.res(), c.res()]
        self.R_jk = c.res()
        self.R_yn = [c.res(), c.res()]
        self.R_yf = [c.res(), c.res()]
        self.R_CON = R_CON

        def fence():
            for e in ("pe", "act", "dve", "pool", "sp"):
                c.wait_all(e)
            c.recycle()

        self.fence = fence

        cond = SM[:, 0:32].rearrange("p (k c) -> p k c", c=2)
        scond = SM[:, 32:64].rearrange("p (k c) -> p k c", c=2)
        R_cond = c.res("cond")
        abT = SM[:, 64:64 + 192].rearrange("p (l n) -> p l n", l=4)
        ngT = SM[:, 256:256 + 64].rearrange("p (l k) -> p l k", l=4)
        fgT = SM[:, 320:336]
        R_small = c.res("small")
        c.dma("sp", cond, condT[:, :, :], writes=[R_cond])
        c.dma("sp", abT, ada_bT.rearrange("l p n -> p l n"), writes=[R_small])
        c.dma("sp", ngT, norm_gT.rearrange("l p k -> p l k"), writes=[R_small], nowaw=True)
        c.dma("sp", fgT, final_gT[:, :], writes=[R_small], nowaw=True)
        c.op("act", lambda: nc.scalar.activation(out=scond, in_=cond, func=AF.Silu), reads=[R_cond], writes=[R_cond])
        AT = [BIG[:, i * 8192:(i + 1) * 8192].rearrange("p (k n) -> p k n", k=KC) for i in range(2)]
        R_AT = [c.res("at0"), c.res("at1")]
        gi = 0
        MROW = XR[0:2, 0:6144]
        R_MROW = c.res("mrow")
        for l in range(4):
            for grp in range(12):
                s = gi % 2
                src = ada_w[l, :, grp * 512:(grp + 1) * 512].rearrange("(k p) n -> p k n", p=128)
                c.dma("sp" if gi % 2 == 0 else "act", AT[s], src, writes=[R_AT[s]])
                bank = 4 + (gi % 2)
                for kc in range(KC):
                    c.op("pe", lambda: nc.tensor.matmul(
                        PS[bank][0:2, :], lhsT=scond[:, kc, :], rhs=AT[s][:, kc, :],
                        start=(kc == 0), stop=(kc == KC - 1)),
                        reads=[R_AT[s], R_cond], writes=[R_PS[bank]], same_ok=True)
                self.copy_any(MROW[:, grp * 512:(grp + 1) * 512], PS[bank][0:2, :], [R_PS[bank]], [R_MROW])
                gi += 1
            for ch in range(48):
                c.op("pe", lambda: nc.tensor.transpose(PS[6][:, 2 * ch:2 * ch + 2], MROW[:, ch * 128:(ch + 1) * 128], ident[0:2, 0:2]),
                     reads=[R_MROW, R_CON], writes=[R_PS[6]], same_ok=True)
            for ci in range(2):
                pv = PS[6][:, 0:96].rearrange("p (j c) -> p j c", c=2)[:, :, ci]
                c.op("dve", lambda: nc.vector.tensor_tensor(
                    out=MOD[:, l, :, ci], in0=pv, in1=abT[:, l, :], op=ALU.add),
                    reads=[R_PS[6], R_small], writes=[R_MOD])
        for l in range(4):
            for ci in range(2):
                c.op("dve", lambda: nc.vector.scalar_tensor_tensor(
                    out=MA[:, l, :, ci], in0=MOD[:, l, 16:32, ci], scalar=1.0, in1=ngT[:, l, :],
                    op0=ALU.add, op1=ALU.mult), reads=[R_MOD, R_small], writes=[R_MOD])
        self.MOD, self.MA, self.R_MOD = MOD, MA, R_MOD
        fence()

        XTOK = [XR[:, i * 2048:(i + 1) * 2048] for i in range(2)]
        XTB = [XR[:, 4096 + i * 2048:4096 + (i + 1) * 2048].rearrange("p (k t) -> p k t", k=KC) for i in range(2)]
        R_XTOK = [c.res(), c.res()]
        R_XTB = [c.res(), c.res()]
        for blk in range(NB):
            s = blk % 2
            c.dma("sp", XTOK[s], xin[blk * 128:(blk + 1) * 128, :], writes=[R_XTOK[s]])
            for g4 in range(4):
                bank = g4 % 4
                for j in range(4):
                    kc = g4 * 4 + j
                    c.op("pe", lambda: nc.tensor.transpose(
                        PS[bank][:, j * 128:(j + 1) * 128], XTOK[s][:, kc * 128:(kc + 1) * 128], ident),
                        reads=[R_XTOK[s], R_CON], writes=[R_PS[bank]], same_ok=True)
                dst = XTB[s][:, g4 * 4:(g4 + 1) * 4, :]
                srcp = PS[bank][:].rearrange("p (j t) -> p j t", j=4)
                if g4 % 2 == 0:
                    c.op("dve", lambda: nc.vector.tensor_copy(out=dst, in_=srcp), reads=[R_PS[bank]], writes=[R_XTB[s]])
                else:
                    c.op("act", lambda: nc.scalar.copy(out=dst, in_=srcp), reads=[R_PS[bank]], writes=[R_XTB[s]])
            t = blk // 4
            c.dma("act", xs[:, :, blk * 128:(blk + 1) * 128].rearrange("k p t -> p k t"), XTB[s][:],
                  reads=[R_XTB[s]], writes=[R_xs[t]], nowaw=True)
        fence()

        XP = [XR[:, i * 2048:(i + 1) * 2048].rearrange("p (k t) -> p k t", k=4) for i in range(4)]
        R_XP = [c.res() for _ in range(4)]
        SQ = [MB[:, i * 1024:(i + 1) * 1024].bitcast(BF16).rearrange("p (k t) -> p k t", k=4) for i in range(2)]
        R_SQ = [c.res(), c.res()]
        RSTD = MB[:, 2048:2560]
        RTMP = MB[:, 2560:3072]
        HTMP = [MB[:, 3072 + i * 512:3072 + (i + 1) * 512] for i in range(3)]
        R_RSTD = c.res()
        R_HTMP = [c.res(), c.res(), c.res()]
        xp_i = [0]

        def next_xp():
            i = xp_i[0] % 4
            xp_i[0] += 1
            return i

        def phase_A(l):
            for t in range(NT):
                ci = 0 if t < 4 else 1
                bank = 4 + (t % 2)
                for pc in range(4):
                    i = next_xp()
                    c.dma("sp", XP[i], xs[pc * 4:(pc + 1) * 4, :, t * 512:(t + 1) * 512].rearrange("k p t -> p k t"),
                          reads=[R_xs[t]], writes=[R_XP[i]])
                    sq = pc % 2
                    c.op("act", lambda: nc.scalar.activation(out=SQ[sq][:], in_=XP[i][:], func=AF.Square),
                         reads=[R_XP[i]], writes=[R_SQ[sq]])
                    for k in range(4):
                        c.op("pe", lambda: nc.tensor.matmul(
                            PS[bank][:], lhsT=onesb, rhs=SQ[sq][:, k, :],
                            start=(pc == 0 and k == 0), stop=(pc == 3 and k == 3)),
                            reads=[R_SQ[sq], R_CON], writes=[R_PS[bank]], same_ok=True)
                c.op("act", lambda: nc.scalar.activation(out=RTMP, in_=PS[bank][:], func=AF.Sqrt,
                                                         scale=1.0 / D, bias=self.eps_ap),
                     reads=[R_PS[bank], R_CON], writes=[R_RSTD])
                c.op("dve", lambda: nc.vector.reciprocal(out=RSTD, in_=RTMP), reads=[R_RSTD], writes=[R_RSTD])
                for pc in range(4):
                    i = next_xp()
                    c.dma("sp", XP[i], xs[pc * 4:(pc + 1) * 4, :, t * 512:(t + 1) * 512].rearrange("k p t -> p k t"),
                          reads=[R_xs[t]], writes=[R_XP[i]])
                    for k in range(4):
                        kc = pc * 4 + k
                        if kc % 3 == 2:
                            hs = 2
                            c.op("pool", lambda: nc.gpsimd.tensor_tensor(out=HTMP[hs], in0=XP[i][:, k, :], in1=RSTD, op=ALU.mult),
                                 reads=[R_XP[i], R_RSTD], writes=[R_HTMP[hs]])
                        else:
                            hs = kc % 2
                            c.op("dve", lambda: nc.vector.tensor_tensor(out=HTMP[hs], in0=XP[i][:, k, :], in1=RSTD, op=ALU.mult),
                                 reads=[R_XP[i], R_RSTD], writes=[R_HTMP[hs]])
                        c.op("act", lambda: nc.scalar.activation(
                            out=HT[:, kc, t * 512:(t + 1) * 512], in_=HTMP[hs], func=AF.Identity,
                            scale=MA[:, l, kc, ci:ci + 1], bias=MOD[:, l, kc, ci:ci + 1]),
                            reads=[R_HTMP[hs], R_MOD], writes=[R_HT[t]])

        self.eps_ap = SM[:, 336:337]
        self.eps5_ap = SM[:, 337:338]
        c.op("dve", lambda: nc.vector.memset(self.eps_ap, 1e-6), writes=[R_CON])
        c.op("dve", lambda: nc.vector.memset(self.eps5_ap, 1e-5), writes=[R_CON])

        def phase_C(l):
            WO = BIG[:].bitcast(BF16)[:, 0:KC * D].rearrange("p (k n) -> p k n", k=KC)
            R_WO = [c.res() for _ in range(4)]
            for q4 in range(4):
                c.dma("pool", WO[:, :, q4 * 512:(q4 + 1) * 512],
                      w_out[l, :, q4 * 512:(q4 + 1) * 512].rearrange("(k p) n -> p k n", p=128),
                      writes=[R_WO[q4]] + R_HT, nowaw=True)
            for t in range(NT):
                ci = 0 if t < 4 else 1
                ws = self.next_w()
                c.dma("act", WS[ws][:], ygT[:, :, t * 512:(t + 1) * 512].rearrange("k p t -> p k t"),
                      reads=[R_ygT[t]], writes=[R_WS[ws]])
                for pc in range(4):
                    i = next_xp()
                    c.dma("sp", XP[i], xs[pc * 4:(pc + 1) * 4, :, t * 512:(t + 1) * 512].rearrange("k p t -> p k t"),
                          reads=[R_xs[t]], writes=[R_XP[i]])
                    for k in range(4):
                        dc = pc * 4 + k
                        bank = dc % 4
                        for kc in range(KC):
                            c.op("pe", lambda: nc.tensor.matmul(
                                PS[bank][:], lhsT=WO[:, kc, dc * 128:(dc + 1) * 128], rhs=WS[ws][:, kc, :],
                                start=(kc == 0), stop=(kc == KC - 1)),
                                reads=[R_WS[ws], R_WO[dc // 4]], writes=[R_PS[bank]], same_ok=True)
                        c.op("dve", lambda: nc.vector.scalar_tensor_tensor(
                            out=XP[i][:, k, :], in0=PS[bank][:], scalar=MOD[:, l, 32 + dc, ci:ci + 1],
                            in1=XP[i][:, k, :], op0=ALU.mult, op1=ALU.add),
                            reads=[R_PS[bank], R_MOD, R_XP[i]], writes=[R_XP[i]])
                    c.dma("sp", xs[pc * 4:(pc + 1) * 4, :, t * 512:(t + 1) * 512].rearrange("k p t -> p k t"), XP[i],
                          reads=[R_XP[i]], writes=[R_xs[t]])

        def phase_F():
            YTOK = [MB[:, 5120 + i * 2048:5120 + (i + 1) * 2048] for i in range(2)]
            R_YTOK = [c.res(), c.res()]
            FB = [MB[:, 4096 + i * 512:4096 + (i + 1) * 512].rearrange("p (k t) -> p k t", k=4) for i in range(2)]
            R_FB = [c.res(), c.res()]
            RS2 = SM[:, 0:128]
            RT2 = SM[:, 128:256]
            for blk in range(NB):
                t = blk // 4
                bank = 4 + (blk % 2)
                ys = blk % 2
                xsl = lambda pc: xs[pc * 4:(pc + 1) * 4, :, blk * 128:(blk + 1) * 128].rearrange("k p t -> p k t")
                for pc in range(4):
                    i = next_xp()
                    xv = XP[i][:, :, 0:128]
                    c.dma("sp", xv, xsl(pc), reads=[R_xs[t]], writes=[R_XP[i]])
                    sq = pc % 2
                    c.op("act", lambda: nc.scalar.activation(out=SQ[sq][:, :, 0:128], in_=xv, func=AF.Square),
                         reads=[R_XP[i]], writes=[R_SQ[sq]])
                    for k in range(4):
                        c.op("pe", lambda: nc.tensor.matmul(
                            PS[bank][:, 0:128], lhsT=onesb, rhs=SQ[sq][:, k, 0:128],
                            start=(pc == 0 and k == 0), stop=(pc == 3 and k == 3)),
                            reads=[R_SQ[sq], R_CON], writes=[R_PS[bank]], same_ok=True)
                c.op("act", lambda: nc.scalar.activation(out=RT2, in_=PS[bank][:, 0:128], func=AF.Sqrt,
                                                         scale=1.0 / D, bias=self.eps_ap),
                     reads=[R_PS[bank], R_CON], writes=[R_RSTD])
                c.op("dve", lambda: nc.vector.reciprocal(out=RS2, in_=RT2), reads=[R_RSTD], writes=[R_RSTD])
                for pc in range(4):
                    i = next_xp()
                    xv = XP[i][:, :, 0:128]
                    c.dma("sp", xv, xsl(pc), reads=[R_xs[t]], writes=[R_XP[i]])
                    fb = pc % 2
                    for k in range(4):
                        kc = pc * 4 + k
                        c.op("dve", lambda: nc.vector.scalar_tensor_tensor(
                            out=FB[fb][:, k, :], in0=xv[:, k, :], scalar=fgT[:, kc:kc + 1], in1=RS2,
                            op0=ALU.mult, op1=ALU.mult), reads=[R_XP[i], R_RSTD, R_small], writes=[R_FB[fb]])
                    pb = pc % 4
                    for k in range(4):
                        c.op("pe", lambda: nc.tensor.transpose(
                            PS[pb][:, k * 128:(k + 1) * 128], FB[fb][:, k, :], ident),
                            reads=[R_FB[fb], R_CON], writes=[R_PS[pb]], same_ok=True)
                    dst = YTOK[ys][:, pc * 512:(pc + 1) * 512]
                    if pc % 2 == 0:
                        c.op("act", lambda: nc.scalar.copy(out=dst, in_=PS[pb][:]), reads=[R_PS[pb]], writes=[R_YTOK[ys]])
                    else:
                        c.op("dve", lambda: nc.vector.tensor_copy(out=dst, in_=PS[pb][:]), reads=[R_PS[pb]], writes=[R_YTOK[ys]])
                c.dma("act", y_out[blk * 128:(blk + 1) * 128, :], YTOK[ys], reads=[R_YTOK[ys]], writes=[self.R_out], nowaw=True)

        self.R_out = c.res("out")
        self.R_dbg = c.res("dbg")
        self.R_yT, self.R_ygT, self.yT, self.ygT = R_yT, R_ygT, yT, ygT
        self.w_in = w_in
        self.MBuf, self.XRbuf, self.SM = MB, XR, SM
        self.dram = dict(ck_a=ck_a, cv_a=cv_a, ropeC=ropeC_d, ropeS=ropeS_d, sink=sink_d, s0=s0_d,
                         wg_s=wg_s_d, wg_p=wg_p_d, gla_gn=gla_gn_d, ck_c=ck_c, cv_c=cv_c, na_tab=na_tab_d,
                         wsT_s=wsT_s_d, wsT_p=wsT_p_d, gb_s=gb_s_d, gb_p=gb_p_d, gmlp_gn=gmlp_gn_d, pm=pm_d, onesrow=onesrow_d,
                         nk_a=nk_a, nv_a=nv_a, ngs=ngs, nk_c=nk_c, nv_c=nv_c, osp=osp, qts=qts, xs=xs)

        nlayers = 4 if self.upto is None else self.upto[0]
        stage = None if self.upto is None else self.upto[1]
        def scoped(name, fn, *a):
            with nc.named_scope(name):
                fn(*a)
                fence()

        for l in range(nlayers):
            last = (l == nlayers - 1)
            scoped(f"A{l}", phase_A, l)
            if last and stage == "A":
                self.dump_ht()
                break
            if l % 2 == 0:
                self.phase_B_even(l)
            else:
                self.phase_B_odd(l)
            fence()
            if last and stage in ("B", "B1", "B2"):
                self.dump_fm(yT, R_yT, BF16)
                break
            with nc.named_scope(f"gate{l}"):
                self.phase_gate(l)
            if last and stage == "G":
                fence()
            if last and stage == "G":
                self.dump_fm(ygT, R_ygT, BF16)
                break
            scoped(f"C{l}", phase_C, l)
            if last and stage == "C":
                self.dump_fm(xs, R_xs, F32)
                break
        else:
            scoped("F", phase_F)
        fence()

    def next_w(self):
        i = self.wslot % 3
        self.wslot += 1
        return i

    def load_w(self, l, parts):
        c = self.c
        s = self.next_w()
        wsrc = self.w_in[l % 2]
        off = 0
        first = True
        for col0, n in parts:
            src = wsrc[l // 2, :, col0:col0 + n].rearrange("(k p) n -> p k n", p=128)
            c.dma("pool", self.WS[s][:, :, off:off + n], src, writes=[self.R_WS[s]], nowaw=not first)
            first = False
            off += n
        return s

    def proj_fm(self, ws, wcol, tiles, evac, banks=(0, 1, 2)):
        nc, c = self.nc, self.c
        for t in tiles:
            b = banks[self.pbank % len(banks)]
            self.pbank += 1
            if isinstance(t, tuple):
                t0, n = t
            else:
                t0, n = t * 512, 512
            ti = t0 // 512
            for kc in range(KC):
                c.op("pe", lambda: nc.tensor.matmul(
                    self.PS[b][:, 0:n], lhsT=self.WS[ws][:, kc, wcol:wcol + 128], rhs=self.HT[:, kc, t0:t0 + n],
                    start=(kc == 0), stop=(kc == KC - 1)),
                    reads=[self.R_WS[ws], self.R_HT[ti]], writes=[self.R_PS[b]], same_ok=True)
            evac(t, self.PS[b][:, 0:n], self.R_PS[b])

    def proj_tm(self, ws, wcol, ncols, blocks, evac, banks=(0, 1, 2)):
        nc, c = self.nc, self.c
        for blk in blocks:
            b = banks[self.pbank % len(banks)]
            self.pbank += 1
            t = blk // 4
            for kc in range(KC):
                c.op("pe", lambda: nc.tensor.matmul(
                    self.PS[b][:, 0:ncols], lhsT=self.HT[:, kc, blk * 128:(blk + 1) * 128],
                    rhs=self.WS[ws][:, kc, wcol:wcol + ncols], start=(kc == 0), stop=(kc == KC - 1)),
                    reads=[self.R_WS[ws], self.R_HT[t]], writes=[self.R_PS[b]], same_ok=True)
            evac(blk, self.PS[b][:, 0:ncols], self.R_PS[b])

    pbank = 0
    evi = 0

    def copy_any(self, out, in_, reads, writes, scale=None):
        nc, c = self.nc, self.c
        self.evi += 1
        if self.evi % 2 == 0:
            if scale is None:
                return c.op("act", lambda: nc.scalar.copy(out=out, in_=in_), reads=reads, writes=writes)
            return c.op("act", lambda: nc.scalar.activation(out=out, in_=in_, func=AF.Copy, scale=scale), reads=reads, writes=writes)
        if scale is None:
            return c.op("dve", lambda: nc.vector.tensor_copy(out=out, in_=in_), reads=reads, writes=writes)
        return c.op("dve", lambda: nc.vector.tensor_scalar(out=out, in0=in_, scalar1=scale, scalar2=None, op0=ALU.mult), reads=reads, writes=writes)

    def dump_ht(self):
        c = self.c
        for t in range(NT):
            c.dma("pool", self.dbg_out[:, :, t * 512:(t + 1) * 512].rearrange("k p t -> p k t"),
                  self.HT[:, :, t * 512:(t + 1) * 512], reads=[self.R_HT[t]], writes=[self.R_dbg], nowaw=True)

    def dump_fm(self, src, R, dt):
        c = self.c
        for t in range(NT):
            for k in range(KC):
                c.dma("pool", self.dbg_out[k, :, t * 512:(t + 1) * 512], src[k, :, t * 512:(t + 1) * 512],
                      reads=[R[t]], writes=[self.R_dbg], nowaw=True)

    def phase_gate(self, l):
        nc, c = self.nc, self.c
        MB = self.MBuf
        off = 3616 if l % 2 == 0 else 5120
        YB = [MB[:, i * 1024:(i + 1) * 1024].bitcast(BF16).rearrange("p (k t) -> p k t", k=4) for i in range(2)]
        SG = [MB[:, 2048 + i * 256:2048 + (i + 1) * 256].bitcast(BF16) for i in range(2)]
        R_YB = [c.res(), c.res()]
        R_SG = [c.res(), c.res()]
        it = 0
        for blk in range(4):
            ws = self.load_w(l, [(off + blk * 512, 512)])
            for t in range(NT):
                yb = it % 2
                it += 1
                c.dma("sp", YB[yb][:], self.yT[blk * 4:(blk + 1) * 4, :, t * 512:(t + 1) * 512].rearrange("k p t -> p k t"),
                      reads=[self.R_yT[t]], writes=[R_YB[yb]])
                for j in range(4):
                    sg = j % 2

                    def ev(tt, ps, Rb, j=j, sg=sg, yb=yb):
                        c.op("act", lambda: nc.scalar.activation(out=SG[sg], in_=ps, func=AF.Silu),
                             reads=[Rb], writes=[R_SG[sg]])
                        c.op("dve", lambda: nc.vector.tensor_tensor(out=YB[yb][:, j, :], in0=YB[yb][:, j, :], in1=SG[sg], op=ALU.mult),
                             reads=[R_SG[sg], R_YB[yb]], writes=[R_YB[yb]])
                    self.proj_fm(ws, j * 128, [t], ev)
                c.dma("act", self.ygT[blk * 4:(blk + 1) * 4, :, t * 512:(t + 1) * 512].rearrange("k p t -> p k t"), YB[yb][:],
                      reads=[R_YB[yb]], writes=[self.R_ygT[t]], nowaw=True)

    def phase_B_even(self, l):
        sub = self.upto[1] if self.upto else None
        if sub != "B2":
            with self.nc.named_scope(f"gla{l}"):
                self.gla(l)
                self.fence()
        if sub == "B1":
            return
        with self.nc.named_scope(f"win{l}"):
            self.win_attn(l)
            self.fence()
        if sub == "B2":
            return
        with self.nc.named_scope(f"fix{l}"):
            self.gla_fix(l)

    def gla(self, l):
        nc, c = self.nc, self.c
        e = l // 2
        MB, XR, SM, CON = self.MBuf, self.XRbuf, self.SM, self.CON
        PS, R_PS = self.PS, self.R_PS
        dr = self.dram
        bf = lambda ap: ap.bitcast(BF16)
        QT = bf(MB[:, 0:1280])
        KT = bf(MB[:, 1280:2560])
        KTOK = bf(MB[:, 2560:3840]).rearrange("p (b d) -> p b d", b=NB)
        VTOK = bf(MB[:, 3840:6400]).rearrange("p (b d) -> p b d", b=NB)
        RT = bf(MB[:, 6400:7680])
        WGS = bf(MB[0:33, 7680:8192])
        WGP = bf(MB[0:33, 8192:8704])
        TMPB = [bf(MB[:, 8704 + i * 64:8704 + (i + 1) * 64]) for i in range(8)]
        SBF = [bf(MB[:, 9216 + i * 128:9216 + (i + 1) * 128]) for i in range(2)]
        G = XR[:, 0:5120].rearrange("p (b d) -> p b d", b=NB)
        ET = [XR[:, 5120 + i * 128:5120 + (i + 1) * 128] for i in range(6)]
        S32 = XR[:, 5888:6144]
        OT = [XR[:, 6144 + i * 256:6144 + (i + 1) * 256] for i in range(2)]
        OP = XR[:, 7168:7680].rearrange("p (b d) -> p b d", b=2)
        GN = XR[:, 7680:7936]
        DL = SM[:, 340:341]
        ST = SM[:, 344:352]
        self.GN = GN
        R = {k: c.res(k) for k in ["QT", "KT", "KTOK", "VTOK", "RT", "WG", "G", "S32", "OP", "GN", "DL", "ST"]}
        R_TMPB = [c.res() for _ in range(8)]
        R_SBF = [c.res() for _ in range(2)]
        R_ET = [c.res() for _ in range(6)]
        R_OT = [c.res() for _ in range(2)]
        R_olf = [c.res() for _ in range(4)]
        R_osum = [c.res() for _ in range(4)]
        R_qts = self.R_qts
        self.R_osum = R_osum
        osp, qts = dr["osp"], dr["qts"]
        olf = self.olf
        self.R_ccg = c.res("ccg_in")
        if STOP_AT == -1:
            return
        c.dma("pool", WGS, dr["wg_s"][e, :, :], writes=[R["WG"]])
        c.dma("pool", WGP, dr["wg_p"][e, :, :], writes=[R["WG"]], nowaw=True)
        c.dma("sp", GN, dr["gla_gn"][e, :].partition_broadcast(128), writes=[R["GN"]])
        if STOP_AT == -2:
            return
        ws = self.load_w(l, [(3584, 128)])
        if STOP_AT == -3:
            return
        c.dma("pool", RT[32:33, :], dr["onesrow"][:, :], writes=[R["RT"]])
        if STOP_AT == -4:
            return
        for t in range(NT):
            b = self.pbank % 3
            self.pbank += 1
            for kc in range(KC):
                c.op("pe", lambda: nc.tensor.matmul(
                    PS[b][:], lhsT=self.WS[ws][:, kc, 0:128], rhs=self.HT[:, kc, t * 512:(t + 1) * 512],
                    start=(kc == 0), stop=(kc == KC - 1)),
                    reads=[self.R_WS[ws], self.R_HT[t]], writes=[R_PS[b]], same_ok=True)
            if STOP_AT != -5:
                self.copy_any(RT[0:32, t * 512:(t + 1) * 512], PS[b][0:32, :], [R_PS[b]], [R["RT"]])
        if STOP_AT == 1:
            return
        ti = [0]
        ei = [0]
        oi = [0]

        ami = [0]

        def tmpb():
            i = ti[0] % 6
            ti[0] += 1
            return i

        def etmp():
            i = ei[0] % 6
            ei[0] += 1
            return i

        def otmp():
            i = oi[0] % 2
            oi[0] += 1
            return i

        for h in range(4):
            ws = self.load_w(l, [(1536 + h * 128, 128), (2048 + h * 128, 128), (2560 + h * 256, 256)])
            if STOP_AT == 11:
                return
            self.proj_fm(ws, 0, range(NT), lambda t, ps, Rb: self.copy_any(
                QT[:, t * 512:(t + 1) * 512], ps, [Rb], [R["QT"]], scale=SC))
            if STOP_AT == 12:
                return
            self.proj_fm(ws, 128, range(NT), lambda t, ps, Rb: self.copy_any(
                KT[:, t * 512:(t + 1) * 512], ps, [Rb], [R["KT"]]))
            if STOP_AT == 13:
                return

            def ev_kv(blk, ps, Rb):
                self.copy_any(KTOK[:, blk, :], ps[:, 0:128], [Rb], [R["KTOK"]])
                self.copy_any(VTOK[:, blk, :], ps[:, 128:384], [Rb], [R["VTOK"]])
            self.proj_tm(ws, 128, 384, range(NB), ev_kv)
            if STOP_AT == 2:
                return
            for blk in range(NB):
                wgm = WGS if blk < 16 else WGP
                b = self.pbank % 3
                self.pbank += 1
                for ld in range(2):
                    c.op("pe", lambda: nc.tensor.matmul(
                        PS[b][:, ld * 128:(ld + 1) * 128], lhsT=RT[0:33, blk * 128:(blk + 1) * 128],
                        rhs=wgm[:, ld * 512 + h * 128:ld * 512 + (h + 1) * 128], start=True, stop=True),
                        reads=[R["RT"], R["WG"]], writes=[R_PS[b]], same_ok=True)
                c.op("act", lambda: nc.scalar.activation(out=G[:, blk, :], in_=PS[b][:, 0:256], func=AF.Exp, scale=-1.0),
                     reads=[R_PS[b]], writes=[R["G"]])
                c.op("act", lambda: nc.scalar.activation(out=G[:, blk, :], in_=G[:, blk, :], func=AF.Ln, bias=1.0, scale=1.0),
                     reads=[R["G"]], writes=[R["G"]])

            if STOP_AT == 3:
                return

            def chain(seq, ld, blocks, init):
                sbi = 0
                if init is None:
                    c.op("dve", lambda: nc.vector.memset(S32, 0.0), writes=[R["S32"]])
                else:
                    c.dma("sp", S32, init, writes=[R["S32"]])
                c.op("act", lambda: nc.scalar.copy(out=SBF[0], in_=S32), reads=[R["S32"]], writes=[R_SBF[0]])
                if seq == "s" and ld == 1:
                    c.op("dve", lambda: nc.vector.memset(DL, 1.0), writes=[R["DL"]])
                cmat = C_UI if ld == 0 else C_LI
                rmat = C_LS if ld == 0 else C_US
                last = 127 if ld == 0 else 0
                def prep(blk):
                    Gd = G[:, blk, ld * 128:(ld + 1) * 128]
                    c.op("pe", lambda: nc.tensor.matmul(PS[3][:, 0:128], lhsT=Gd, rhs=CON[:, cmat, :], start=True, stop=True),
                         reads=[R["G"], self.R_CON], writes=[R_PS[3]], same_ok=True)
                    c.op("pe", lambda: nc.tensor.matmul(PS[4][:, 0:128], lhsT=CON[:, rmat, :], rhs=Gd, start=True, stop=True),
                         reads=[R["G"], self.R_CON], writes=[R_PS[4]], same_ok=True)
                    e1, e2, e3 = etmp(), etmp(), etmp()
                    c.op("act", lambda: nc.scalar.activation(out=ET[e1], in_=PS[3][:, 0:128], func=AF.Exp, scale=-1.0 / 16),
                         reads=[R_PS[3]], writes=[R_ET[e1]])
                    c.op("act", lambda: nc.scalar.activation(out=ET[e2], in_=PS[3][:, 0:128], func=AF.Exp, scale=1.0 / 16),
                         reads=[R_PS[3]], writes=[R_ET[e2]])
                    c.op("act", lambda: nc.scalar.activation(out=ET[e3], in_=PS[4][:, 0:128], func=AF.Exp, scale=-1.0 / 16),
                         reads=[R_PS[4]], writes=[R_ET[e3]])
                    qi, ki, ks = tmpb(), tmpb(), tmpb()
                    sl = slice(blk * 128, (blk + 1) * 128)
                    c.op("dve", lambda: nc.vector.tensor_tensor(out=TMPB[qi], in0=QT[:, sl], in1=ET[e1], op=ALU.mult),
                         reads=[R["QT"], R_ET[e1]], writes=[R_TMPB[qi]])
                    c.op("dve", lambda: nc.vector.tensor_tensor(out=TMPB[ki], in0=KT[:, sl], in1=ET[e2], op=ALU.mult),
                         reads=[R["KT"], R_ET[e2]], writes=[R_TMPB[ki]])
                    c.op("dve", lambda: nc.vector.tensor_tensor(out=TMPB[ks], in0=KTOK[:, blk, :], in1=ET[e3], op=ALU.mult),
                         reads=[R["KTOK"], R_ET[e3]], writes=[R_TMPB[ks]])
                    return (e1, qi, ki, ks)

                nxt = prep(blocks[0])
                for bi, blk in enumerate(blocks):
                    e1, qi, ki, ks = nxt
                    if bi + 1 < len(blocks):
                        nxt = prep(blocks[bi + 1])
                    am = 6 + (ami[0] % 2)
                    ami[0] += 1
                    sl = slice(blk * 128, (blk + 1) * 128)
                    c.op("pe", lambda: nc.tensor.matmul(PS[5][:, 0:128], lhsT=TMPB[ki], rhs=TMPB[qi], start=True, stop=True),
                         reads=[R_TMPB[ki], R_TMPB[qi]], writes=[R_PS[5]], same_ok=True)
                    c.op("dve", lambda: nc.vector.tensor_tensor(out=TMPB[am], in0=PS[5][:, 0:128], in1=CON[:, cmat, :], op=ALU.mult),
                         reads=[R_PS[5], self.R_CON], writes=[R_TMPB[am]])
                    c.op("pe", lambda: nc.tensor.matmul(PS[6][:, 0:256], lhsT=TMPB[am], rhs=VTOK[:, blk, :], start=True, stop=False),
                         reads=[R_TMPB[am], R["VTOK"]], writes=[R_PS[6]], same_ok=True)
                    c.op("pe", lambda: nc.tensor.matmul(PS[6][:, 0:256], lhsT=TMPB[qi], rhs=SBF[sbi], start=False, stop=True),
                         reads=[R_TMPB[qi], R_SBF[sbi]], writes=[R_PS[6]], same_ok=True)
                    c.op("pe", lambda: nc.tensor.matmul(PS[7][:, 0:256], lhsT=TMPB[ks], rhs=VTOK[:, blk, :], start=True, stop=True),
                         reads=[R_TMPB[ks], R["VTOK"]], writes=[R_PS[7]], same_ok=True)
                    c.op("dve", lambda: nc.vector.scalar_tensor_tensor(
                        out=S32, in0=S32, scalar=ET[e1][:, last:last + 1], in1=PS[7][:, 0:256], op0=ALU.mult, op1=ALU.add),
                        reads=[R["S32"], R_ET[e1], R_PS[7]], writes=[R["S32"]])
                    sbi = 1 - sbi
                    c.op("act", lambda: nc.scalar.copy(out=SBF[sbi], in_=S32), reads=[R["S32"]], writes=[R_SBF[sbi]])
                    if seq == "s" and ld == 0:
                        o = otmp()
                        c.op("act", lambda: nc.scalar.copy(out=OT[o], in_=PS[6][:, 0:256]), reads=[R_PS[6]], writes=[R_OT[o]])
                        c.dma("sp", olf[h, blk], OT[o], reads=[R_OT[o]], writes=[R_olf[h]], nowaw=True)
                    elif seq == "s":
                        o = otmp()
                        c.dma("sp", OT[o], olf[h, blk], reads=[R_olf[h]], writes=[R_OT[o]])
                        c.op("dve", lambda: nc.vector.tensor_tensor(out=OT[o], in0=OT[o], in1=PS[6][:, 0:256], op=ALU.add),
                             reads=[R_OT[o], R_PS[6]], writes=[R_OT[o]])
                        c.dma("sp", osp[h, blk], OT[o], reads=[R_OT[o]], writes=[R_osum[h]], nowaw=True)
                        qt = qi
                        c.op("dve", lambda: nc.vector.tensor_scalar(out=TMPB[qt], in0=TMPB[qi], scalar1=DL, scalar2=None, op0=ALU.mult),
                             reads=[R_TMPB[qi], R["DL"]], writes=[R_TMPB[qt]])
                        c.dma("act", qts[h, :, sl], TMPB[qt], reads=[R_TMPB[qt]], writes=[R_qts[h]], nowaw=True)
                        c.op("dve", lambda: nc.vector.tensor_tensor(out=DL, in0=DL, in1=ET[e1][:, last:last + 1], op=ALU.mult),
                             reads=[R["DL"], R_ET[e1]], writes=[R["DL"]])
                    elif ld == 0:
                        pb = blk - (16 + 2 * seq)
                        c.op("act", lambda: nc.scalar.copy(out=OP[:, pb, :], in_=PS[6][:, 0:256]), reads=[R_PS[6]], writes=[R["OP"]])
                    else:
                        pb = blk - (16 + 2 * seq)
                        o = otmp()
                        c.op("dve", lambda: nc.vector.tensor_tensor(out=OT[o], in0=OP[:, pb, :], in1=PS[6][:, 0:256], op=ALU.add),
                             reads=[R["OP"], R_PS[6]], writes=[R_OT[o]])
                        self.gla_finalize(OT[o], R_OT[o], blk, h)
                if seq == "s":
                    if ld == 0:
                        c.dma("sp", self.ccg_in[h * 128:(h + 1) * 128, :], S32, reads=[R["S32"]], writes=[self.R_ccg], nowaw=True)
                else:
                    c.dma("sp", dr["ngs"][seq, e, ld, h], S32, reads=[R["S32"]], writes=[self.R_out], nowaw=True)

            chain("s", 0, list(range(16)), dr["s0"][e, h])
            if STOP_AT == 4:
                return
            chain("s", 1, list(range(15, -1, -1)), None)
            if STOP_AT == 5:
                return
            for pi in range(2):
                chain(pi, 0, [16 + 2 * pi, 17 + 2 * pi], None)
                chain(pi, 1, [17 + 2 * pi, 16 + 2 * pi], None)
        if STOP_AT == 6:
            return
        c.collective([self.ccg_in_t.ap().opt()], [self.ccg_out_t.ap().opt()], self.groups,
                     reads=[self.R_ccg], writes=[self.R_ccgo])

    def gla_finalize(self, o_ap, R_o, blk, h, defer=False):
        nc, c = self.nc, self.c
        SM, MB = self.SM, self.MBuf
        fi = self.fin_i % 2
        self.fin_i += 1
        ST = SM[:, 352 + fi * 4:352 + fi * 4 + 4]
        JK = self.XRbuf[:, 7936:8192]
        R_st = self.R_fin[fi]
        XRb = self.XRbuf
        YN = XRb[:, 6656 + fi * 128:6656 + (fi + 1) * 128].bitcast(BF16)
        YF = XRb[:, 6912 + fi * 128:6912 + (fi + 1) * 128].bitcast(BF16).rearrange("p (a t) -> p a t", a=2)
        c.op("act", lambda: nc.scalar.activation(out=JK, in_=o_ap, func=AF.Square, accum_out=ST[:, 0:1]),
             reads=[R_o], writes=[R_st, self.R_jk])
        c.op("act", lambda: nc.scalar.activation(out=ST[:, 1:2], in_=ST[:, 0:1], func=AF.Sqrt, scale=1.0 / 256, bias=self.eps_ap),
             reads=[R_st], writes=[R_st])
        c.op("dve", lambda: nc.vector.reciprocal(out=ST[:, 2:3], in_=ST[:, 1:2]), reads=[R_st], writes=[R_st])
        c.op("dve", lambda: nc.vector.scalar_tensor_tensor(out=YN, in0=o_ap, scalar=ST[:, 2:3], in1=self.GN,
                                                           op0=ALU.mult, op1=ALU.mult),
             reads=[R_o, R_st], writes=[self.R_yn[fi]])
        def part_b():
            pt = self.PS[2][:].bitcast(BF16)
            for a in range(2):
                c.op("pe", lambda: nc.tensor.transpose(pt[:, a * 128:(a + 1) * 128], YN[:, a * 128:(a + 1) * 128], self.identb),
                     reads=[self.R_yn[fi], self.R_CON], writes=[self.R_PS[2]], same_ok=True)
            self.copy_any(YF, pt[:, 0:256].rearrange("p (a t) -> p a t", a=2), [self.R_PS[2]], [self.R_yf[fi]])
            t = blk // 4
            c.dma("act", self.yT[8 + 2 * h:10 + 2 * h, :, blk * 128:(blk + 1) * 128].rearrange("k p t -> p k t"), YF,
                  reads=[self.R_yf[fi]], writes=[self.R_yT[t]], nowaw=True)
        if defer:
            return part_b
        part_b()
        return None

    fin_i = 0

    def gla_fix(self, l):
        nc, c = self.nc, self.c
        MB, XR = self.MBuf, self.XRbuf
        bf = lambda ap: ap.bitcast(BF16)
        SA = XR[:, 0:256]
        SB = XR[:, 256:512]
        SIN = bf(MB[:, 0:128])
        QB = [bf(MB[:, 128 + i * 64:128 + (i + 1) * 64]) for i in range(2)]
        OB = [XR[:, 512 + i * 256:512 + (i + 1) * 256] for i in range(2)]
        R_s, R_sin = c.res(), c.res()
        R_QB = [c.res(), c.res()]
        R_OB = [c.res(), c.res()]
        pm = self.PM
        it = 0
        pend_b = None
        for h in range(4):
            c.dma("sp", SA, self.ccg_out[0, h * 128:(h + 1) * 128, :], reads=[self.R_ccgo], writes=[R_s])
            c.dma("sp", SB, self.ccg_out[1, h * 128:(h + 1) * 128, :], reads=[self.R_ccgo], writes=[R_s], nowaw=True)
            c.op("dve", lambda: nc.vector.tensor_scalar(out=SA, in0=SA, scalar1=pm[:, 0:1], scalar2=None, op0=ALU.mult),
                 reads=[R_s, self.R_pm], writes=[R_s])
            c.op("dve", lambda: nc.vector.scalar_tensor_tensor(out=SIN, in0=SB, scalar=pm[:, 1:2], in1=SA, op0=ALU.mult, op1=ALU.add),
                 reads=[R_s, self.R_pm], writes=[R_sin])
            for blk in range(16):
                i = it % 2
                it += 1
                c.dma("sp", QB[i], self.dram["qts"][h, :, blk * 128:(blk + 1) * 128], reads=[self.R_qts[h]], writes=[R_QB[i]])
                c.dma("act", OB[i], self.dram["osp"][h, blk], reads=[self.R_osum[h]], writes=[R_OB[i]])
                b = 3 + (it % 2)
                c.op("pe", lambda: nc.tensor.matmul(self.PS[b][:, 0:256], lhsT=QB[i], rhs=SIN, start=True, stop=True),
                     reads=[R_QB[i], R_sin], writes=[self.R_PS[b]], same_ok=True)
                c.op("dve", lambda: nc.vector.tensor_tensor(out=OB[i], in0=OB[i], in1=self.PS[b][:, 0:256], op=ALU.add),
                     reads=[R_OB[i], self.R_PS[b]], writes=[R_OB[i]])
                pb_new = self.gla_finalize(OB[i], R_OB[i], blk, h, defer=True)
                if pend_b is not None:
                    pend_b()
                pend_b = pb_new
        if pend_b is not None:
            pend_b()

    def attn_push(self, pend, s2):
        s3_new = pend[0]() if pend[0] is not None else None
        if pend[1] is not None:
            pend[1]()
        pend[0], pend[1] = s2, s3_new

    def attn_flush(self, pend):
        s3_new = pend[0]() if pend[0] is not None else None
        if pend[1] is not None:
            pend[1]()
        if s3_new is not None:
            s3_new()
        pend[0], pend[1] = None, None

    def attn_unit(self, q_ap, R_q, segs, es_ap, out_cb, PB, R_PB, ST, R_ST, ui, OTB, R_OTB):
        nc, c = self.nc, self.c
        i = ui % 2
        SS = self.PSsets[i]
        R_S = self.R_PS[3 * i:3 * i + 3]
        nkb = len(segs)
        RS = [R_S[b] for b in range(3) if b * 512 < nkb * 128]
        for kb, (kT, R_k, mask, vb, R_v) in enumerate(segs):
            o_ = SS[:, kb * 128:(kb + 1) * 128]
            c.op("pe", lambda: nc.tensor.matmul(o_, lhsT=kT, rhs=q_ap, start=True, stop=(mask is None)),
                 reads=[R_q, R_k], writes=[R_S[kb // 4]], same_ok=True)
            if mask is not None:
                c.op("pe", lambda: nc.tensor.matmul(o_, lhsT=self.identb, rhs=mask, start=False, stop=True),
                     reads=[self.R_CON, self.R_tab], writes=[R_S[kb // 4]], same_ok=True)
        pb = PB[i]
        st = ST[i]
        PVb, R_PV = self.PS[6 + i], self.R_PS[6 + i]
        c.op("act", lambda: nc.scalar.activation(out=pb[:, 0:nkb * 128], in_=SS[:, 0:nkb * 128], func=AF.Exp),
             reads=RS, writes=[R_PB[i]])
        otp = PVb[:, 256:320].bitcast(BF16)

        def stage3():
            c.op("pe", lambda: nc.tensor.transpose(otp, OTB[i], self.identb),
                 reads=[R_OTB[i], self.R_CON], writes=[R_PV], same_ok=True)
            out_cb(otp, R_PV)

        def stage2():
            for kb, (kT, R_k, mask, vb, R_v) in enumerate(segs):
                c.op("pe", lambda: nc.tensor.matmul(PVb[:, 0:129], lhsT=pb[:, kb * 128:(kb + 1) * 128], rhs=vb,
                                                    start=(kb == 0), stop=(kb == nkb - 1)),
                     reads=[R_v, R_PB[i]], writes=[R_PV], same_ok=True)
            if es_ap is not None:
                c.op("dve", lambda: nc.vector.tensor_tensor(out=st[:, 0:1], in0=PVb[:, 128:129], in1=es_ap, op=ALU.add),
                     reads=[R_PV, self.R_sink], writes=[R_ST[i]])
                c.op("dve", lambda: nc.vector.reciprocal(out=st[:, 1:2], in_=st[:, 0:1]), reads=[R_ST[i]], writes=[R_ST[i]])
            else:
                c.op("dve", lambda: nc.vector.reciprocal(out=st[:, 1:2], in_=PVb[:, 128:129]), reads=[R_PV], writes=[R_ST[i]])
            c.op("act", lambda: nc.scalar.activation(out=OTB[i], in_=PVb[:, 0:128], func=AF.Copy, scale=st[:, 1:2]),
                 reads=[R_PV, R_ST[i]], writes=[R_OTB[i]])
            return stage3
        return stage2

    def win_attn(self, l):
        nc, c = self.nc, self.c
        e = l // 2
        MB, XR, SM, CON = self.MBuf, self.XRbuf, self.SM, self.CON
        PS, R_PS = self.PS, self.R_PS
        dr = self.dram
        bf = lambda ap: ap.bitcast(BF16)
        QT = [bf(MB[:, i * 1280:(i + 1) * 1280]) for i in range(2)]
        KT = bf(MB[:, 2560:3904])
        VT = bf(MB[:, 3904:5290]).rearrange("p (b d) -> p b d", b=21)
        CK = bf(MB[:, 5290:5546])
        CV = bf(MB[:, 5546:5810]).rearrange("p (b d) -> p b d", b=4)
        PB = [bf(MB[:, 5810 + i * 512:5810 + (i + 1) * 512]) for i in range(2)]
        YA = [bf(MB[:, 6834 + i * 64:6834 + (i + 1) * 64]) for i in range(2)]
        HS = [bf(MB[:, 6962 + i * 128:6962 + (i + 1) * 128]) for i in range(3)]
        ST = [MB[:, 7346 + i * 8:7346 + (i + 1) * 8] for i in range(2)]
        KVO = [MB[:, 7362 + i * 256:7362 + (i + 1) * 256] for i in range(2)]
        SINK = MB[:, 7874:7882]
        ES = MB[:, 7882:7890]
        OTB = [bf(MB[:, 7890 + i * 64:7890 + (i + 1) * 64]) for i in range(2)]
        PTS = R_PTS = None
        R_OTB = [c.res(), c.res()]
        RC = XR[:, 0:2048]
        RS_ = XR[:, 2048:4096]
        KF = [XR[:, 4096 + i * 512:4096 + (i + 1) * 512] for i in range(2)]
        T1 = [XR[:, 5120 + i * 512:5120 + (i + 1) * 512] for i in range(2)]
        R = {k: c.res(k) for k in ["KT", "VT", "CK", "CV", "rope", "HSin", "HSout"]}
        R_QT = [c.res(), c.res()]
        R_PB = [c.res(), c.res()]
        R_YA = [c.res(), c.res()]
        R_ST = [c.res(), c.res()]
        R_KVO = [c.res(), c.res()]
        R_KF = [c.res(), c.res()]
        R_T1 = [c.res(), c.res()]
        self.R_sink = c.res("sink")
        self.R_tab = self.R_CON
        c.dma("sp", RC, dr["ropeC"][:, :], writes=[R["rope"]])
        c.dma("act", RS_, dr["ropeS"][:, :], writes=[R["rope"]], nowaw=True)
        c.dma("sp", SINK, dr["sink"][e, :].partition_broadcast(128), writes=[self.R_sink])
        c.op("act", lambda: nc.scalar.activation(out=ES, in_=SINK, func=AF.Exp), reads=[self.R_sink], writes=[self.R_sink])
        c.op("dve", lambda: nc.vector.memset(VT[:, :, 128:129], 1.0), writes=[R["VT"]])
        c.op("dve", lambda: nc.vector.memset(CV[:, :, 128:129], 1.0), writes=[R["CV"]])
        ki = [0]

        def rope_evac(dst_of_tile, R_dst, scale):
            def ev(t, ps, Rb):
                if t >= 4:
                    self.copy_any(dst_of_tile(t), ps, [Rb], [R_dst], scale=scale)
                    return
                i = ki[0] % 2
                ki[0] += 1
                if scale is None:
                    c.op("act", lambda: nc.scalar.copy(out=KF[i], in_=ps), reads=[Rb], writes=[R_KF[i]])
                else:
                    c.op("act", lambda: nc.scalar.activation(out=KF[i], in_=ps, func=AF.Copy, scale=scale), reads=[Rb], writes=[R_KF[i]])
                c.op("pe", lambda: nc.tensor.matmul(PS[3][:], lhsT=CON[:, C_PERM, :], rhs=KF[i], start=True, stop=True),
                     reads=[R_KF[i], self.R_CON], writes=[R_PS[3]], same_ok=True)
                c.op("dve", lambda: nc.vector.tensor_tensor(out=T1[i], in0=PS[3][:], in1=RS_[:, t * 512:(t + 1) * 512], op=ALU.mult),
                     reads=[R_PS[3], R["rope"]], writes=[R_T1[i]])
                c.op("pool", lambda: nc.gpsimd.tensor_tensor(out=KF[i], in0=KF[i], in1=RC[:, t * 512:(t + 1) * 512], op=ALU.mult),
                     reads=[R_KF[i], R["rope"]], writes=[R_KF[i]])
                c.op("dve", lambda: nc.vector.tensor_tensor(out=dst_of_tile(t), in0=KF[i], in1=T1[i], op=ALU.add),
                     reads=[R_KF[i], R_T1[i]], writes=[R_dst])
            return ev

        ui = 0
        pend = [None, None]
        for g in range(2):
            ws = self.load_w(l, [(1024 + g * 128, 128), (1280 + g * 128, 128)])
            c.dma("pool", CK, dr["ck_a"][e, g], writes=[R["CK"]])
            c.dma("pool", CV[:, :, 0:128], dr["cv_a"][e, g].rearrange("(b p) d -> p b d", p=128), writes=[R["CV"]])
            self.proj_fm(ws, 0, range(NT), rope_evac(lambda t: KT[:, t * 512:(t + 1) * 512], R["KT"], None))

            def ev_v(blk, ps, Rb):
                self.copy_any(VT[:, blk, 0:128], ps[:, 128:256], [Rb], [R["VT"]])
                if blk >= 16:
                    i = blk % 2
                    pi, sb_ = (blk - 16) // 2, (blk - 16) % 2
                    c.op("act", lambda: nc.scalar.copy(out=KVO[i], in_=ps[:, 0:256]), reads=[Rb], writes=[R_KVO[i]])
                    c.dma("sp", dr["nk_a"][pi, e, sb_ * 128:(sb_ + 1) * 128, g, :], KVO[i][:, 0:128], reads=[R_KVO[i]], writes=[self.R_out], nowaw=True)
                    c.dma("sp", dr["nv_a"][pi, e, sb_ * 128:(sb_ + 1) * 128, g, :], KVO[i][:, 128:256], reads=[R_KVO[i]], writes=[self.R_out], nowaw=True)
            self.proj_tm(ws, 0, 256, range(NB), ev_v)
            c.op("dve", lambda: nc.vector.tensor_copy(out=HS[0][:, 0:128], in_=KT[:, 1920:2048]), reads=[R["KT"]], writes=[R["HSin"]])
            c.op("dve", lambda: nc.vector.tensor_copy(out=HS[0][:, 128:256], in_=VT[:, 15, 0:128]), reads=[R["VT"]], writes=[R["HSin"]])
            cin, cout, R_cin, R_cout = self.cc_kv[g]
            c.dma("sp", cin.ap(), HS[0], reads=[R["HSin"]], writes=[R_cin])
            c.collective([cin.ap().opt()], [cout.ap().opt()], self.groups, reads=[R_cin], writes=[R_cout])
            co = cout.ap().rearrange("(r p) n -> r p n", r=2)
            c.dma("sp", HS[1], co[0], reads=[R_cout], writes=[R["HSout"]])
            c.dma("sp", HS[2], co[1], reads=[R_cout], writes=[R["HSout"]], nowaw=True)
            c.op("dve", lambda: nc.vector.tensor_scalar(out=HS[1], in0=HS[1], scalar1=self.PM[:, 0:1], scalar2=None, op0=ALU.mult),
                 reads=[R["HSout"], self.R_pm], writes=[R["HSout"]])
            c.op("dve", lambda: nc.vector.scalar_tensor_tensor(out=HS[1], in0=HS[2], scalar=self.PM[:, 1:2], in1=HS[1], op0=ALU.mult, op1=ALU.add),
                 reads=[R["HSout"], self.R_pm], writes=[R["HSout"]])
            c.op("dve", lambda: nc.vector.tensor_copy(out=KT[:, 2560:2688], in_=HS[1][:, 0:128]), reads=[R["HSout"]], writes=[R["KT"]])
            c.op("dve", lambda: nc.vector.tensor_copy(out=VT[:, 20, 0:128], in_=HS[1][:, 128:256]), reads=[R["HSout"]], writes=[R["VT"]])
            for s2 in range(2):
                h0 = 4 * g + 2 * s2
                ws = self.load_w(l, [(h0 * 128, 256)])
                for hh in range(2):
                    self.proj_fm(ws, hh * 128, range(NT),
                                 rope_evac(lambda t, hh=hh: QT[hh][:, t * 512:(t + 1) * 512], R_QT[hh], SC))
                for hh in range(2):
                    hd = h0 + hh
                    for qb in range(NB):
                        qap = QT[hh][:, qb * 128:(qb + 1) * 128]
                        if qb < 16:
                            segs = [(CK[:, k * 128:(k + 1) * 128], R["CK"], None, CV[:, k, 0:129], R["CV"]) for k in range(4)]
                            if qb > 0:
                                segs.append((KT[:, (qb - 1) * 128:qb * 128], R["KT"], self.maskb[0], VT[:, qb - 1, 0:129], R["VT"]))
                            segs.append((KT[:, qb * 128:(qb + 1) * 128], R["KT"], None, VT[:, qb, 0:129], R["VT"]))
                            if qb < 15:
                                segs.append((KT[:, (qb + 1) * 128:(qb + 2) * 128], R["KT"], self.maskb[1], VT[:, qb + 1, 0:129], R["VT"]))
                            else:
                                segs.append((KT[:, 2560:2688], R["KT"], self.maskb[2], VT[:, 20, 0:129], R["VT"]))
                        else:
                            p0 = 16 + 2 * ((qb - 16) // 2)
                            segs = [(KT[:, (p0 + j) * 128:(p0 + j + 1) * 128], R["KT"], None, VT[:, p0 + j, 0:129], R["VT"]) for j in range(2)]

                        def out_cb(ps, Rb, qb=qb, hd=hd):
                            i = qb % 2
                            self.copy_any(YA[i], ps, [Rb], [R_YA[i]])
                            c.dma("sp", self.yT[hd, :, qb * 128:(qb + 1) * 128], YA[i], reads=[R_YA[i]], writes=[self.R_yT[qb // 4]], nowaw=True)
                        s2 = self.attn_unit(qap, R_QT[hh], segs, ES[:, hd:hd + 1], out_cb, PB, R_PB, ST, R_ST, ui, OTB, R_OTB)
                        self.attn_push(pend, s2)
                        ui += 1
                self.attn_flush(pend)


_IN_NAMES = ["xin", "condT", "ada_w", "ada_bT", "norm_gT", "final_gT", "w_in_even", "w_in_odd", "w_out", "consts",
             "ck_a", "cv_a", "ropeC", "ropeS", "sink", "s0", "wg_s", "wg_p", "gla_gn", "ck_c", "cv_c", "na_tab",
             "wsT_s", "wsT_p", "gb_s", "gb_p", "gmlp_gn", "pm", "onesrow"]


def _run(inputs, ncores=8, upto=None, dbg=False):
    I = {k: np.asarray(v) for k, v in inputs.items()}
    prog = Prog(ncores, upto=upto, dbg=dbg)
    nc = prog.build()
    in_maps = []
    for cid in range(ncores):
        m = _prep_core(cid, I)
        in_maps.append({k: np.ascontiguousarray(m[k], dtype=np.float32) for k in _IN_NAMES})
    res = run_bass_kernel_spmd(nc, in_maps, core_ids=list(range(ncores)))
    return res, prog


def kernel(**inputs):
    res, _ = _run(inputs)
    R = res.results
    f32 = np.float32
    y_prompt = np.zeros((16, 256, D), f32)
    y_sample = np.zeros((4, 4096, D), f32)
    nk_a = np.zeros((16, 2, 256, 2, 128), f32)
    nv_a = np.zeros((16, 2, 256, 2, 128), f32)
    ngs = np.zeros((16, 2, 2, 4, 128, 256), f32)
    nk_c = np.zeros((16, 2, 256, 8, 128), f32)
    nv_c = np.zeros((16, 2, 256, 8, 128), f32)
    for cid in range(8):
        r = R[cid]
        b, side = cid // 2, cid % 2
        yo = r["y_out"]
        if side == 0:
            y_sample[b, :NS] = yo[:NS]
        else:
            y_sample[b, NS:] = yo[:NS][::-1]
        for pi in range(2):
            y_prompt[2 * cid + pi] = yo[NS + pi * NPR:NS + (pi + 1) * NPR]
            nk_a[2 * cid + pi] = r["nk_a"][pi]
            nv_a[2 * cid + pi] = r["nv_a"][pi]
            ngs[2 * cid + pi] = r["ngs"][pi]
            nk_c[2 * cid + pi] = r["nk_c"][pi]
            nv_c[2 * cid + pi] = r["nv_c"][pi]
    return (y_prompt, y_sample, nk_a, nv_a, ngs, nk_c, nv_c)


def _phase_B_odd(self, l):
    with self.nc.named_scope(f"na{l}"):
        self.na_attn(l)
        self.fence()
    with self.nc.named_scope(f"gmlp{l}"):
        self.gmlp(l)


def _na_attn(self, l):
    nc, c = self.nc, self.c
    o = l // 2
    MB, XR = self.MBuf, self.XRbuf
    dr = self.dram
    bf = lambda ap: ap.bitcast(BF16)
    QT = bf(MB[:, 0:1280])
    KT = bf(MB[:, 1280:2688])
    VT = bf(MB[:, 2688:4140]).rearrange("p (b d) -> p b d", b=22)
    CK = bf(MB[:, 4140:4396])
    CV = bf(MB[:, 4396:4660]).rearrange("p (b d) -> p b d", b=4)
    PB = [bf(MB[:, 4660 + i * 576:4660 + (i + 1) * 576]) for i in range(2)]
    YA = [bf(MB[:, 5812 + i * 64:5812 + (i + 1) * 64]) for i in range(2)]
    HS = [bf(MB[:, 5940 + i * 256:5940 + (i + 1) * 256]) for i in range(3)]
    ST = [MB[:, 6708 + i * 8:6708 + (i + 1) * 8] for i in range(2)]
    KVO = [MB[:, 6724 + i * 256:6724 + (i + 1) * 256] for i in range(2)]
    NTAB = bf(XR[:, 0:1600]).rearrange("p (a b q) -> p a b q", a=5, b=5)
    OTB = [bf(MB[:, 7236 + i * 64:7236 + (i + 1) * 64]) for i in range(2)]
    R_OTB = [c.res(), c.res()]
    R = {k: c.res(k) for k in ["QT", "KT", "VT", "CK", "CV", "HSin", "HSout", "tab"]}
    R_PB = [c.res(), c.res()]
    R_YA = [c.res(), c.res()]
    R_ST = [c.res(), c.res()]
    R_KVO = [c.res(), c.res()]
    self.R_tab = R["tab"]
    c.op("dve", lambda: nc.vector.memset(VT[:, :, 128:129], 1.0), writes=[R["VT"]])
    c.op("dve", lambda: nc.vector.memset(CV[:, :, 128:129], 1.0), writes=[R["CV"]])
    self.R_sink = self.R_CON
    KPR = 2304
    ui = 0
    pend = [None, None]
    for h in range(8):
        ws = self.load_w(l, [(h * 128, 128), (1024 + h * 128, 128), (2048 + h * 128, 128)])
        c.dma("pool", CK, dr["ck_c"][o, h], writes=[R["CK"]])
        c.dma("pool", CV[:, :, 0:128], dr["cv_c"][o, h].rearrange("(b p) d -> p b d", p=128), writes=[R["CV"]])
        c.dma("pool", NTAB, dr["na_tab"][o, h].rearrange("a (b p) q -> p a b q", p=128), writes=[R["tab"]])
        self.proj_fm(ws, 0, range(NT), lambda t, ps, Rb: self.copy_any(
            QT[:, t * 512:(t + 1) * 512], ps, [Rb], [R["QT"]], scale=SC))

        def ev_k(t, ps, Rb):
            dst = KT[:, t * 512:(t + 1) * 512] if t < 4 else KT[:, KPR:KPR + 512]
            self.copy_any(dst, ps, [Rb], [R["KT"]])
        self.proj_fm(ws, 128, range(NT), ev_k)

        def ev_v(blk, ps, Rb):
            self.copy_any(VT[:, blk, 0:128], ps, [Rb], [R["VT"]])
        self.proj_tm(ws, 256, 128, range(16), ev_v)

        def ev_kv(blk, ps, Rb, h=h):
            self.copy_any(VT[:, blk + 2, 0:128], ps[:, 128:256], [Rb], [R["VT"]])
            i = blk % 2
            pi, sb_ = (blk - 16) // 2, (blk - 16) % 2
            c.op("act", lambda: nc.scalar.copy(out=KVO[i], in_=ps[:, 0:256]), reads=[Rb], writes=[R_KVO[i]])
            c.dma("sp", dr["nk_c"][pi, o, sb_ * 128:(sb_ + 1) * 128, h, :], KVO[i][:, 0:128], reads=[R_KVO[i]], writes=[self.R_out], nowaw=True)
            c.dma("sp", dr["nv_c"][pi, o, sb_ * 128:(sb_ + 1) * 128, h, :], KVO[i][:, 128:256], reads=[R_KVO[i]], writes=[self.R_out], nowaw=True)
        self.proj_tm(ws, 128, 256, range(16, 20), ev_kv)
        c.op("dve", lambda: nc.vector.tensor_copy(out=HS[0][:, 0:256], in_=KT[:, 1792:2048]), reads=[R["KT"]], writes=[R["HSin"]])
        c.op("dve", lambda: nc.vector.tensor_copy(out=HS[0][:, 256:512].rearrange("p (b d) -> p b d", b=2), in_=VT[:, 14:16, 0:128]),
             reads=[R["VT"]], writes=[R["HSin"]])
        cin, cout, R_cin, R_cout = self.cc_na[h]
        c.dma("sp", cin.ap(), HS[0], reads=[R["HSin"]], writes=[R_cin])
        c.collective([cin.ap().opt()], [cout.ap().opt()], self.groups, reads=[R_cin], writes=[R_cout])
        co = cout.ap().rearrange("(r p) n -> r p n", r=2)
        c.dma("sp", HS[1], co[0], reads=[R_cout], writes=[R["HSout"]])
        c.dma("sp", HS[2], co[1], reads=[R_cout], writes=[R["HSout"]], nowaw=True)
        c.op("dve", lambda: nc.vector.tensor_scalar(out=HS[1], in0=HS[1], scalar1=self.PM[:, 0:1], scalar2=None, op0=ALU.mult),
             reads=[R["HSout"], self.R_pm], writes=[R["HSout"]])
        c.op("dve", lambda: nc.vector.scalar_tensor_tensor(out=HS[1], in0=HS[2], scalar=self.PM[:, 1:2], in1=HS[1], op0=ALU.mult, op1=ALU.add),
             reads=[R["HSout"], self.R_pm], writes=[R["HSout"]])
        for hb in range(2):
            c.op("dve", lambda: nc.vector.tensor_copy(out=KT[:, 2048 + hb * 128:2048 + (hb + 1) * 128],
                                                      in_=HS[1][:, (1 - hb) * 128:(2 - hb) * 128]),
                 reads=[R["HSout"]], writes=[R["KT"]])
            c.op("dve", lambda: nc.vector.tensor_copy(out=VT[:, 16 + hb, 0:128], in_=HS[1][:, 256 + (1 - hb) * 128:256 + (2 - hb) * 128]),
                 reads=[R["HSout"]], writes=[R["VT"]])
        for qb in range(NB):
            qap = QT[:, qb * 128:(qb + 1) * 128]
            if qb < 16:
                lo = _na_lo(qb)
                pat = _na_pattern_of(qb)
                k0 = lo * 64
                vb0 = lo // 2
                segs = [(CK[:, k * 128:(k + 1) * 128], R["CK"], None, CV[:, k, 0:129], R["CV"]) for k in range(4)]
                segs += [(KT[:, k0 + j * 128:k0 + (j + 1) * 128], R["KT"], NTAB[:, pat, j, :], VT[:, vb0 + j, 0:129], R["VT"]) for j in range(5)]
            else:
                p0 = 2 * ((qb - 16) // 2)
                segs = [(KT[:, KPR + (p0 + j) * 128:KPR + (p0 + j + 1) * 128], R["KT"], None, VT[:, 18 + p0 + j, 0:129], R["VT"]) for j in range(2)]

            def out_cb(ps, Rb, qb=qb, h=h):
                i = qb % 2
                self.copy_any(YA[i], ps, [Rb], [R_YA[i]])
                c.dma("sp", self.yT[h, :, qb * 128:(qb + 1) * 128], YA[i], reads=[R_YA[i]], writes=[self.R_yT[qb // 4]], nowaw=True)
            s2 = self.attn_unit(qap, R["QT"], segs, None, out_cb, PB, R_PB, ST, R_ST, ui, OTB, R_OTB)
            self.attn_push(pend, s2)
            ui += 1
        self.attn_flush(pend)


def _gmlp(self, l):
    nc, c = self.nc, self.c
    o = l // 2
    MB, XR = self.MBuf, self.XRbuf
    dr = self.dram
    bf = lambda ap: ap.bitcast(BF16)
    UT = bf(MB[:, 0:5120]).rearrange("p (k t) -> p k t", k=8)
    VN = [bf(MB[:, 5120 + i * 512:5120 + (i + 1) * 512]) for i in range(2)]
    YD = [bf(MB[:, 6144 + i * 512:6144 + (i + 1) * 512]).rearrange("p (k t) -> p k t", k=8) for i in range(2)]
    WST = [bf(MB[:, 7168 + i * 256:7168 + (i + 1) * 256]).rearrange("p (g t) -> p g t", g=4) for i in range(2)]
    STS = [MB[:, 7680 + i * 16:7680 + (i + 1) * 16] for i in range(2)]
    VF = [XR[:, i * 1024:(i + 1) * 1024] for i in range(2)]
    GNB = XR[:, 2048:3072]
    BB8 = [XR[:, 3072 + i * 1024:3072 + (i + 1) * 1024].rearrange("p (k t) -> p k t", k=8) for i in range(2)]
    TF = [XR[:, 5120 + i * 512:5120 + (i + 1) * 512].rearrange("p (k t) -> p k t", k=4) for i in range(2)]
    R = {k: c.res(k) for k in ["UT", "WST", "GNB", "BB8"]}
    R_VN = [c.res(), c.res()]
    R_YD = [c.res(), c.res()]
    R_VF = [c.res(), c.res()]
    R_TF = [c.res(), c.res()]
    R_STS = [c.res(), c.res()]
    c.dma("pool", WST[0], dr["wsT_s"][o].rearrange("g j i -> j g i"), writes=[R["WST"]])
    c.dma("pool", WST[1], dr["wsT_p"][o].rearrange("g j i -> j g i"), writes=[R["WST"]], nowaw=True)
    c.dma("sp", GNB, dr["gmlp_gn"][o, :].partition_broadcast(128), writes=[R["GNB"]])
    first = True
    for vi, key in enumerate(["gb_s", "gb_p"]):
        for g in range(4):
            for d2 in range(2):
                c.dma("sp", BB8[vi][:, 2 * g + d2, :], dr[key][o, g, :].partition_broadcast(128), writes=[R["BB8"]], nowaw=not first)
                first = False
    halves = [[(0, 512), (512, 512), (1024, 256)], [(1280, 256), (1536, 512), (2048, 512)]]
    it = 0
    for hf in range(2):
        base = hf * 1280
        for ub in range(2):
            ws = self.load_w(l, [(3072 + ub * 512, 512)])
            for j in range(4):
                cb = ub * 4 + j
                self.proj_fm(ws, j * 128, halves[hf], lambda t, ps, Rb, cb=cb: self.copy_any(
                    UT[:, cb, t[0] - base:t[0] - base + t[1]], ps, [Rb], [R["UT"]]))
        wv = [self.load_w(l, [(4096 + vb * 512, 512)]) for vb in range(2)]
        def vproj(blk, i):
            for vb in range(2):
                self.proj_tm(wv[vb], 0, 512, [blk], lambda b_, ps, Rb, vb=vb: self.copy_any(
                    VF[i][:, vb * 512:(vb + 1) * 512], ps, [Rb], [R_VF[i]]))

        blks = list(range(hf * 10, hf * 10 + 10))
        vproj(blks[0], it % 2)
        for bi, blk in enumerate(blks):
            i = it % 2
            it += 1
            vi = 0 if blk < 16 else 1
            if bi + 1 < len(blks):
                vproj(blks[bi + 1], it % 2)
            st = STS[i]
            for vb in range(2):
                c.op("dve", lambda: nc.vector.bn_stats(out=st[:, vb * 6:(vb + 1) * 6], in_=VF[i][:, vb * 512:(vb + 1) * 512]),
                     reads=[R_VF[i]], writes=[R_STS[i]])
            c.op("dve", lambda: nc.vector.bn_aggr(out=st[:, 12:14], in_=st[:, 0:12]), reads=[R_STS[i]], writes=[R_STS[i]])
            c.op("act", lambda: nc.scalar.activation(out=st[:, 14:15], in_=st[:, 13:14], func=AF.Sqrt, scale=1.0, bias=self.eps5_ap),
                 reads=[R_STS[i], self.R_CON], writes=[R_STS[i]])
            c.op("dve", lambda: nc.vector.reciprocal(out=st[:, 15:16], in_=st[:, 14:15]), reads=[R_STS[i]], writes=[R_STS[i]])
            c.op("dve", lambda: nc.vector.tensor_scalar(out=VF[i], in0=VF[i], scalar1=st[:, 12:13], scalar2=st[:, 15:16],
                                                        op0=ALU.subtract, op1=ALU.mult),
                 reads=[R_VF[i], R_STS[i]], writes=[R_VF[i]])
            c.op("dve", lambda: nc.vector.tensor_tensor(out=VN[i], in0=VF[i], in1=GNB, op=ALU.mult),
                 reads=[R_VF[i], R["GNB"]], writes=[R_VN[i]])
            for q4 in range(2):
                bank = 4 + q4
                for j in range(4):
                    cb = q4 * 4 + j
                    c.op("pe", lambda: nc.tensor.matmul(self.PS[bank][:, j * 128:(j + 1) * 128], lhsT=VN[i][:, cb * 128:(cb + 1) * 128],
                                                        rhs=WST[vi][:, cb // 2, :], start=True, stop=True),
                         reads=[R_VN[i], R["WST"]], writes=[self.R_PS[bank]], same_ok=True)
                c.op("dve", lambda: nc.vector.tensor_tensor(
                    out=TF[q4], in0=self.PS[bank][:].rearrange("p (k t) -> p k t", k=4), in1=BB8[vi][:, q4 * 4:(q4 + 1) * 4, :], op=ALU.add),
                    reads=[self.R_PS[bank], R["BB8"]], writes=[R_TF[q4]])
                tl = blk * 128 - base
                c.op("dve", lambda: nc.vector.tensor_tensor(
                    out=YD[i][:, q4 * 4:(q4 + 1) * 4, :], in0=TF[q4], in1=UT[:, q4 * 4:(q4 + 1) * 4, tl:tl + 128], op=ALU.mult),
                    reads=[R_TF[q4], R["UT"]], writes=[R_YD[i]])
            c.dma("sp", self.yT[8:16, :, blk * 128:(blk + 1) * 128].rearrange("k p t -> p k t"), YD[i],
                  reads=[R_YD[i]], writes=[self.R_yT[blk // 4]], nowaw=True)


Prog.phase_B_odd = _phase_B_odd
Prog.na_attn = _na_attn
Prog.gmlp = _gmlp
```

```python
import contextlib
import numpy as np
import concourse.bass as bass
import concourse.mybir as mybir
from concourse.bass_utils import run_bass_kernel_spmd

F32 = mybir.dt.float32
BF16 = mybir.dt.bfloat16
AF = mybir.ActivationFunctionType
ALU = mybir.AluOpType
AX = mybir.AxisListType

D = 2048
KC = 16
NS = 2048
NPR = 256
TOK = NS + 2 * NPR
NT = TOK // 512
NB = TOK // 128
IN_EVEN = 5664
IN_ODD = 7168
NEG = -1.0e9
SC = 128 ** -0.5

import os
STOP_AT = int(os.environ.get("KSTOP", "0"))
SEM_ROLL = int(os.environ.get("KROLL", "12000"))
SAME_ENGINE_SYNC = True


class Res:
    __slots__ = ("name", "w", "r", "dsem", "dcnt", "excl")

    def __init__(self, name=""):
        self.name = name
        self.excl = False
        self.w = {}
        self.r = {}
        self.dsem = {}
        self.dcnt = {}


class Ctx:
    def __init__(self, nc):
        self.nc = nc
        self.eng = {"pe": nc.tensor, "act": nc.scalar, "dve": nc.vector,
                    "pool": nc.gpsimd, "sp": nc.sync}
        self.esem = {}
        self.ecnt = {}
        self.seen = {k: {} for k in self.eng}
        self.nsem = 0
        self.nwait = 0
        self.nops = {k: 0 for k in self.eng}
        self.all_res = []
        self.pool = {"sw": [], "hw": [], "cc": []}
        self.ccs = []
        for k in self.eng:
            self._new_esem(k)

    def recycle(self):
        for r in self.all_res:
            for kind, sem in r.dsem.items():
                self.pool[kind].append((sem, r.dcnt[kind]))
            r.dsem = {}
            r.dcnt = {}
            r.w = {}
            r.r = {}
        self.pool["cc"].extend(self.ccs)
        self.ccs = []

    def res(self, name=""):
        r = Res(name)
        self.all_res.append(r)
        return r

    def _alloc_sem(self, name):
        self.nsem += 1
        return self.nc.alloc_semaphore(name=f"{name}_{self.nsem}")

    def _new_esem(self, k):
        self.esem[k] = self._alloc_sem("e" + k)
        self.ecnt[k] = 0

    def _wait(self, e, deps):
        seen = self.seen[e]
        for sem, val in deps.items():
            if seen.get(sem, 0) < val:
                self.eng[e].wait_ge(sem, val)
                self.nwait += 1
                seen[sem] = val

    def _deps(self, reads, writes, skip_sem=None, own_sem=None):
        deps = {}

        def add(s, v):
            if s is skip_sem:
                return
            if deps.get(s, 0) < v:
                deps[s] = v
        for r in reads:
            for s, v in r.w.items():
                add(s, v)
            if r.excl:
                for s, v in r.r.items():
                    if s is not own_sem:
                        add(s, v)
        for w in writes:
            for s, v in w.w.items():
                add(s, v)
            for s, v in w.r.items():
                add(s, v)
        return deps

    def _commit(self, ev, reads, writes, merge=False):
        s, v = ev
        for r in reads:
            if r.r.get(s, 0) < v:
                r.r[s] = v
        for w in writes:
            if merge:
                w.w = dict(w.w)
                w.w[s] = v
            else:
                w.w = {s: v}
            w.r = {}

    def op(self, e, fn, reads=(), writes=(), same_ok=False):
        skip = self.esem[e] if (same_ok or not SAME_ENGINE_SYNC) else None
        deps = self._deps(reads, writes, skip_sem=skip, own_sem=self.esem[e])
        self._wait(e, deps)
        if self.ecnt[e] >= SEM_ROLL:
            self._new_esem(e)
        inst = fn()
        self.ecnt[e] += 1
        self.nops[e] += 1
        inst.then_inc(self.esem[e], 1)
        ev = (self.esem[e], self.ecnt[e])
        self._commit(ev, reads, writes)
        return ev

    def dma(self, q, out, in_, reads=(), writes=(), nowaw=False, **kw):
        sr = writes[0]
        kind = "sw" if q == "pool" else "hw"
        dsem = sr.dsem.get(kind)
        deps = self._deps(reads, writes, skip_sem=dsem if nowaw else None)
        self._wait(q, deps)
        if dsem is None:
            if self.pool[kind]:
                dsem, cnt = self.pool[kind].pop()
            else:
                dsem, cnt = self._alloc_sem("d"), 0
            sr.dsem[kind] = dsem
            sr.dcnt[kind] = cnt
        inst = self.eng[q].dma_start(out=out, in_=in_, **kw)
        sr.dcnt[kind] += 1
        self.nops[q] += 1
        inst.then_inc(dsem, 16)
        ev = (dsem, 16 * sr.dcnt[kind])
        self._commit(ev, reads, writes, merge=nowaw)
        return ev

    def collective(self, ins, outs, groups, reads=(), writes=()):
        deps = self._deps(reads, writes)
        self._wait("pool", deps)
        if self.pool["cc"]:
            sem, cnt = self.pool["cc"].pop()
        else:
            sem, cnt = self._alloc_sem("cc"), 0
        inst = self.nc.gpsimd.collective_compute(
            "AllGather", ALU.bypass, replica_groups=groups, ins=ins, outs=outs)
        inst.then_inc(sem, 1)
        ev = (sem, cnt + 1)
        self.ccs.append((sem, cnt + 1))
        self._commit(ev, reads, writes)
        return ev

    def wait_all(self, e, ress=None):
        deps = {}
        for r in (ress if ress is not None else self.all_res):
            for s, v in list(r.r.items()) + list(r.w.items()):
                if deps.get(s, 0) < v:
                    deps[s] = v
        self._wait(e, deps)


def _consts():
    p = np.arange(128)[:, None]
    f = np.arange(128)[None, :]
    ident = (p == f)
    u_incl = (p <= f)
    l_incl = (p >= f)
    l_strict = (p > f)
    u_strict = (p < f)
    ones = np.ones((128, 128), bool)
    partner = np.where((np.arange(128) % 64) < 32, np.arange(128) + 32, np.arange(128) - 32)
    perm = (p == partner[None, :])
    mats = [ident, u_incl, l_incl, l_strict, u_strict, ones, perm]
    c = np.stack([m.astype(np.float32) for m in mats], axis=1)
    prev = np.where(f >= p, 0.0, NEG)
    nxt = np.where(f <= p, 0.0, NEG)
    halo = np.where(p + f >= 127, 0.0, NEG)
    m = np.stack([prev.T, nxt.T, halo.T], axis=1).astype(np.float32)
    return np.ascontiguousarray(np.concatenate([c, m], axis=1))


C_ID, C_UI, C_LI, C_LS, C_US, C_ONE, C_PERM, C_MPREV, C_MNEXT, C_MHALO = range(10)


def _fm(v):
    return np.ascontiguousarray(v.reshape(KC, 128).T)


def _glob_pos(side, t):
    return t if side == 0 else 4095 - t


def _rope_tables(side):
    t = np.arange(NS)
    g = _glob_pos(side, t)
    pos = np.stack([g // 64, g % 64], axis=-1).astype(np.float32)
    nf = 32
    inv = (np.float32(10000.0) ** (-np.arange(nf, dtype=np.float32) / nf)).astype(np.float32)
    ang = pos[:, :, None] * inv
    cos = np.cos(ang).astype(np.float32)
    sin = np.sin(ang).astype(np.float32)
    C = np.zeros((128, NS), np.float32)
    S = np.zeros((128, NS), np.float32)
    for half in range(2):
        for ab in range(2):
            d0 = half * 64 + ab * 32
            C[d0:d0 + 32, :] = cos[:, half, :].T
            S[d0:d0 + 32, :] = sin[:, half, :].T * (-1.0 if ab == 0 else 1.0)
    return C, S


NA_PAT_M = [0, 1, 2, 14, 15]


def _na_pattern_of(m):
    if m <= 1:
        return m
    if m >= 14:
        return m - 11
    return 2


def _na_lo(m):
    return min(max(2 * m - 4, 0), 26)


def _na_tables(side, na_bias_o):
    out = np.full((8, 5, 128, 640), NEG, np.float32)
    for pi, m in enumerate(NA_PAT_M):
        lo = _na_lo(m)
        qt = m * 128 + np.arange(128)
        qg = _glob_pos(side, qt)
        qr, qc = qg // 64, qg % 64
        kl = lo * 64 + np.arange(640)
        own = kl < NS
        up = np.maximum(kl - NS, 0)
        ptok = np.where(up < 128, 1920 + up, 1664 + up)
        kg = np.where(own, _glob_pos(side, np.minimum(kl, NS - 1)), _glob_pos(1 - side, ptok))
        kr, kc = kg // 64, kg % 64
        rs = np.clip(qr - 4, 0, 56)[:, None]
        cs = np.clip(qc - 8, 0, 48)[:, None]
        valid = (kr[None] >= rs) & (kr[None] < rs + 8) & (kc[None] >= cs) & (kc[None] < cs + 16)
        dr = np.clip(kr[None] - qr[:, None] + 7, 0, 14)
        dc = np.clip(kc[None] - qc[:, None], -15, 15) + 15
        for h in range(8):
            b = na_bias_o[h][dr, dc]
            out[h, pi] = np.where(valid, b, NEG)
    return out


def _prep_core(cid, I):
    b, side = cid // 2, cid % 2
    xs = I["x_sample"][b]
    xloc = xs[:NS] if side == 0 else xs[::-1][:NS]
    m = {}
    m["xin"] = np.ascontiguousarray(np.concatenate(
        [xloc, I["x_prompt"][2 * cid], I["x_prompt"][2 * cid + 1]], axis=0))
    m["condT"] = np.ascontiguousarray(np.stack([_fm(I["c"][b]), _fm(I["c_ctx"])], axis=-1))
    m["ada_w"] = I["ada_w"]
    m["ada_bT"] = np.ascontiguousarray(I["ada_b"].reshape(4, 48, 128).transpose(0, 2, 1))
    m["norm_gT"] = np.ascontiguousarray(I["norm_g"].reshape(4, KC, 128).transpose(0, 2, 1))
    m["final_gT"] = _fm(I["final_norm_g"])
    m["w_in_even"] = I["w_in_even"]
    m["w_in_odd"] = I["w_in_odd"]
    m["w_out"] = I["w_out"]
    m["consts"] = _consts()
    m["ck_a"] = np.ascontiguousarray(I["cache_attn_k"][b].transpose(0, 2, 3, 1))
    m["cv_a"] = np.ascontiguousarray(I["cache_attn_v"][b].transpose(0, 2, 1, 3))
    rc, rs = _rope_tables(side)
    m["ropeC"], m["ropeS"] = rc, rs
    m["sink"] = np.ascontiguousarray(I["attn_sink"])
    m["s0"] = np.ascontiguousarray(I["state_gla"][b][:, side])
    wg = I["gla_wg"]
    bg = I["gla_bg"]

    def wg_mat(first_dir):
        W = np.zeros((2, 33, 1024), np.float32)
        for e in range(2):
            for ld in range(2):
                d = first_dir if ld == 0 else 1 - first_dir
                W[e, d * 16:(d + 1) * 16, ld * 512:(ld + 1) * 512] = wg[e, d]
                W[e, 32, ld * 512:(ld + 1) * 512] = bg[e, d]
        return W
    m["wg_s"] = wg_mat(side)
    m["wg_p"] = wg_mat(0)
    m["gla_gn"] = np.ascontiguousarray(I["gla_norm_g"])
    m["ck_c"] = np.ascontiguousarray(I["cache_na_k"][b].transpose(0, 2, 3, 1))
    m["cv_c"] = np.ascontiguousarray(I["cache_na_v"][b].transpose(0, 2, 1, 3))
    m["na_tab"] = np.ascontiguousarray(np.stack([_na_tables(side, I["na_bias"][o]) for o in range(2)]).transpose(0, 1, 2, 4, 3))
    ws = I["gmlp_ws"]
    wsT = np.ascontiguousarray(ws.transpose(0, 1, 3, 2))
    m["wsT_p"] = wsT
    m["wsT_s"] = wsT if side == 0 else np.ascontiguousarray(wsT[:, :, ::-1, ::-1])
    gb = I["gmlp_b"]
    m["gb_p"] = np.ascontiguousarray(gb)
    m["gb_s"] = np.ascontiguousarray(gb if side == 0 else gb[:, :, ::-1])
    m["gmlp_gn"] = np.ascontiguousarray(I["gmlp_norm_g"])
    pm = np.zeros((128, 2), np.float32)
    pm[:, 1 - side] = 1.0
    m["pm"] = pm
    m["onesrow"] = np.ones((1, TOK), np.float32)
    return m


class Prog:
    def __init__(self, n_pairs_cores=8, upto=None, dbg=None):
        self.upto = upto
        self.dbgsel = dbg
        self.ncores = n_pairs_cores
        nc = bass.Bass("TRN2", target_bir_lowering=False)
        self.nc = nc
        self.c = Ctx(nc)
        self.es = contextlib.ExitStack()

    def din(self, name, shape, dt=F32):
        return self.nc.dram_tensor(name, list(shape), dt, kind="ExternalInput").ap()

    def dout(self, name, shape, dt=F32):
        return self.nc.dram_tensor(name, list(shape), dt, kind="ExternalOutput").ap()

    def dscr(self, name, shape, dt=F32):
        return self.nc.dram_tensor(name, list(shape), dt)

    def sb(self, name, shape, dt=F32):
        return self.es.enter_context(self.nc.sbuf_tensor(name, list(shape), dt))

    def build(self):
        with self.es:
            self._build()
        return self.nc

    def _build(self):
        nc, c = self.nc, self.c
        P = self
        xin = P.din("xin", [TOK, D])
        condT = P.din("condT", [128, KC, 2])
        ada_w = P.din("ada_w", [4, D, 3 * D])
        ada_bT = P.din("ada_bT", [4, 128, 48])
        norm_gT = P.din("norm_gT", [4, 128, KC])
        final_gT = P.din("final_gT", [128, KC])
        w_in = [P.din("w_in_even", [2, D, IN_EVEN]), P.din("w_in_odd", [2, D, IN_ODD])]
        w_out = P.din("w_out", [4, D, D])
        consts_d = P.din("consts", [128, 10, 128])
        ck_a = P.din("ck_a", [2, 2, 128, 512])
        cv_a = P.din("cv_a", [2, 2, 512, 128])
        ropeC_d = P.din("ropeC", [128, NS])
        ropeS_d = P.din("ropeS", [128, NS])
        sink_d = P.din("sink", [2, 8])
        s0_d = P.din("s0", [2, 4, 128, 256])
        wg_s_d = P.din("wg_s", [2, 33, 1024])
        wg_p_d = P.din("wg_p", [2, 33, 1024])
        gla_gn_d = P.din("gla_gn", [2, 256])
        ck_c = P.din("ck_c", [2, 8, 128, 512])
        cv_c = P.din("cv_c", [2, 8, 512, 128])
        na_tab_d = P.din("na_tab", [2, 8, 5, 640, 128])
        wsT_s_d = P.din("wsT_s", [2, 4, 128, 128])
        wsT_p_d = P.din("wsT_p", [2, 4, 128, 128])
        gb_s_d = P.din("gb_s", [2, 4, 128])
        gb_p_d = P.din("gb_p", [2, 4, 128])
        gmlp_gn_d = P.din("gmlp_gn", [2, 1024])
        pm_d = P.din("pm", [128, 2])
        onesrow_d = P.din("onesrow", [1, TOK])

        y_out = P.dout("y_out", [TOK, D])
        nk_a = P.dout("nk_a", [2, 2, NPR, 2, 128])
        nv_a = P.dout("nv_a", [2, 2, NPR, 2, 128])
        ngs = P.dout("ngs", [2, 2, 2, 4, 128, 256])
        nk_c = P.dout("nk_c", [2, 2, NPR, 8, 128])
        nv_c = P.dout("nv_c", [2, 2, NPR, 8, 128])
        self.dbg_out = P.dout("dbg", [KC, 128, TOK]) if self.dbgsel else None

        xs = P.dscr("xs", [KC, 128, TOK]).ap()
        yT = P.dscr("yT", [KC, 128, TOK], BF16).ap()
        ygT = P.dscr("ygT", [KC, 128, TOK], BF16).ap()
        osp = P.dscr("osp", [4, 16, 128, 256]).ap()
        qts = P.dscr("qts", [4, 128, NS], BF16).ap()
        R_xs = [c.res(f"xs{t}") for t in range(NT)]
        R_yT = [c.res(f"yT{t}") for t in range(NT)]
        R_ygT = [c.res(f"ygT{t}") for t in range(NT)]
        self.R_osp = [[c.res() for _ in range(16)] for _ in range(4)]
        self.R_qts = [c.res() for _ in range(4)]

        BIG = P.sb("BIG", [128, 20480])
        XR = P.sb("XR", [128, 8192])
        WR = P.sb("WR", [128, 12288])
        MB = P.sb("MB", [128, 9472])
        CON = P.sb("CON", [128, 10, 128])
        CONB = P.sb("CONB", [128, 5, 128], BF16)
        MOD = P.sb("MOD", [128, 4, 48, 2])
        MA = P.sb("MA", [128, 4, KC, 2])
        SM = P.sb("SM", [128, 512])
        PSA = self.es.enter_context(nc.psum_tensor("psa", [128, 1536], F32))
        PSB = self.es.enter_context(nc.psum_tensor("psb", [128, 1536], F32))
        PS = [PSA[:, i * 512:(i + 1) * 512] for i in range(3)] + [PSB[:, i * 512:(i + 1) * 512] for i in range(3)]
        PS.append(self.es.enter_context(nc.psum_tensor("ps6", [128, 512], F32)))
        PS.append(self.es.enter_context(nc.psum_tensor("ps7", [128, 512], F32)))
        self.PSsets = [PSA, PSB]
        R_PS = [c.res(f"ps{i}") for i in range(8)]
        for r_ in R_PS:
            r_.excl = True
        self.PS, self.R_PS = PS, R_PS
        self.CON, self.CONB = CON, CONB

        HT = BIG[:].bitcast(BF16)[:, 0:KC * TOK].rearrange("p (k t) -> p k t", k=KC)
        R_HT = [c.res(f"ht{t}") for t in range(NT)]
        R_BIG = R_HT
        WS = [WR[:, i * 4096:(i + 1) * 4096].bitcast(BF16).rearrange("p (k n) -> p k n", k=KC) for i in range(3)]
        R_WS = [c.res(f"w{i}") for i in range(3)]
        self.HT, self.R_HT, self.WS, self.R_WS = HT, R_HT, WS, R_WS
        R_XR = c.res("xr")
        R_MB = c.res("mb")
        R_CON = c.res("con")
        R_MOD = c.res("mod")
        self.wslot = 0

        c.dma("sp", CON[:], consts_d[:, :, :], writes=[R_CON])
        c.op("dve", lambda: nc.vector.tensor_copy(out=CONB[:, 0, :], in_=CON[:, C_ID, :]), reads=[R_CON], writes=[R_CON])
        c.op("dve", lambda: nc.vector.tensor_copy(out=CONB[:, 1, :], in_=CON[:, C_ONE, :]), reads=[R_CON], writes=[R_CON])
        c.op("dve", lambda: nc.vector.tensor_copy(out=CONB[:, 2:5, :], in_=CON[:, C_MPREV:C_MHALO + 1, :]), reads=[R_CON], writes=[R_CON])
        self.maskb = [CONB[:, 2 + i, :] for i in range(3)]
        ident = CON[:, C_ID, :]
        identb = CONB[:, 0, :]
        onesb = CONB[:, 1, :]
        self.identb, self.onesb = identb, onesb
        self.olf = P.dscr("olf", [4, 16, 128, 256]).ap()
        self.ccg_in_t = P.dscr("ccg_in", [512, 256])
        self.ccg_out_t = P.dscr("ccg_out", [1024, 256])
        self.ccg_in = self.ccg_in_t.ap()
        self.ccg_out = self.ccg_out_t.ap().rearrange("(r p) n -> r p n", r=2)
        self.R_ccgo = c.res("ccgo")
        self.groups = [[2 * i, 2 * i + 1] for i in range(self.ncores // 2)]
        self.PM = SM[:, 360:362]
        self.R_pm = c.res("pm")
        c.dma("sp", self.PM, pm_d[:, :], writes=[self.R_pm])
        self.cc_kv = [(P.dscr(f"cckv_in{g}", [128, 256], BF16), P.dscr(f"cckv_out{g}", [256, 256], BF16), c.res(), c.res())
                      for g in range(2)]
        self.cc_na = [(P.dscr(f"ccna_in{h}", [128, 512], BF16), P.dscr(f"ccna_out{h}", [256, 512], BF16), c.res(), c.res())
                      for h in range(8)]
        self.R_fin = [c.res(), c.res()]
        self.R_jk = c.res()
        self.R_yn = [c.res(), c.res()]
        self.R_yf = [c.res(), c.res()]
        self.R_CON = R_CON

        def fence():
            for e in ("pe", "act", "dve", "pool", "sp"):
                c.wait_all(e)
            c.recycle()

        self.fence = fence

        cond = SM[:, 0:32].rearrange("p (k c) -> p k c", c=2)
        scond = SM[:, 32:64].rearrange("p (k c) -> p k c", c=2)
        R_cond = c.res("cond")
        abT = SM[:, 64:64 + 192].rearrange("p (l n) -> p l n", l=4)
        ngT = SM[:, 256:256 + 64].rearrange("p (l k) -> p l k", l=4)
        fgT = SM[:, 320:336]
        R_small = c.res("small")
        c.dma("sp", cond, condT[:, :, :], writes=[R_cond])
        c.dma("sp", abT, ada_bT.rearrange("l p n -> p l n"), writes=[R_small])
        c.dma("sp", ngT, norm_gT.rearrange("l p k -> p l k"), writes=[R_small], nowaw=True)
        c.dma("sp", fgT, final_gT[:, :], writes=[R_small], nowaw=True)
        c.op("act", lambda: nc.scalar.activation(out=scond, in_=cond, func=AF.Silu), reads=[R_cond], writes=[R_cond])
        AT = [BIG[:, i * 8192:(i + 1) * 8192].rearrange("p (k n) -> p k n", k=KC) for i in range(2)]
        R_AT = [c.res("at0"), c.res("at1")]
        gi = 0
        MROW = XR[0:2, 0:6144]
        R_MROW = c.res("mrow")
        for l in range(4):
            for grp in range(12):
                s = gi % 2
                src = ada_w[l, :, grp * 512:(grp + 1) * 512].rearrange("(k p) n -> p k n", p=128)
                c.dma("sp" if gi % 2 == 0 else "act", AT[s], src, writes=[R_AT[s]])
                bank = 4 + (gi % 2)
                for kc in range(KC):
                    c.op("pe", lambda: nc.tensor.matmul(
                        PS[bank][0:2, :], lhsT=scond[:, kc, :], rhs=AT[s][:, kc, :],
                        start=(kc == 0), stop=(kc == KC - 1)),
                        reads=[R_AT[s], R_cond], writes=[R_PS[bank]], same_ok=True)
                self.copy_any(MROW[:, grp * 512:(grp + 1) * 512], PS[bank][0:2, :], [R_PS[bank]], [R_MROW])
                gi += 1
            for ch in range(48):
                c.op("pe", lambda: nc.tensor.transpose(PS[6][:, 2 * ch:2 * ch + 2], MROW[:, ch * 128:(ch + 1) * 128], ident[0:2, 0:2]),
                     reads=[R_MROW, R_CON], writes=[R_PS[6]], same_ok=True)
            for ci in range(2):
                pv = PS[6][:, 0:96].rearrange("p (j c) -> p j c", c=2)[:, :, ci]
                c.op("dve", lambda: nc.vector.tensor_tensor(
                    out=MOD[:, l, :, ci], in0=pv, in1=abT[:, l, :], op=ALU.add),
                    reads=[R_PS[6], R_small], writes=[R_MOD])
        for l in range(4):
            for ci in range(2):
                c.op("dve", lambda: nc.vector.scalar_tensor_tensor(
                    out=MA[:, l, :, ci], in0=MOD[:, l, 16:32, ci], scalar=1.0, in1=ngT[:, l, :],
                    op0=ALU.add, op1=ALU.mult), reads=[R_MOD, R_small], writes=[R_MOD])
        self.MOD, self.MA, self.R_MOD = MOD, MA, R_MOD
        fence()

        XTOK = [XR[:, i * 2048:(i + 1) * 2048] for i in range(2)]
        XTB = [XR[:, 4096 + i * 2048:4096 + (i + 1) * 2048].rearrange("p (k t) -> p k t", k=KC) for i in range(2)]
        R_XTOK = [c.res(), c.res()]
        R_XTB = [c.res(), c.res()]
        for blk in range(NB):
            s = blk % 2
            c.dma("sp", XTOK[s], xin[blk * 128:(blk + 1) * 128, :], writes=[R_XTOK[s]])
            for g4 in range(4):
                bank = g4 % 4
                for j in range(4):
                    kc = g4 * 4 + j
                    c.op("pe", lambda: nc.tensor.transpose(
                        PS[bank][:, j * 128:(j + 1) * 128], XTOK[s][:, kc * 128:(kc + 1) * 128], ident),
                        reads=[R_XTOK[s], R_CON], writes=[R_PS[bank]], same_ok=True)
                dst = XTB[s][:, g4 * 4:(g4 + 1) * 4, :]
                srcp = PS[bank][:].rearrange("p (j t) -> p j t", j=4)
                if g4 % 2 == 0:
                    c.op("dve", lambda: nc.vector.tensor_copy(out=dst, in_=srcp), reads=[R_PS[bank]], writes=[R_XTB[s]])
                else:
                    c.op("act", lambda: nc.scalar.copy(out=dst, in_=srcp), reads=[R_PS[bank]], writes=[R_XTB[s]])
            t = blk // 4
            c.dma("act", xs[:, :, blk * 128:(blk + 1) * 128].rearrange("k p t -> p k t"), XTB[s][:],
                  reads=[R_XTB[s]], writes=[R_xs[t]], nowaw=True)
        fence()

        XP = [XR[:, i * 2048:(i + 1) * 2048].rearrange("p (k t) -> p k t", k=4) for i in range(4)]
        R_XP = [c.res() for _ in range(4)]
        SQ = [MB[:, i * 1024:(i + 1) * 1024].bitcast(BF16).rearrange("p (k t) -> p k t", k=4) for i in range(2)]
        R_SQ = [c.res(), c.res()]
        RSTD = MB[:, 2048:2560]
        RTMP = MB[:, 2560:3072]
        HTMP = [MB[:, 3072 + i * 512:3072 + (i + 1) * 512] for i in range(2)]
        R_RSTD = c.res()
        R_HTMP = [c.res(), c.res()]
        xp_i = [0]

        def next_xp():
            i = xp_i[0] % 4
            xp_i[0] += 1
            return i

        def phase_A(l):
            for t in range(NT):
                ci = 0 if t < 4 else 1
                bank = 4 + (t % 2)
                for pc in range(4):
                    i = next_xp()
                    c.dma("sp", XP[i], xs[pc * 4:(pc + 1) * 4, :, t * 512:(t + 1) * 512].rearrange("k p t -> p k t"),
                          reads=[R_xs[t]], writes=[R_XP[i]])
                    sq = pc % 2
                    c.op("act", lambda: nc.scalar.activation(out=SQ[sq][:], in_=XP[i][:], func=AF.Square),
                         reads=[R_XP[i]], writes=[R_SQ[sq]])
                    for k in range(4):
                        c.op("pe", lambda: nc.tensor.matmul(
                            PS[bank][:], lhsT=onesb, rhs=SQ[sq][:, k, :],
                            start=(pc == 0 and k == 0), stop=(pc == 3 and k == 3)),
                            reads=[R_SQ[sq], R_CON], writes=[R_PS[bank]], same_ok=True)
                c.op("act", lambda: nc.scalar.activation(out=RTMP, in_=PS[bank][:], func=AF.Sqrt,
                                                         scale=1.0 / D, bias=self.eps_ap),
                     reads=[R_PS[bank], R_CON], writes=[R_RSTD])
                c.op("dve", lambda: nc.vector.reciprocal(out=RSTD, in_=RTMP), reads=[R_RSTD], writes=[R_RSTD])
                for pc in range(4):
                    i = next_xp()
                    c.dma("sp", XP[i], xs[pc * 4:(pc + 1) * 4, :, t * 512:(t + 1) * 512].rearrange("k p t -> p k t"),
                          reads=[R_xs[t]], writes=[R_XP[i]])
                    for k in range(4):
                        kc = pc * 4 + k
                        hs = kc % 2
                        c.op("dve", lambda: nc.vector.tensor_tensor(out=HTMP[hs], in0=XP[i][:, k, :], in1=RSTD, op=ALU.mult),
                             reads=[R_XP[i], R_RSTD], writes=[R_HTMP[hs]])
                        c.op("act", lambda: nc.scalar.activation(
                            out=HT[:, kc, t * 512:(t + 1) * 512], in_=HTMP[hs], func=AF.Identity,
                            scale=MA[:, l, kc, ci:ci + 1], bias=MOD[:, l, kc, ci:ci + 1]),
                            reads=[R_HTMP[hs], R_MOD], writes=[R_HT[t]])

        self.eps_ap = SM[:, 336:337]
        self.eps5_ap = SM[:, 337:338]
        c.op("dve", lambda: nc.vector.memset(self.eps_ap, 1e-6), writes=[R_CON])
        c.op("dve", lambda: nc.vector.memset(self.eps5_ap, 1e-5), writes=[R_CON])

        def phase_C(l):
            WO = BIG[:].bitcast(BF16)[:, 0:KC * D].rearrange("p (k n) -> p k n", k=KC)
            R_WO = [c.res() for _ in range(4)]
            for q4 in range(4):
                c.dma("pool", WO[:, :, q4 * 512:(q4 + 1) * 512],
                      w_out[l, :, q4 * 512:(q4 + 1) * 512].rearrange("(k p) n -> p k n", p=128),
                      writes=[R_WO[q4]] + R_HT, nowaw=True)
            for t in range(NT):
                ci = 0 if t < 4 else 1
                ws = self.next_w()
                c.dma("act", WS[ws][:], ygT[:, :, t * 512:(t + 1) * 512].rearrange("k p t -> p k t"),
                      reads=[R_ygT[t]], writes=[R_WS[ws]])
                for pc in range(4):
                    i = next_xp()
                    c.dma("sp", XP[i], xs[pc * 4:(pc + 1) * 4, :, t * 512:(t + 1) * 512].rearrange("k p t -> p k t"),
                          reads=[R_xs[t]], writes=[R_XP[i]])
                    for k in range(4):
                        dc = pc * 4 + k
                        bank = dc % 4
                        for kc in range(KC):
                            c.op("pe", lambda: nc.tensor.matmul(
                                PS[bank][:], lhsT=WO[:, kc, dc * 128:(dc + 1) * 128], rhs=WS[ws][:, kc, :],
                                start=(kc == 0), stop=(kc == KC - 1)),
                                reads=[R_WS[ws], R_WO[dc // 4]], writes=[R_PS[bank]], same_ok=True)
                        c.op("dve", lambda: nc.vector.scalar_tensor_tensor(
                            out=XP[i][:, k, :], in0=PS[bank][:], scalar=MOD[:, l, 32 + dc, ci:ci + 1],
                            in1=XP[i][:, k, :], op0=ALU.mult, op1=ALU.add),
                            reads=[R_PS[bank], R_MOD, R_XP[i]], writes=[R_XP[i]])
                    c.dma("sp", xs[pc * 4:(pc + 1) * 4, :, t * 512:(t + 1) * 512].rearrange("k p t -> p k t"), XP[i],
                          reads=[R_XP[i]], writes=[R_xs[t]])

        def phase_F():
            YTOK = [MB[:, 5120 + i * 2048:5120 + (i + 1) * 2048] for i in range(2)]
            R_YTOK = [c.res(), c.res()]
            FB = [MB[:, 4096 + i * 512:4096 + (i + 1) * 512].rearrange("p (k t) -> p k t", k=4) for i in range(2)]
            R_FB = [c.res(), c.res()]
            RS2 = SM[:, 0:128]
            RT2 = SM[:, 128:256]
            for blk in range(NB):
                t = blk // 4
                bank = 4 + (blk % 2)
                ys = blk % 2
                xsl = lambda pc: xs[pc * 4:(pc + 1) * 4, :, blk * 128:(blk + 1) * 128].rearrange("k p t -> p k t")
                for pc in range(4):
                    i = next_xp()
                    xv = XP[i][:, :, 0:128]
                    c.dma("sp", xv, xsl(pc), reads=[R_xs[t]], writes=[R_XP[i]])
                    sq = pc % 2
                    c.op("act", lambda: nc.scalar.activation(out=SQ[sq][:, :, 0:128], in_=xv, func=AF.Square),
                         reads=[R_XP[i]], writes=[R_SQ[sq]])
                    for k in range(4):
                        c.op("pe", lambda: nc.tensor.matmul(
                            PS[bank][:, 0:128], lhsT=onesb, rhs=SQ[sq][:, k, 0:128],
                            start=(pc == 0 and k == 0), stop=(pc == 3 and k == 3)),
                            reads=[R_SQ[sq], R_CON], writes=[R_PS[bank]], same_ok=True)
                c.op("act", lambda: nc.scalar.activation(out=RT2, in_=PS[bank][:, 0:128], func=AF.Sqrt,
                                                         scale=1.0 / D, bias=self.eps_ap),
                     reads=[R_PS[bank], R_CON], writes=[R_RSTD])
                c.op("dve", lambda: nc.vector.reciprocal(out=RS2, in_=RT2), reads=[R_RSTD], writes=[R_RSTD])
                for pc in range(4):
                    i = next_xp()
                    xv = XP[i][:, :, 0:128]
                    c.dma("sp", xv, xsl(pc), reads=[R_xs[t]], writes=[R_XP[i]])
                    fb = pc % 2
                    for k in range(4):
                        kc = pc * 4 + k
                        c.op("dve", lambda: nc.vector.scalar_tensor_tensor(
                            out=FB[fb][:, k, :], in0=xv[:, k, :], scalar=fgT[:, kc:kc + 1], in1=RS2,
                            op0=ALU.mult, op1=ALU.mult), reads=[R_XP[i], R_RSTD, R_small], writes=[R_FB[fb]])
                    pb = pc % 4
                    for k in range(4):
                        c.op("pe", lambda: nc.tensor.transpose(
                            PS[pb][:, k * 128:(k + 1) * 128], FB[fb][:, k, :], ident),
                            reads=[R_FB[fb], R_CON], writes=[R_PS[pb]], same_ok=True)
                    dst = YTOK[ys][:, pc * 512:(pc + 1) * 512]
                    if pc % 2 == 0:
                        c.op("act", lambda: nc.scalar.copy(out=dst, in_=PS[pb][:]), reads=[R_PS[pb]], writes=[R_YTOK[ys]])
                    else:
                        c.op("dve", lambda: nc.vector.tensor_copy(out=dst, in_=PS[pb][:]), reads=[R_PS[pb]], writes=[R_YTOK[ys]])
                c.dma("act", y_out[blk * 128:(blk + 1) * 128, :], YTOK[ys], reads=[R_YTOK[ys]], writes=[self.R_out], nowaw=True)

        self.R_out = c.res("out")
        self.R_dbg = c.res("dbg")
        self.R_yT, self.R_ygT, self.yT, self.ygT = R_yT, R_ygT, yT, ygT
        self.w_in = w_in
        self.MBuf, self.XRbuf, self.SM = MB, XR, SM
        self.dram = dict(ck_a=ck_a, cv_a=cv_a, ropeC=ropeC_d, ropeS=ropeS_d, sink=sink_d, s0=s0_d,
                         wg_s=wg_s_d, wg_p=wg_p_d, gla_gn=gla_gn_d, ck_c=ck_c, cv_c=cv_c, na_tab=na_tab_d,
                         wsT_s=wsT_s_d, wsT_p=wsT_p_d, gb_s=gb_s_d, gb_p=gb_p_d, gmlp_gn=gmlp_gn_d, pm=pm_d, onesrow=onesrow_d,
                         nk_a=nk_a, nv_a=nv_a, ngs=ngs, nk_c=nk_c, nv_c=nv_c, osp=osp, qts=qts, xs=xs)

        nlayers = 4 if self.upto is None else self.upto[0]
        stage = None if self.upto is None else self.upto[1]
        def scoped(name, fn, *a):
            with nc.named_scope(name):
                fn(*a)
                fence()

        for l in range(nlayers):
            last = (l == nlayers - 1)
            scoped(f"A{l}", phase_A, l)
            if last and stage == "A":
                self.dump_ht()
                break
            if l % 2 == 0:
                self.phase_B_even(l)
            else:
                self.phase_B_odd(l)
            fence()
            if last and stage in ("B", "B1", "B2"):
                self.dump_fm(yT, R_yT, BF16)
                break
            with nc.named_scope(f"gate{l}"):
                self.phase_gate(l)
            if last and stage == "G":
                fence()
            if last and stage == "G":
                self.dump_fm(ygT, R_ygT, BF16)
                break
            scoped(f"C{l}", phase_C, l)
            if last and stage == "C":
                self.dump_fm(xs, R_xs, F32)
                break
        else:
            scoped("F", phase_F)
        fence()

    def next_w(self):
        i = self.wslot % 3
        self.wslot += 1
        return i

    def load_w(self, l, parts):
        c = self.c
        s = self.next_w()
        wsrc = self.w_in[l % 2]
        off = 0
        first = True
        for col0, n in parts:
            src = wsrc[l // 2, :, col0:col0 + n].rearrange("(k p) n -> p k n", p=128)
            c.dma("pool", self.WS[s][:, :, off:off + n], src, writes=[self.R_WS[s]], nowaw=not first)
            first = False
            off += n
        return s

    def proj_fm(self, ws, wcol, tiles, evac, banks=(0, 1, 2)):
        nc, c = self.nc, self.c
        for t in tiles:
            b = banks[self.pbank % len(banks)]
            self.pbank += 1
            if isinstance(t, tuple):
                t0, n = t
            else:
                t0, n = t * 512, 512
            ti = t0 // 512
            for kc in range(KC):
                c.op("pe", lambda: nc.tensor.matmul(
                    self.PS[b][:, 0:n], lhsT=self.WS[ws][:, kc, wcol:wcol + 128], rhs=self.HT[:, kc, t0:t0 + n],
                    start=(kc == 0), stop=(kc == KC - 1)),
                    reads=[self.R_WS[ws], self.R_HT[ti]], writes=[self.R_PS[b]], same_ok=True)
            evac(t, self.PS[b][:, 0:n], self.R_PS[b])

    def proj_tm(self, ws, wcol, ncols, blocks, evac, banks=(0, 1, 2)):
        nc, c = self.nc, self.c
        for blk in blocks:
            b = banks[self.pbank % len(banks)]
            self.pbank += 1
            t = blk // 4
            for kc in range(KC):
                c.op("pe", lambda: nc.tensor.matmul(
                    self.PS[b][:, 0:ncols], lhsT=self.HT[:, kc, blk * 128:(blk + 1) * 128],
                    rhs=self.WS[ws][:, kc, wcol:wcol + ncols], start=(kc == 0), stop=(kc == KC - 1)),
                    reads=[self.R_WS[ws], self.R_HT[t]], writes=[self.R_PS[b]], same_ok=True)
            evac(blk, self.PS[b][:, 0:ncols], self.R_PS[b])

    pbank = 0
    evi = 0

    def copy_any(self, out, in_, reads, writes, scale=None):
        nc, c = self.nc, self.c
        self.evi += 1
        if self.evi % 2 == 0:
            if scale is None:
                return c.op("act", lambda: nc.scalar.copy(out=out, in_=in_), reads=reads, writes=writes)
            return c.op("act", lambda: nc.scalar.activation(out=out, in_=in_, func=AF.Copy, scale=scale), reads=reads, writes=writes)
        if scale is None:
            return c.op("dve", lambda: nc.vector.tensor_copy(out=out, in_=in_), reads=reads, writes=writes)
        return c.op("dve", lambda: nc.vector.tensor_scalar(out=out, in0=in_, scalar1=scale, scalar2=None, op0=ALU.mult), reads=reads, writes=writes)

    def dump_ht(self):
        c = self.c
        for t in range(NT):
            c.dma("pool", self.dbg_out[:, :, t * 512:(t + 1) * 512].rearrange("k p t -> p k t"),
                  self.HT[:, :, t * 512:(t + 1) * 512], reads=[self.R_HT[t]], writes=[self.R_dbg], nowaw=True)

    def dump_fm(self, src, R, dt):
        c = self.c
        for t in range(NT):
            for k in range(KC):
                c.dma("pool", self.dbg_out[k, :, t * 512:(t + 1) * 512], src[k, :, t * 512:(t + 1) * 512],
                      reads=[R[t]], writes=[self.R_dbg], nowaw=True)

    def phase_gate(self, l):
        nc, c = self.nc, self.c
        MB = self.MBuf
        off = 3616 if l % 2 == 0 else 5120
        YB = [MB[:, i * 1024:(i + 1) * 1024].bitcast(BF16).rearrange("p (k t) -> p k t", k=4) for i in range(2)]
        SG = [MB[:, 2048 + i * 256:2048 + (i + 1) * 256].bitcast(BF16) for i in range(2)]
        R_YB = [c.res(), c.res()]
        R_SG = [c.res(), c.res()]
        it = 0
        for blk in range(4):
            ws = self.load_w(l, [(off + blk * 512, 512)])
            for t in range(NT):
                yb = it % 2
                it += 1
                c.dma("sp", YB[yb][:], self.yT[blk * 4:(blk + 1) * 4, :, t * 512:(t + 1) * 512].rearrange("k p t -> p k t"),
                      reads=[self.R_yT[t]], writes=[R_YB[yb]])
                for j in range(4):
                    sg = j % 2

                    def ev(tt, ps, Rb, j=j, sg=sg, yb=yb):
                        c.op("act", lambda: nc.scalar.activation(out=SG[sg], in_=ps, func=AF.Silu),
                             reads=[Rb], writes=[R_SG[sg]])
                        c.op("dve", lambda: nc.vector.tensor_tensor(out=YB[yb][:, j, :], in0=YB[yb][:, j, :], in1=SG[sg], op=ALU.mult),
                             reads=[R_SG[sg], R_YB[yb]], writes=[R_YB[yb]])
                    self.proj_fm(ws, j * 128, [t], ev, banks=(0, 1, 2, 3, 4, 5))
                c.dma("act", self.ygT[blk * 4:(blk + 1) * 4, :, t * 512:(t + 1) * 512].rearrange("k p t -> p k t"), YB[yb][:],
                      reads=[R_YB[yb]], writes=[self.R_ygT[t]], nowaw=True)

    def phase_B_even(self, l):
        sub = self.upto[1] if self.upto else None
        if sub != "B2":
            with self.nc.named_scope(f"gla{l}"):
                self.gla(l)
                self.fence()
        if sub == "B1":
            return
        with self.nc.named_scope(f"win{l}"):
            self.win_attn(l)
            self.fence()
        if sub == "B2":
            return
        with self.nc.named_scope(f"fix{l}"):
            self.gla_fix(l)

    def gla(self, l):
        nc, c = self.nc, self.c
        e = l // 2
        MB, XR, SM, CON = self.MBuf, self.XRbuf, self.SM, self.CON
        PS, R_PS = self.PS, self.R_PS
        dr = self.dram
        bf = lambda ap: ap.bitcast(BF16)
        QT = bf(MB[:, 0:1280])
        KT = bf(MB[:, 1280:2560])
        KTOK = bf(MB[:, 2560:3840]).rearrange("p (b d) -> p b d", b=NB)
        VTOK = bf(MB[:, 3840:6400]).rearrange("p (b d) -> p b d", b=NB)
        RT = bf(MB[:, 6400:7680])
        WGS = bf(MB[0:33, 7680:8192])
        WGP = bf(MB[0:33, 8192:8704])
        TMPB = [bf(MB[:, 8704 + i * 64:8704 + (i + 1) * 64]) for i in range(8)]
        SBF = [bf(MB[:, 9216 + i * 128:9216 + (i + 1) * 128]) for i in range(2)]
        G = XR[:, 0:5120].rearrange("p (b d) -> p b d", b=NB)
        ET = [XR[:, 5120 + i * 128:5120 + (i + 1) * 128] for i in range(6)]
        S32 = XR[:, 5888:6144]
        OT = [XR[:, 6144 + i * 256:6144 + (i + 1) * 256] for i in range(2)]
        OP = XR[:, 7168:7680].rearrange("p (b d) -> p b d", b=2)
        GN = XR[:, 7680:7936]
        DL = SM[:, 340:341]
        ST = SM[:, 344:352]
        self.GN = GN
        R = {k: c.res(k) for k in ["QT", "KT", "KTOK", "VTOK", "RT", "WG", "G", "S32", "OP", "GN", "DL", "ST"]}
        R_TMPB = [c.res() for _ in range(8)]
        R_SBF = [c.res() for _ in range(2)]
        R_ET = [c.res() for _ in range(6)]
        R_OT = [c.res() for _ in range(2)]
        R_olf = [c.res() for _ in range(4)]
        R_osum = [c.res() for _ in range(4)]
        R_qts = self.R_qts
        self.R_osum = R_osum
        osp, qts = dr["osp"], dr["qts"]
        olf = self.olf
        self.R_ccg = c.res("ccg_in")
        if STOP_AT == -1:
            return
        c.dma("pool", WGS, dr["wg_s"][e, :, :], writes=[R["WG"]])
        c.dma("pool", WGP, dr["wg_p"][e, :, :], writes=[R["WG"]], nowaw=True)
        c.dma("sp", GN, dr["gla_gn"][e, :].partition_broadcast(128), writes=[R["GN"]])
        if STOP_AT == -2:
            return
        ws = self.load_w(l, [(3584, 128)])
        if STOP_AT == -3:
            return
        c.dma("pool", RT[32:33, :], dr["onesrow"][:, :], writes=[R["RT"]])
        if STOP_AT == -4:
            return
        for t in range(NT):
            b = self.pbank % 3
            self.pbank += 1
            for kc in range(KC):
                c.op("pe", lambda: nc.tensor.matmul(
                    PS[b][:], lhsT=self.WS[ws][:, kc, 0:128], rhs=self.HT[:, kc, t * 512:(t + 1) * 512],
                    start=(kc == 0), stop=(kc == KC - 1)),
                    reads=[self.R_WS[ws], self.R_HT[t]], writes=[R_PS[b]], same_ok=True)
            if STOP_AT != -5:
                self.copy_any(RT[0:32, t * 512:(t + 1) * 512], PS[b][0:32, :], [R_PS[b]], [R["RT"]])
        if STOP_AT == 1:
            return
        ti = [0]
        ei = [0]
        oi = [0]

        ami = [0]

        def tmpb():
            i = ti[0] % 6
            ti[0] += 1
            return i

        def etmp():
            i = ei[0] % 6
            ei[0] += 1
            return i

        def otmp():
            i = oi[0] % 2
            oi[0] += 1
            return i

        for h in range(4):
            ws = self.load_w(l, [(1536 + h * 128, 128), (2048 + h * 128, 128), (2560 + h * 256, 256)])
            if STOP_AT == 11:
                return
            self.proj_fm(ws, 0, range(NT), lambda t, ps, Rb: self.copy_any(
                QT[:, t * 512:(t + 1) * 512], ps, [Rb], [R["QT"]], scale=SC))
            if STOP_AT == 12:
                return
            self.proj_fm(ws, 128, range(NT), lambda t, ps, Rb: self.copy_any(
                KT[:, t * 512:(t + 1) * 512], ps, [Rb], [R["KT"]]))
            if STOP_AT == 13:
                return

            def ev_kv(blk, ps, Rb):
                self.copy_any(KTOK[:, blk, :], ps[:, 0:128], [Rb], [R["KTOK"]])
                self.copy_any(VTOK[:, blk, :], ps[:, 128:384], [Rb], [R["VTOK"]])
            self.proj_tm(ws, 128, 384, range(NB), ev_kv)
            if STOP_AT == 2:
                return
            for blk in range(NB):
                wgm = WGS if blk < 16 else WGP
                b = self.pbank % 3
                self.pbank += 1
                for ld in range(2):
                    c.op("pe", lambda: nc.tensor.matmul(
                        PS[b][:, ld * 128:(ld + 1) * 128], lhsT=RT[0:33, blk * 128:(blk + 1) * 128],
                        rhs=wgm[:, ld * 512 + h * 128:ld * 512 + (h + 1) * 128], start=True, stop=True),
                        reads=[R["RT"], R["WG"]], writes=[R_PS[b]], same_ok=True)
                c.op("act", lambda: nc.scalar.activation(out=G[:, blk, :], in_=PS[b][:, 0:256], func=AF.Exp, scale=-1.0),
                     reads=[R_PS[b]], writes=[R["G"]])
                c.op("act", lambda: nc.scalar.activation(out=G[:, blk, :], in_=G[:, blk, :], func=AF.Ln, bias=1.0, scale=1.0),
                     reads=[R["G"]], writes=[R["G"]])

            if STOP_AT == 3:
                return

            def chain(seq, ld, blocks, init):
                sbi = 0
                if init is None:
                    c.op("dve", lambda: nc.vector.memset(S32, 0.0), writes=[R["S32"]])
                else:
                    c.dma("sp", S32, init, writes=[R["S32"]])
                c.op("act", lambda: nc.scalar.copy(out=SBF[0], in_=S32), reads=[R["S32"]], writes=[R_SBF[0]])
                if seq == "s" and ld == 1:
                    c.op("dve", lambda: nc.vector.memset(DL, 1.0), writes=[R["DL"]])
                cmat = C_UI if ld == 0 else C_LI
                rmat = C_LS if ld == 0 else C_US
                last = 127 if ld == 0 else 0
                def prep(blk):
                    Gd = G[:, blk, ld * 128:(ld + 1) * 128]
                    c.op("pe", lambda: nc.tensor.matmul(PS[3][:, 0:128], lhsT=Gd, rhs=CON[:, cmat, :], start=True, stop=True),
                         reads=[R["G"], self.R_CON], writes=[R_PS[3]], same_ok=True)
                    c.op("pe", lambda: nc.tensor.matmul(PS[4][:, 0:128], lhsT=CON[:, rmat, :], rhs=Gd, start=True, stop=True),
                         reads=[R["G"], self.R_CON], writes=[R_PS[4]], same_ok=True)
                    e1, e2, e3 = etmp(), etmp(), etmp()
                    c.op("act", lambda: nc.scalar.activation(out=ET[e1], in_=PS[3][:, 0:128], func=AF.Exp, scale=-1.0 / 16),
                         reads=[R_PS[3]], writes=[R_ET[e1]])
                    c.op("act", lambda: nc.scalar.activation(out=ET[e2], in_=PS[3][:, 0:128], func=AF.Exp, scale=1.0 / 16),
                         reads=[R_PS[3]], writes=[R_ET[e2]])
                    c.op("act", lambda: nc.scalar.activation(out=ET[e3], in_=PS[4][:, 0:128], func=AF.Exp, scale=-1.0 / 16),
                         reads=[R_PS[4]], writes=[R_ET[e3]])
                    qi, ki, ks = tmpb(), tmpb(), tmpb()
                    sl = slice(blk * 128, (blk + 1) * 128)
                    c.op("dve", lambda: nc.vector.tensor_tensor(out=TMPB[qi], in0=QT[:, sl], in1=ET[e1], op=ALU.mult),
                         reads=[R["QT"], R_ET[e1]], writes=[R_TMPB[qi]])
                    c.op("dve", lambda: nc.vector.tensor_tensor(out=TMPB[ki], in0=KT[:, sl], in1=ET[e2], op=ALU.mult),
                         reads=[R["KT"], R_ET[e2]], writes=[R_TMPB[ki]])
                    c.op("dve", lambda: nc.vector.tensor_tensor(out=TMPB[ks], in0=KTOK[:, blk, :], in1=ET[e3], op=ALU.mult),
                         reads=[R["KTOK"], R_ET[e3]], writes=[R_TMPB[ks]])
                    return (e1, qi, ki, ks)

                nxt = prep(blocks[0])
                for bi, blk in enumerate(blocks):
                    e1, qi, ki, ks = nxt
                    if bi + 1 < len(blocks):
                        nxt = prep(blocks[bi + 1])
                    am = 6 + (ami[0] % 2)
                    ami[0] += 1
                    sl = slice(blk * 128, (blk + 1) * 128)
                    c.op("pe", lambda: nc.tensor.matmul(PS[5][:, 0:128], lhsT=TMPB[ki], rhs=TMPB[qi], start=True, stop=True),
                         reads=[R_TMPB[ki], R_TMPB[qi]], writes=[R_PS[5]], same_ok=True)
                    c.op("dve", lambda: nc.vector.tensor_tensor(out=TMPB[am], in0=PS[5][:, 0:128], in1=CON[:, cmat, :], op=ALU.mult),
                         reads=[R_PS[5], self.R_CON], writes=[R_TMPB[am]])
                    c.op("pe", lambda: nc.tensor.matmul(PS[6][:, 0:256], lhsT=TMPB[am], rhs=VTOK[:, blk, :], start=True, stop=False),
                         reads=[R_TMPB[am], R["VTOK"]], writes=[R_PS[6]], same_ok=True)
                    c.op("pe", lambda: nc.tensor.matmul(PS[6][:, 0:256], lhsT=TMPB[qi], rhs=SBF[sbi], start=False, stop=True),
                         reads=[R_TMPB[qi], R_SBF[sbi]], writes=[R_PS[6]], same_ok=True)
                    c.op("pe", lambda: nc.tensor.matmul(PS[7][:, 0:256], lhsT=TMPB[ks], rhs=VTOK[:, blk, :], start=True, stop=True),
                         reads=[R_TMPB[ks], R["VTOK"]], writes=[R_PS[7]], same_ok=True)
                    c.op("dve", lambda: nc.vector.scalar_tensor_tensor(
                        out=S32, in0=S32, scalar=ET[e1][:, last:last + 1], in1=PS[7][:, 0:256], op0=ALU.mult, op1=ALU.add),
                        reads=[R["S32"], R_ET[e1], R_PS[7]], writes=[R["S32"]])
                    sbi = 1 - sbi
                    c.op("act", lambda: nc.scalar.copy(out=SBF[sbi], in_=S32), reads=[R["S32"]], writes=[R_SBF[sbi]])
                    if seq == "s" and ld == 0:
                        o = otmp()
                        c.op("act", lambda: nc.scalar.copy(out=OT[o], in_=PS[6][:, 0:256]), reads=[R_PS[6]], writes=[R_OT[o]])
                        c.dma("sp", olf[h, blk], OT[o], reads=[R_OT[o]], writes=[R_olf[h]], nowaw=True)
                    elif seq == "s":
                        o = otmp()
                        c.dma("sp", OT[o], olf[h, blk], reads=[R_olf[h]], writes=[R_OT[o]])
                        c.op("dve", lambda: nc.vector.tensor_tensor(out=OT[o], in0=OT[o], in1=PS[6][:, 0:256], op=ALU.add),
                             reads=[R_OT[o], R_PS[6]], writes=[R_OT[o]])
                        c.dma("sp", osp[h, blk], OT[o], reads=[R_OT[o]], writes=[R_osum[h]], nowaw=True)
                        qt = qi
                        c.op("dve", lambda: nc.vector.tensor_scalar(out=TMPB[qt], in0=TMPB[qi], scalar1=DL, scalar2=None, op0=ALU.mult),
                             reads=[R_TMPB[qi], R["DL"]], writes=[R_TMPB[qt]])
                        c.dma("act", qts[h, :, sl], TMPB[qt], reads=[R_TMPB[qt]], writes=[R_qts[h]], nowaw=True)
                        c.op("dve", lambda: nc.vector.tensor_tensor(out=DL, in0=DL, in1=ET[e1][:, last:last + 1], op=ALU.mult),
                             reads=[R["DL"], R_ET[e1]], writes=[R["DL"]])
                    elif ld == 0:
                        pb = blk - (16 + 2 * seq)
                        c.op("act", lambda: nc.scalar.copy(out=OP[:, pb, :], in_=PS[6][:, 0:256]), reads=[R_PS[6]], writes=[R["OP"]])
                    else:
                        pb = blk - (16 + 2 * seq)
                        o = otmp()
                        c.op("dve", lambda: nc.vector.tensor_tensor(out=OT[o], in0=OP[:, pb, :], in1=PS[6][:, 0:256], op=ALU.add),
                             reads=[R["OP"], R_PS[6]], writes=[R_OT[o]])
                        self.gla_finalize(OT[o], R_OT[o], blk, h)
                if seq == "s":
                    if ld == 0:
                        c.dma("sp", self.ccg_in[h * 128:(h + 1) * 128, :], S32, reads=[R["S32"]], writes=[self.R_ccg], nowaw=True)
                else:
                    c.dma("sp", dr["ngs"][seq, e, ld, h], S32, reads=[R["S32"]], writes=[self.R_out], nowaw=True)

            chain("s", 0, list(range(16)), dr["s0"][e, h])
            if STOP_AT == 4:
                return
            chain("s", 1, list(range(15, -1, -1)), None)
            if STOP_AT == 5:
                return
            for pi in range(2):
                chain(pi, 0, [16 + 2 * pi, 17 + 2 * pi], None)
                chain(pi, 1, [17 + 2 * pi, 16 + 2 * pi], None)
        if STOP_AT == 6:
            return
        c.collective([self.ccg_in_t.ap().opt()], [self.ccg_out_t.ap().opt()], self.groups,
                     reads=[self.R_ccg], writes=[self.R_ccgo])

    def gla_finalize(self, o_ap, R_o, blk, h, defer=False):
        nc, c = self.nc, self.c
        SM, MB = self.SM, self.MBuf
        fi = self.fin_i % 2
        self.fin_i += 1
        ST = SM[:, 352 + fi * 4:352 + fi * 4 + 4]
        JK = self.XRbuf[:, 7936:8192]
        R_st = self.R_fin[fi]
        XRb = self.XRbuf
        YN = XRb[:, 6656 + fi * 128:6656 + (fi + 1) * 128].bitcast(BF16)
        YF = XRb[:, 6912 + fi * 128:6912 + (fi + 1) * 128].bitcast(BF16).rearrange("p (a t) -> p a t", a=2)
        c.op("act", lambda: nc.scalar.activation(out=JK, in_=o_ap, func=AF.Square, accum_out=ST[:, 0:1]),
             reads=[R_o], writes=[R_st, self.R_jk])
        c.op("act", lambda: nc.scalar.activation(out=ST[:, 1:2], in_=ST[:, 0:1], func=AF.Sqrt, scale=1.0 / 256, bias=self.eps_ap),
             reads=[R_st], writes=[R_st])
        c.op("dve", lambda: nc.vector.reciprocal(out=ST[:, 2:3], in_=ST[:, 1:2]), reads=[R_st], writes=[R_st])
        c.op("dve", lambda: nc.vector.scalar_tensor_tensor(out=YN, in0=o_ap, scalar=ST[:, 2:3], in1=self.GN,
                                                           op0=ALU.mult, op1=ALU.mult),
             reads=[R_o, R_st], writes=[self.R_yn[fi]])
        def part_b():
            pt = self.PS[2][:].bitcast(BF16)
            for a in range(2):
                c.op("pe", lambda: nc.tensor.transpose(pt[:, a * 128:(a + 1) * 128], YN[:, a * 128:(a + 1) * 128], self.identb),
                     reads=[self.R_yn[fi], self.R_CON], writes=[self.R_PS[2]], same_ok=True)
            self.copy_any(YF, pt[:, 0:256].rearrange("p (a t) -> p a t", a=2), [self.R_PS[2]], [self.R_yf[fi]])
            t = blk // 4
            c.dma("act", self.yT[8 + 2 * h:10 + 2 * h, :, blk * 128:(blk + 1) * 128].rearrange("k p t -> p k t"), YF,
                  reads=[self.R_yf[fi]], writes=[self.R_yT[t]], nowaw=True)
        if defer:
            return part_b
        part_b()
        return None

    fin_i = 0

    def gla_fix(self, l):
        nc, c = self.nc, self.c
        MB, XR = self.MBuf, self.XRbuf
        bf = lambda ap: ap.bitcast(BF16)
        SA = XR[:, 0:256]
        SB = XR[:, 256:512]
        SIN = bf(MB[:, 0:128])
        QB = [bf(MB[:, 128 + i * 64:128 + (i + 1) * 64]) for i in range(2)]
        OB = [XR[:, 512 + i * 256:512 + (i + 1) * 256] for i in range(2)]
        R_s, R_sin = c.res(), c.res()
        R_QB = [c.res(), c.res()]
        R_OB = [c.res(), c.res()]
        pm = self.PM
        it = 0
        pend_b = None
        for h in range(4):
            c.dma("sp", SA, self.ccg_out[0, h * 128:(h + 1) * 128, :], reads=[self.R_ccgo], writes=[R_s])
            c.dma("sp", SB, self.ccg_out[1, h * 128:(h + 1) * 128, :], reads=[self.R_ccgo], writes=[R_s], nowaw=True)
            c.op("dve", lambda: nc.vector.tensor_scalar(out=SA, in0=SA, scalar1=pm[:, 0:1], scalar2=None, op0=ALU.mult),
                 reads=[R_s, self.R_pm], writes=[R_s])
            c.op("dve", lambda: nc.vector.scalar_tensor_tensor(out=SIN, in0=SB, scalar=pm[:, 1:2], in1=SA, op0=ALU.mult, op1=ALU.add),
                 reads=[R_s, self.R_pm], writes=[R_sin])
            for blk in range(16):
                i = it % 2
                it += 1
                c.dma("sp", QB[i], self.dram["qts"][h, :, blk * 128:(blk + 1) * 128], reads=[self.R_qts[h]], writes=[R_QB[i]])
                c.dma("act", OB[i], self.dram["osp"][h, blk], reads=[self.R_osum[h]], writes=[R_OB[i]])
                b = 3 + (it % 2)
                c.op("pe", lambda: nc.tensor.matmul(self.PS[b][:, 0:256], lhsT=QB[i], rhs=SIN, start=True, stop=True),
                     reads=[R_QB[i], R_sin], writes=[self.R_PS[b]], same_ok=True)
                c.op("dve", lambda: nc.vector.tensor_tensor(out=OB[i], in0=OB[i], in1=self.PS[b][:, 0:256], op=ALU.add),
                     reads=[R_OB[i], self.R_PS[b]], writes=[R_OB[i]])
                pb_new = self.gla_finalize(OB[i], R_OB[i], blk, h, defer=True)
                if pend_b is not None:
                    pend_b()
                pend_b = pb_new
        if pend_b is not None:
            pend_b()

    def attn_push(self, pend, s2):
        s3_new = pend[0]() if pend[0] is not None else None
        if pend[1] is not None:
            pend[1]()
        pend[0], pend[1] = s2, s3_new

    def attn_flush(self, pend):
        s3_new = pend[0]() if pend[0] is not None else None
        if pend[1] is not None:
            pend[1]()
        if s3_new is not None:
            s3_new()
        pend[0], pend[1] = None, None

    def attn_unit(self, q_ap, R_q, segs, es_ap, out_cb, PB, R_PB, ST, R_ST, ui, OTB, R_OTB):
        nc, c = self.nc, self.c
        i = ui % 2
        SS = self.PSsets[i]
        R_S = self.R_PS[3 * i:3 * i + 3]
        nkb = len(segs)
        RS = [R_S[b] for b in range(3) if b * 512 < nkb * 128]
        for kb, (kT, R_k, mask, vb, R_v) in enumerate(segs):
            o_ = SS[:, kb * 128:(kb + 1) * 128]
            c.op("pe", lambda: nc.tensor.matmul(o_, lhsT=kT, rhs=q_ap, start=True, stop=(mask is None)),
                 reads=[R_q, R_k], writes=[R_S[kb // 4]], same_ok=True)
            if mask is not None:
                c.op("pe", lambda: nc.tensor.matmul(o_, lhsT=self.identb, rhs=mask, start=False, stop=True),
                     reads=[self.R_CON, self.R_tab], writes=[R_S[kb // 4]], same_ok=True)
        pb = PB[i]
        st = ST[i]
        PVb, R_PV = self.PS[6 + i], self.R_PS[6 + i]
        c.op("act", lambda: nc.scalar.activation(out=pb[:, 0:nkb * 128], in_=SS[:, 0:nkb * 128], func=AF.Exp),
             reads=RS, writes=[R_PB[i]])
        otp = PVb[:, 256:320].bitcast(BF16)

        def stage3():
            c.op("pe", lambda: nc.tensor.transpose(otp, OTB[i], self.identb),
                 reads=[R_OTB[i], self.R_CON], writes=[R_PV], same_ok=True)
            out_cb(otp, R_PV)

        def stage2():
            for kb, (kT, R_k, mask, vb, R_v) in enumerate(segs):
                c.op("pe", lambda: nc.tensor.matmul(PVb[:, 0:129], lhsT=pb[:, kb * 128:(kb + 1) * 128], rhs=vb,
                                                    start=(kb == 0), stop=(kb == nkb - 1)),
                     reads=[R_v, R_PB[i]], writes=[R_PV], same_ok=True)
            if es_ap is not None:
                c.op("dve", lambda: nc.vector.tensor_tensor(out=st[:, 0:1], in0=PVb[:, 128:129], in1=es_ap, op=ALU.add),
                     reads=[R_PV, self.R_sink], writes=[R_ST[i]])
                c.op("dve", lambda: nc.vector.reciprocal(out=st[:, 1:2], in_=st[:, 0:1]), reads=[R_ST[i]], writes=[R_ST[i]])
            else:
                c.op("dve", lambda: nc.vector.reciprocal(out=st[:, 1:2], in_=PVb[:, 128:129]), reads=[R_PV], writes=[R_ST[i]])
            c.op("act", lambda: nc.scalar.activation(out=OTB[i], in_=PVb[:, 0:128], func=AF.Copy, scale=st[:, 1:2]),
                 reads=[R_PV, R_ST[i]], writes=[R_OTB[i]])
            return stage3
        return stage2

    def win_attn(self, l):
        nc, c = self.nc, self.c
        e = l // 2
        MB, XR, SM, CON = self.MBuf, self.XRbuf, self.SM, self.CON
        PS, R_PS = self.PS, self.R_PS
        dr = self.dram
        bf = lambda ap: ap.bitcast(BF16)
        QT = [bf(MB[:, i * 1280:(i + 1) * 1280]) for i in range(2)]
        KT = bf(MB[:, 2560:3904])
        VT = bf(MB[:, 3904:5290]).rearrange("p (b d) -> p b d", b=21)
        CK = bf(MB[:, 5290:5546])
        CV = bf(MB[:, 5546:5810]).rearrange("p (b d) -> p b d", b=4)
        PB = [bf(MB[:, 5810 + i * 512:5810 + (i + 1) * 512]) for i in range(2)]
        YA = [bf(MB[:, 6834 + i * 64:6834 + (i + 1) * 64]) for i in range(2)]
        HS = [bf(MB[:, 6962 + i * 128:6962 + (i + 1) * 128]) for i in range(3)]
        ST = [MB[:, 7346 + i * 8:7346 + (i + 1) * 8] for i in range(2)]
        KVO = [MB[:, 7362 + i * 256:7362 + (i + 1) * 256] for i in range(2)]
        SINK = MB[:, 7874:7882]
        ES = MB[:, 7882:7890]
        OTB = [bf(MB[:, 7890 + i * 64:7890 + (i + 1) * 64]) for i in range(2)]
        PTS = R_PTS = None
        R_OTB = [c.res(), c.res()]
        RC = XR[:, 0:2048]
        RS_ = XR[:, 2048:4096]
        KF = [XR[:, 4096 + i * 512:4096 + (i + 1) * 512] for i in range(2)]
        T1 = [XR[:, 5120 + i * 512:5120 + (i + 1) * 512] for i in range(2)]
        R = {k: c.res(k) for k in ["KT", "VT", "CK", "CV", "rope", "HSin", "HSout"]}
        R_QT = [c.res(), c.res()]
        R_PB = [c.res(), c.res()]
        R_YA = [c.res(), c.res()]
        R_ST = [c.res(), c.res()]
        R_KVO = [c.res(), c.res()]
        R_KF = [c.res(), c.res()]
        R_T1 = [c.res(), c.res()]
        self.R_sink = c.res("sink")
        self.R_tab = self.R_CON
        c.dma("sp", RC, dr["ropeC"][:, :], writes=[R["rope"]])
        c.dma("act", RS_, dr["ropeS"][:, :], writes=[R["rope"]], nowaw=True)
        c.dma("sp", SINK, dr["sink"][e, :].partition_broadcast(128), writes=[self.R_sink])
        c.op("act", lambda: nc.scalar.activation(out=ES, in_=SINK, func=AF.Exp), reads=[self.R_sink], writes=[self.R_sink])
        c.op("dve", lambda: nc.vector.memset(VT[:, :, 128:129], 1.0), writes=[R["VT"]])
        c.op("dve", lambda: nc.vector.memset(CV[:, :, 128:129], 1.0), writes=[R["CV"]])
        ki = [0]

        def rope_evac(dst_of_tile, R_dst, scale):
            def ev(t, ps, Rb):
                if t >= 4:
                    self.copy_any(dst_of_tile(t), ps, [Rb], [R_dst], scale=scale)
                    return
                i = ki[0] % 2
                ki[0] += 1
                if scale is None:
                    c.op("act", lambda: nc.scalar.copy(out=KF[i], in_=ps), reads=[Rb], writes=[R_KF[i]])
                else:
                    c.op("act", lambda: nc.scalar.activation(out=KF[i], in_=ps, func=AF.Copy, scale=scale), reads=[Rb], writes=[R_KF[i]])
                c.op("pe", lambda: nc.tensor.matmul(PS[3][:], lhsT=CON[:, C_PERM, :], rhs=KF[i], start=True, stop=True),
                     reads=[R_KF[i], self.R_CON], writes=[R_PS[3]], same_ok=True)
                c.op("dve", lambda: nc.vector.tensor_tensor(out=T1[i], in0=PS[3][:], in1=RS_[:, t * 512:(t + 1) * 512], op=ALU.mult),
                     reads=[R_PS[3], R["rope"]], writes=[R_T1[i]])
                c.op("pool", lambda: nc.gpsimd.tensor_tensor(out=KF[i], in0=KF[i], in1=RC[:, t * 512:(t + 1) * 512], op=ALU.mult),
                     reads=[R_KF[i], R["rope"]], writes=[R_KF[i]])
                c.op("dve", lambda: nc.vector.tensor_tensor(out=dst_of_tile(t), in0=KF[i], in1=T1[i], op=ALU.add),
                     reads=[R_KF[i], R_T1[i]], writes=[R_dst])
            return ev

        ui = 0
        pend = [None, None]
        for g in range(2):
            ws = self.load_w(l, [(1024 + g * 128, 128), (1280 + g * 128, 128)])
            c.dma("pool", CK, dr["ck_a"][e, g], writes=[R["CK"]])
            c.dma("pool", CV[:, :, 0:128], dr["cv_a"][e, g].rearrange("(b p) d -> p b d", p=128), writes=[R["CV"]])
            self.proj_fm(ws, 0, range(NT), rope_evac(lambda t: KT[:, t * 512:(t + 1) * 512], R["KT"], None))

            def ev_v(blk, ps, Rb):
                self.copy_any(VT[:, blk, 0:128], ps[:, 128:256], [Rb], [R["VT"]])
                if blk >= 16:
                    i = blk % 2
                    pi, sb_ = (blk - 16) // 2, (blk - 16) % 2
                    c.op("act", lambda: nc.scalar.copy(out=KVO[i], in_=ps[:, 0:256]), reads=[Rb], writes=[R_KVO[i]])
                    c.dma("sp", dr["nk_a"][pi, e, sb_ * 128:(sb_ + 1) * 128, g, :], KVO[i][:, 0:128], reads=[R_KVO[i]], writes=[self.R_out], nowaw=True)
                    c.dma("sp", dr["nv_a"][pi, e, sb_ * 128:(sb_ + 1) * 128, g, :], KVO[i][:, 128:256], reads=[R_KVO[i]], writes=[self.R_out], nowaw=True)
            self.proj_tm(ws, 0, 256, range(NB), ev_v)
            c.op("dve", lambda: nc.vector.tensor_copy(out=HS[0][:, 0:128], in_=KT[:, 1920:2048]), reads=[R["KT"]], writes=[R["HSin"]])
            c.op("dve", lambda: nc.vector.tensor_copy(out=HS[0][:, 128:256], in_=VT[:, 15, 0:128]), reads=[R["VT"]], writes=[R["HSin"]])
            cin, cout, R_cin, R_cout = self.cc_kv[g]
            c.dma("sp", cin.ap(), HS[0], reads=[R["HSin"]], writes=[R_cin])
            c.collective([cin.ap().opt()], [cout.ap().opt()], self.groups, reads=[R_cin], writes=[R_cout])
            co = cout.ap().rearrange("(r p) n -> r p n", r=2)
            c.dma("sp", HS[1], co[0], reads=[R_cout], writes=[R["HSout"]])
            c.dma("sp", HS[2], co[1], reads=[R_cout], writes=[R["HSout"]], nowaw=True)
            c.op("dve", lambda: nc.vector.tensor_scalar(out=HS[1], in0=HS[1], scalar1=self.PM[:, 0:1], scalar2=None, op0=ALU.mult),
                 reads=[R["HSout"], self.R_pm], writes=[R["HSout"]])
            c.op("dve", lambda: nc.vector.scalar_tensor_tensor(out=HS[1], in0=HS[2], scalar=self.PM[:, 1:2], in1=HS[1], op0=ALU.mult, op1=ALU.add),
                 reads=[R["HSout"], self.R_pm], writes=[R["HSout"]])
            c.op("dve", lambda: nc.vector.tensor_copy(out=KT[:, 2560:2688], in_=HS[1][:, 0:128]), reads=[R["HSout"]], writes=[R["KT"]])
            c.op("dve", lambda: nc.vector.tensor_copy(out=VT[:, 20, 0:128], in_=HS[1][:, 128:256]), reads=[R["HSout"]], writes=[R["VT"]])
            for s2 in range(2):
                h0 = 4 * g + 2 * s2
                ws = self.load_w(l, [(h0 * 128, 256)])
                for hh in range(2):
                    self.proj_fm(ws, hh * 128, range(NT),
                                 rope_evac(lambda t, hh=hh: QT[hh][:, t * 512:(t + 1) * 512], R_QT[hh], SC))
                for hh in range(2):
                    hd = h0 + hh
                    for qb in range(NB):
                        qap = QT[hh][:, qb * 128:(qb + 1) * 128]
                        if qb < 16:
                            segs = [(CK[:, k * 128:(k + 1) * 128], R["CK"], None, CV[:, k, 0:129], R["CV"]) for k in range(4)]
                            if qb > 0:
                                segs.append((KT[:, (qb - 1) * 128:qb * 128], R["KT"], self.maskb[0], VT[:, qb - 1, 0:129], R["VT"]))
                            segs.append((KT[:, qb * 128:(qb + 1) * 128], R["KT"], None, VT[:, qb, 0:129], R["VT"]))
                            if qb < 15:
                                segs.append((KT[:, (qb + 1) * 128:(qb + 2) * 128], R["KT"], self.maskb[1], VT[:, qb + 1, 0:129], R["VT"]))
                            else:
                                segs.append((KT[:, 2560:2688], R["KT"], self.maskb[2], VT[:, 20, 0:129], R["VT"]))
                        else:
                            p0 = 16 + 2 * ((qb - 16) // 2)
                            segs = [(KT[:, (p0 + j) * 128:(p0 + j + 1) * 128], R["KT"], None, VT[:, p0 + j, 0:129], R["VT"]) for j in range(2)]

                        def out_cb(ps, Rb, qb=qb, hd=hd):
                            i = qb % 2
                            self.copy_any(YA[i], ps, [Rb], [R_YA[i]])
                            c.dma("sp", self.yT[hd, :, qb * 128:(qb + 1) * 128], YA[i], reads=[R_YA[i]], writes=[self.R_yT[qb // 4]], nowaw=True)
                        s2 = self.attn_unit(qap, R_QT[hh], segs, ES[:, hd:hd + 1], out_cb, PB, R_PB, ST, R_ST, ui, OTB, R_OTB)
                        self.attn_push(pend, s2)
                        ui += 1
                self.attn_flush(pend)


_IN_NAMES = ["xin", "condT", "ada_w", "ada_bT", "norm_gT", "final_gT", "w_in_even", "w_in_odd", "w_out", "consts",
             "ck_a", "cv_a", "ropeC", "ropeS", "sink", "s0", "wg_s", "wg_p", "gla_gn", "ck_c", "cv_c", "na_tab",
             "wsT_s", "wsT_p", "gb_s", "gb_p", "gmlp_gn", "pm", "onesrow"]


def _run(inputs, ncores=8, upto=None, dbg=False):
    I = {k: np.asarray(v) for k, v in inputs.items()}
    prog = Prog(ncores, upto=upto, dbg=dbg)
    nc = prog.build()
    in_maps = []
    for cid in range(ncores):
        m = _prep_core(cid, I)
        in_maps.append({k: np.ascontiguousarray(m[k], dtype=np.float32) for k in _IN_NAMES})
    res = run_bass_kernel_spmd(nc, in_maps, core_ids=list(range(ncores)))
    return res, prog


def kernel(**inputs):
    res, _ = _run(inputs)
    R = res.results
    f32 = np.float32
    y_prompt = np.zeros((16, 256, D), f32)
    y_sample = np.zeros((4, 4096, D), f32)
    nk_a = np.zeros((16, 2, 256, 2, 128), f32)
    nv_a = np.zeros((16, 2, 256, 2, 128), f32)
    ngs = np.zeros((16, 2, 2, 4, 128, 256), f32)
    nk_c = np.zeros((16, 2, 256, 8, 128), f32)
    nv_c = np.zeros((16, 2, 256, 8, 128), f32)
    for cid in range(8):
        r = R[cid]
        b, side = cid // 2, cid % 2
        yo = r["y_out"]
        if side == 0:
            y_sample[b, :NS] = yo[:NS]
        else:
            y_sample[b, NS:] = yo[:NS][::-1]
        for pi in range(2):
            y_prompt[2 * cid + pi] = yo[NS + pi * NPR:NS + (pi + 1) * NPR]
            nk_a[2 * cid + pi] = r["nk_a"][pi]
            nv_a[2 * cid + pi] = r["nv_a"][pi]
            ngs[2 * cid + pi] = r["ngs"][pi]
            nk_c[2 * cid + pi] = r["nk_c"][pi]
            nv_c[2 * cid + pi] = r["nv_c"][pi]
    return (y_prompt, y_sample, nk_a, nv_a, ngs, nk_c, nv_c)


def _phase_B_odd(self, l):
    with self.nc.named_scope(f"na{l}"):
        self.na_attn(l)
        self.fence()
    with self.nc.named_scope(f"gmlp{l}"):
        self.gmlp(l)


def _na_attn(self, l):
    nc, c = self.nc, self.c
    o = l // 2
    MB, XR = self.MBuf, self.XRbuf
    dr = self.dram
    bf = lambda ap: ap.bitcast(BF16)
    QT = bf(MB[:, 0:1280])
    KT = bf(MB[:, 1280:2688])
    VT = bf(MB[:, 2688:4140]).rearrange("p (b d) -> p b d", b=22)
    CK = bf(MB[:, 4140:4396])
    CV = bf(MB[:, 4396:4660]).rearrange("p (b d) -> p b d", b=4)
    PB = [bf(MB[:, 4660 + i * 576:4660 + (i + 1) * 576]) for i in range(2)]
    YA = [bf(MB[:, 5812 + i * 64:5812 + (i + 1) * 64]) for i in range(2)]
    HS = [bf(MB[:, 5940 + i * 256:5940 + (i + 1) * 256]) for i in range(3)]
    ST = [MB[:, 6708 + i * 8:6708 + (i + 1) * 8] for i in range(2)]
    KVO = [MB[:, 6724 + i * 256:6724 + (i + 1) * 256] for i in range(2)]
    NTAB = bf(XR[:, 0:1600]).rearrange("p (a b q) -> p a b q", a=5, b=5)
    OTB = [bf(MB[:, 7236 + i * 64:7236 + (i + 1) * 64]) for i in range(2)]
    R_OTB = [c.res(), c.res()]
    R = {k: c.res(k) for k in ["QT", "KT", "VT", "CK", "CV", "HSin", "HSout", "tab"]}
    R_PB = [c.res(), c.res()]
    R_YA = [c.res(), c.res()]
    R_ST = [c.res(), c.res()]
    R_KVO = [c.res(), c.res()]
    self.R_tab = R["tab"]
    c.op("dve", lambda: nc.vector.memset(VT[:, :, 128:129], 1.0), writes=[R["VT"]])
    c.op("dve", lambda: nc.vector.memset(CV[:, :, 128:129], 1.0), writes=[R["CV"]])
    self.R_sink = self.R_CON
    KPR = 2304
    ui = 0
    pend = [None, None]
    for h in range(8):
        ws = self.load_w(l, [(h * 128, 128), (1024 + h * 128, 128), (2048 + h * 128, 128)])
        c.dma("pool", CK, dr["ck_c"][o, h], writes=[R["CK"]])
        c.dma("pool", CV[:, :, 0:128], dr["cv_c"][o, h].rearrange("(b p) d -> p b d", p=128), writes=[R["CV"]])
        c.dma("pool", NTAB, dr["na_tab"][o, h].rearrange("a (b p) q -> p a b q", p=128), writes=[R["tab"]])
        self.proj_fm(ws, 0, range(NT), lambda t, ps, Rb: self.copy_any(
            QT[:, t * 512:(t + 1) * 512], ps, [Rb], [R["QT"]], scale=SC))

        def ev_k(t, ps, Rb):
            dst = KT[:, t * 512:(t + 1) * 512] if t < 4 else KT[:, KPR:KPR + 512]
            self.copy_any(dst, ps, [Rb], [R["KT"]])
        self.proj_fm(ws, 128, range(NT), ev_k)

        def ev_v(blk, ps, Rb):
            self.copy_any(VT[:, blk, 0:128], ps, [Rb], [R["VT"]])
        self.proj_tm(ws, 256, 128, range(16), ev_v)

        def ev_kv(blk, ps, Rb, h=h):
            self.copy_any(VT[:, blk + 2, 0:128], ps[:, 128:256], [Rb], [R["VT"]])
            i = blk % 2
            pi, sb_ = (blk - 16) // 2, (blk - 16) % 2
            c.op("act", lambda: nc.scalar.copy(out=KVO[i], in_=ps[:, 0:256]), reads=[Rb], writes=[R_KVO[i]])
            c.dma("sp", dr["nk_c"][pi, o, sb_ * 128:(sb_ + 1) * 128, h, :], KVO[i][:, 0:128], reads=[R_KVO[i]], writes=[self.R_out], nowaw=True)
            c.dma("sp", dr["nv_c"][pi, o, sb_ * 128:(sb_ + 1) * 128, h, :], KVO[i][:, 128:256], reads=[R_KVO[i]], writes=[self.R_out], nowaw=True)
        self.proj_tm(ws, 128, 256, range(16, 20), ev_kv)
        c.op("dve", lambda: nc.vector.tensor_copy(out=HS[0][:, 0:256], in_=KT[:, 1792:2048]), reads=[R["KT"]], writes=[R["HSin"]])
        c.op("dve", lambda: nc.vector.tensor_copy(out=HS[0][:, 256:512].rearrange("p (b d) -> p b d", b=2), in_=VT[:, 14:16, 0:128]),
             reads=[R["VT"]], writes=[R["HSin"]])
        cin, cout, R_cin, R_cout = self.cc_na[h]
        c.dma("sp", cin.ap(), HS[0], reads=[R["HSin"]], writes=[R_cin])
        c.collective([cin.ap().opt()], [cout.ap().opt()], self.groups, reads=[R_cin], writes=[R_cout])
        co = cout.ap().rearrange("(r p) n -> r p n", r=2)
        c.dma("sp", HS[1], co[0], reads=[R_cout], writes=[R["HSout"]])
        c.dma("sp", HS[2], co[1], reads=[R_cout], writes=[R["HSout"]], nowaw=True)
        c.op("dve", lambda: nc.vector.tensor_scalar(out=HS[1], in0=HS[1], scalar1=self.PM[:, 0:1], scalar2=None, op0=ALU.mult),
             reads=[R["HSout"], self.R_pm], writes=[R["HSout"]])
        c.op("dve", lambda: nc.vector.scalar_tensor_tensor(out=HS[1], in0=HS[2], scalar=self.PM[:, 1:2], in1=HS[1], op0=ALU.mult, op1=ALU.add),
             reads=[R["HSout"], self.R_pm], writes=[R["HSout"]])
        for hb in range(2):
            c.op("dve", lambda: nc.vector.tensor_copy(out=KT[:, 2048 + hb * 128:2048 + (hb + 1) * 128],
                                                      in_=HS[1][:, (1 - hb) * 128:(2 - hb) * 128]),
                 reads=[R["HSout"]], writes=[R["KT"]])
            c.op("dve", lambda: nc.vector.tensor_copy(out=VT[:, 16 + hb, 0:128], in_=HS[1][:, 256 + (1 - hb) * 128:256 + (2 - hb) * 128]),
                 reads=[R["HSout"]], writes=[R["VT"]])
        for qb in range(NB):
            qap = QT[:, qb * 128:(qb + 1) * 128]
            if qb < 16:
                lo = _na_lo(qb)
                pat = _na_pattern_of(qb)
                k0 = lo * 64
                vb0 = lo // 2
                segs = [(CK[:, k * 128:(k + 1) * 128], R["CK"], None, CV[:, k, 0:129], R["CV"]) for k in range(4)]
                segs += [(KT[:, k0 + j * 128:k0 + (j + 1) * 128], R["KT"], NTAB[:, pat, j, :], VT[:, vb0 + j, 0:129], R["VT"]) for j in range(5)]
            else:
                p0 = 2 * ((qb - 16) // 2)
                segs = [(KT[:, KPR + (p0 + j) * 128:KPR + (p0 + j + 1) * 128], R["KT"], None, VT[:, 18 + p0 + j, 0:129], R["VT"]) for j in range(2)]

            def out_cb(ps, Rb, qb=qb, h=h):
                i = qb % 2
                self.copy_any(YA[i], ps, [Rb], [R_YA[i]])
                c.dma("sp", self.yT[h, :, qb * 128:(qb + 1) * 128], YA[i], reads=[R_YA[i]], writes=[self.R_yT[qb // 4]], nowaw=True)
            s2 = self.attn_unit(qap, R["QT"], segs, None, out_cb, PB, R_PB, ST, R_ST, ui, OTB, R_OTB)
            self.attn_push(pend, s2)
            ui += 1
        self.attn_flush(pend)


def _gmlp(self, l):
    nc, c = self.nc, self.c
    o = l // 2
    MB, XR = self.MBuf, self.XRbuf
    dr = self.dram
    bf = lambda ap: ap.bitcast(BF16)
    UT = bf(MB[:, 0:5120]).rearrange("p (k t) -> p k t", k=8)
    VN = [bf(MB[:, 5120 + i * 512:5120 + (i + 1) * 512]) for i in range(2)]
    YD = [bf(MB[:, 6144 + i * 512:6144 + (i + 1) * 512]).rearrange("p (k t) -> p k t", k=8) for i in range(2)]
    WST = [bf(MB[:, 7168 + i * 256:7168 + (i + 1) * 256]).rearrange("p (g t) -> p g t", g=4) for i in range(2)]
    STS = [MB[:, 7680 + i * 16:7680 + (i + 1) * 16] for i in range(2)]
    VF = [XR[:, i * 1024:(i + 1) * 1024] for i in range(2)]
    GNB = XR[:, 2048:3072]
    BB8 = [XR[:, 3072 + i * 1024:3072 + (i + 1) * 1024].rearrange("p (k t) -> p k t", k=8) for i in range(2)]
    TF = [XR[:, 5120 + i * 512:5120 + (i + 1) * 512].rearrange("p (k t) -> p k t", k=4) for i in range(2)]
    R = {k: c.res(k) for k in ["UT", "WST", "GNB", "BB8"]}
    R_VN = [c.res(), c.res()]
    R_YD = [c.res(), c.res()]
    R_VF = [c.res(), c.res()]
    R_TF = [c.res(), c.res()]
    R_STS = [c.res(), c.res()]
    c.dma("pool", WST[0], dr["wsT_s"][o].rearrange("g j i -> j g i"), writes=[R["WST"]])
    c.dma("pool", WST[1], dr["wsT_p"][o].rearrange("g j i -> j g i"), writes=[R["WST"]], nowaw=True)
    c.dma("sp", GNB, dr["gmlp_gn"][o, :].partition_broadcast(128), writes=[R["GNB"]])
    first = True
    for vi, key in enumerate(["gb_s", "gb_p"]):
        for g in range(4):
            for d2 in range(2):
                c.dma("sp", BB8[vi][:, 2 * g + d2, :], dr[key][o, g, :].partition_broadcast(128), writes=[R["BB8"]], nowaw=not first)
                first = False
    halves = [[(0, 512), (512, 512), (1024, 256)], [(1280, 256), (1536, 512), (2048, 512)]]
    it = 0
    for hf in range(2):
        base = hf * 1280
        for ub in range(2):
            ws = self.load_w(l, [(3072 + ub * 512, 512)])
            for j in range(4):
                cb = ub * 4 + j
                self.proj_fm(ws, j * 128, halves[hf], lambda t, ps, Rb, cb=cb: self.copy_any(
                    UT[:, cb, t[0] - base:t[0] - base + t[1]], ps, [Rb], [R["UT"]]))
        wv = [self.load_w(l, [(4096 + vb * 512, 512)]) for vb in range(2)]
        def vproj(blk, i):
            for vb in range(2):
                self.proj_tm(wv[vb], 0, 512, [blk], lambda b_, ps, Rb, vb=vb: self.copy_any(
                    VF[i][:, vb * 512:(vb + 1) * 512], ps, [Rb], [R_VF[i]]))

        blks = list(range(hf * 10, hf * 10 + 10))
        vproj(blks[0], it % 2)
        for bi, blk in enumerate(blks):
            i = it % 2
            it += 1
            vi = 0 if blk < 16 else 1
            if bi + 1 < len(blks):
                vproj(blks[bi + 1], it % 2)
            st = STS[i]
            for vb in range(2):
                c.op("dve", lambda: nc.vector.bn_stats(out=st[:, vb * 6:(vb + 1) * 6], in_=VF[i][:, vb * 512:(vb + 1) * 512]),
                     reads=[R_VF[i]], writes=[R_STS[i]])
            c.op("dve", lambda: nc.vector.bn_aggr(out=st[:, 12:14], in_=st[:, 0:12]), reads=[R_STS[i]], writes=[R_STS[i]])
            c.op("act", lambda: nc.scalar.activation(out=st[:, 14:15], in_=st[:, 13:14], func=AF.Sqrt, scale=1.0, bias=self.eps5_ap),
                 reads=[R_STS[i], self.R_CON], writes=[R_STS[i]])
            c.op("dve", lambda: nc.vector.reciprocal(out=st[:, 15:16], in_=st[:, 14:15]), reads=[R_STS[i]], writes=[R_STS[i]])
            c.op("dve", lambda: nc.vector.tensor_scalar(out=VF[i], in0=VF[i], scalar1=st[:, 12:13], scalar2=st[:, 15:16],
                                                        op0=ALU.subtract, op1=ALU.mult),
                 reads=[R_VF[i], R_STS[i]], writes=[R_VF[i]])
            c.op("dve", lambda: nc.vector.tensor_tensor(out=VN[i], in0=VF[i], in1=GNB, op=ALU.mult),
                 reads=[R_VF[i], R["GNB"]], writes=[R_VN[i]])
            for q4 in range(2):
                bank = 4 + q4
                for j in range(4):
                    cb = q4 * 4 + j
                    c.op("pe", lambda: nc.tensor.matmul(self.PS[bank][:, j * 128:(j + 1) * 128], lhsT=VN[i][:, cb * 128:(cb + 1) * 128],
                                                        rhs=WST[vi][:, cb // 2, :], start=True, stop=True),
                         reads=[R_VN[i], R["WST"]], writes=[self.R_PS[bank]], same_ok=True)
                c.op("dve", lambda: nc.vector.tensor_tensor(
                    out=TF[q4], in0=self.PS[bank][:].rearrange("p (k t) -> p k t", k=4), in1=BB8[vi][:, q4 * 4:(q4 + 1) * 4, :], op=ALU.add),
                    reads=[self.R_PS[bank], R["BB8"]], writes=[R_TF[q4]])
                tl = blk * 128 - base
                c.op("dve", lambda: nc.vector.tensor_tensor(
                    out=YD[i][:, q4 * 4:(q4 + 1) * 4, :], in0=TF[q4], in1=UT[:, q4 * 4:(q4 + 1) * 4, tl:tl + 128], op=ALU.mult),
                    reads=[R_TF[q4], R["UT"]], writes=[R_YD[i]])
            c.dma("sp", self.yT[8:16, :, blk * 128:(blk + 1) * 128].rearrange("k p t -> p k t"), YD[i],
                  reads=[R_YD[i]], writes=[self.R_yT[blk // 4]], nowaw=True)


Prog.phase_B_odd = _phase_B_odd
Prog.na_attn = _na_attn
Prog.gmlp = _gmlp
```
